# Optimizing a Trainium2 kernel written in Bass

```python
import jax, jax.numpy as jnp
from jax import lax
import numpy as np

D_MODEL = 4096
BATCH = 4
SEQ = 2048
DEPTH = 2
DEC_BATCH = 2
DEC_SEQ = 4096
PAST_LEN = 128

GRID_W = 64
BLOCK_Q = 128
ATT_HEAD_DIM = 128
ATT_WIDTH = D_MODEL // 2
ATT_Q_HEADS = ATT_WIDTH // ATT_HEAD_DIM
ATT_KV_HEADS = ATT_Q_HEADS // 4
ATT_GROUPS = ATT_Q_HEADS // ATT_KV_HEADS
ATT_KV_WIDTH = ATT_KV_HEADS * ATT_HEAD_DIM
ROPE_THETA = 10000.0
RWKV_HEAD_DIM = 64
RWKV_WIDTH = D_MODEL // 2
RWKV_HEADS = RWKV_WIDTH // RWKV_HEAD_DIM
N_DIR = 2
DECAY_LORA = 128
ICLR_LORA = 128
GATE_LORA = 480
D_FF = 256 * (-(-(8 * D_MODEL) // (3 * 256)))
NORM_EPS = 1e-6
LNX_EPS = RWKV_HEAD_DIM * 1e-5
IN_SPLITS = (ATT_WIDTH, ATT_KV_WIDTH, ATT_KV_WIDTH, 3 * RWKV_WIDTH, N_DIR * DECAY_LORA,
             N_DIR * ICLR_LORA, GATE_LORA, D_MODEL, D_MODEL)
IN_WIDTH = sum(IN_SPLITS)

kernel_name = "hybrid_gqa_axialrope_rwkv7_bidir_convffn_encoder"


def rms_norm(x, g, eps=NORM_EPS):
    xf = x.astype(jnp.float32)
    y = xf * lax.rsqrt(jnp.mean(xf * xf, axis=-1, keepdims=True) + eps)
    return (y * g.astype(jnp.float32)).astype(x.dtype)


def dwconv3(x, w, b=None):
    xp = jnp.pad(x, ((0, 0), (1, 1), (0, 0)))
    y = xp[:, :-2] * w[0] + xp[:, 1:-1] * w[1] + xp[:, 2:] * w[2]
    return y if b is None else y + b


def axial_rope_tables(T):
    rows = T // GRID_W
    row = jnp.repeat(jnp.arange(rows, dtype=jnp.float32), GRID_W)
    col = jnp.tile(jnp.arange(GRID_W, dtype=jnp.float32), rows)
    axis_dim = ATT_HEAD_DIM // 2
    inv = ROPE_THETA ** (-jnp.arange(0, axis_dim, 2, dtype=jnp.float32) / axis_dim)
    ang_r = row[:, None] * inv
    ang_c = col[:, None] * inv
    return (jnp.cos(ang_r), jnp.sin(ang_r), jnp.cos(ang_c), jnp.sin(ang_c))


def rotate_axis(x, cos, sin):
    x1, x2 = jnp.split(x, 2, axis=-1)
    c = cos[:, None, :].astype(x.dtype)
    s = sin[:, None, :].astype(x.dtype)
    return jnp.concatenate([x1 * c - x2 * s, x1 * s + x2 * c], axis=-1)


def apply_axial_rope(x, rope):
    cos_r, sin_r, cos_c, sin_c = rope
    xr, xc = jnp.split(x, 2, axis=-1)
    return jnp.concatenate([rotate_axis(xr, cos_r, sin_r), rotate_axis(xc, cos_c, sin_c)], axis=-1)


def block_attention(q, k, v):
    B, T = q.shape[:2]
    nb = T // BLOCK_Q
    qb = q.reshape(B, nb, BLOCK_Q, ATT_KV_HEADS, ATT_GROUPS, ATT_HEAD_DIM).transpose(1, 0, 2, 3, 4, 5)
    scale = ATT_HEAD_DIM ** -0.5

    def one_block(qi):
        s = jnp.einsum('bqhgd,bkhd->bhgqk', qi, k).astype(jnp.float32) * scale
        p = jax.nn.softmax(s, axis=-1).astype(v.dtype)
        return jnp.einsum('bhgqk,bkhd->bqhgd', p, v)

    o = lax.map(one_block, qb)
    return o.transpose(1, 0, 2, 3, 4, 5).reshape(B, T, ATT_WIDTH)


def _both_dirs(x):
    return jnp.stack([x, jnp.flip(x, axis=1)], axis=0)


def _flip_second(x):
    return jnp.stack([x[0], jnp.flip(x[1], axis=1)], axis=0)


def rwkv7_scan(r, w, k, v, kk, b):
    def step(S, inp):
        r_t, w_t, k_t, v_t, kk_t, b_t = inp
        sa = jnp.einsum('dbhvk,dbhk->dbhv', S, kk_t)
        S = S * w_t[..., None, :] - sa[..., None] * b_t[..., None, :] + v_t[..., :, None] * k_t[..., None, :]
        return S, jnp.einsum('dbhvk,dbhk->dbhv', S, r_t)

    xs = tuple(jnp.moveaxis(a, 2, 0) for a in (r, w, k, v, kk, b))
    S0 = jnp.zeros(r.shape[:2] + (RWKV_HEADS, RWKV_HEAD_DIM, RWKV_HEAD_DIM), jnp.float32)
    _, ys = lax.scan(step, S0, xs)
    return jnp.moveaxis(ys, 0, 2)


def rwkv7_bidirectional(rkv, w_low, a_low, g_low, decay_w0, decay_w2, iclr_a0, iclr_a2, gate_g2,
                        k_k, k_a, r_k, lnx_w, lnx_b):
    B, T, _ = rkv.shape
    dt = rkv.dtype
    H, K = RWKV_HEADS, RWKV_HEAD_DIM
    f32 = jnp.float32
    heads = lambda z: z.reshape(z.shape[:-1] + (H, K))
    r, k, v = jnp.split(rkv.astype(f32), 3, axis=-1)
    wl = jnp.tanh(w_low.astype(f32).reshape(B, T, N_DIR, DECAY_LORA))
    w_raw = decay_w0.astype(f32)[:, None, None, :] + jnp.einsum('btdr,drc->dbtc', wl, decay_w2.astype(f32))
    decay = jnp.exp(-jnp.exp(-jax.nn.softplus(-w_raw) - 0.5))
    al = a_low.astype(f32).reshape(B, T, N_DIR, ICLR_LORA)
    a_gate = jax.nn.sigmoid(iclr_a0.astype(f32)[:, None, None, :]
                            + jnp.einsum('btdr,drc->dbtc', al, iclr_a2.astype(f32)))
    g = jax.nn.sigmoid(g_low.astype(f32)) @ gate_g2.astype(f32)
    kk = heads(k * k_k.astype(f32))
    kk = kk / jnp.maximum(jnp.sqrt(jnp.sum(kk * kk, axis=-1, keepdims=True)), 1e-12)
    k_dir = heads(k[None] * (1.0 + (a_gate - 1.0) * k_a.astype(f32)))
    b = heads(a_gate) * kk[None]
    rh, vh = heads(r), heads(v)
    ys = rwkv7_scan(_both_dirs(rh), _flip_second(heads(decay)), _flip_second(k_dir),
                    _both_dirs(vh), _both_dirs(kk), _flip_second(b))
    y = ys[0] + jnp.flip(ys[1], axis=1)
    mu = jnp.mean(y, axis=-1, keepdims=True)
    var = jnp.mean(jnp.square(y - mu), axis=-1, keepdims=True)
    y = (y - mu) * lax.rsqrt(var + LNX_EPS) * heads(lnx_w.astype(f32)) + heads(lnx_b.astype(f32))
    bonus = jnp.sum(rh[None] * k_dir * r_k.astype(f32), axis=-1, keepdims=True) * vh[None]
    y = y + jnp.sum(bonus, axis=0)
    return (y.reshape(B, T, H * K) * g).astype(dt)


def encoder_layer(x, rope, norm_mix, w_in, q_gain, k_gain, rwkv_conv, decay_w0, decay_w2, iclr_a0,
                  iclr_a2, gate_g2, k_k, k_a, r_k, lnx_w, lnx_b, w_up_attn, w_up_rwkv, w_o,
                  norm_ffn, w_ffn_up, ffn_conv, ffn_conv_b, w_ffn_down):
    B, T, _ = x.shape
    h = rms_norm(x, norm_mix)
    proj = h @ w_in
    points = np.cumsum(IN_SPLITS)[:-1].tolist()
    q, k, v, rkv, w_low, a_low, g_low, gate_att, gate_rwkv = jnp.split(proj, points, axis=-1)
    q = apply_axial_rope(rms_norm(q.reshape(B, T, ATT_Q_HEADS, ATT_HEAD_DIM), q_gain), rope)
    k = apply_axial_rope(rms_norm(k.reshape(B, T, ATT_KV_HEADS, ATT_HEAD_DIM), k_gain), rope)
    v = v.reshape(B, T, ATT_KV_HEADS, ATT_HEAD_DIM)
    att = block_attention(q, k, v)
    rkv = dwconv3(rkv, rwkv_conv)
    rw = rwkv7_bidirectional(rkv, w_low, a_low, g_low, decay_w0, decay_w2, iclr_a0, iclr_a2,
                             gate_g2, k_k, k_a, r_k, lnx_w, lnx_b)
    mixed = jax.nn.sigmoid(gate_att) * (att @ w_up_attn) + jax.nn.sigmoid(gate_rwkv) * (rw @ w_up_rwkv)
    x = x + mixed @ w_o
    h = rms_norm(x, norm_ffn)
    u = dwconv3(h @ w_ffn_up, ffn_conv, ffn_conv_b)
    val, gate = jnp.split(u, 2, axis=-1)
    return x + (jax.nn.silu(gate) * val) @ w_ffn_down


def run_trunk(x, weights):
    rope = axial_rope_tables(x.shape[1])
    for l in range(DEPTH):
        x = encoder_layer(x, rope, *[w[l] for w in weights])
    return x


def setup_inputs(seed: int = 0) -> dict:
    key = jax.random.key(seed)
    ks = jax.random.split(key, 25)
    L, D = DEPTH, D_MODEL
    f32 = jnp.float32

    def nrm(i, shape, scale):
        return scale * jax.random.normal(ks[i], shape, f32)

    x_prompt = nrm(0, (BATCH, SEQ, D), 1.0)
    x_sample = nrm(1, (DEC_BATCH, DEC_SEQ, D), 1.0)
    norm_mix = 1.0 + nrm(2, (L, D), 0.02)
    w_in = nrm(3, (L, D, IN_WIDTH), D ** -0.5)
    q_gain = 1.0 + nrm(4, (L, ATT_HEAD_DIM), 0.02)
    k_gain = 1.0 + nrm(5, (L, ATT_HEAD_DIM), 0.02)
    rwkv_conv = jnp.array([0.25, 0.5, 0.25], f32)[None, :, None] + nrm(6, (L, 3, 3 * RWKV_WIDTH), 0.1)
    decay_w0 = jax.random.uniform(ks[7], (L, N_DIR, RWKV_WIDTH), f32, -6.0, -1.0)
    decay_w2 = nrm(8, (L, N_DIR, DECAY_LORA, RWKV_WIDTH), 0.1 * DECAY_LORA ** -0.5)
    iclr_a0 = nrm(9, (L, N_DIR, RWKV_WIDTH), 0.1)
    iclr_a2 = nrm(10, (L, N_DIR, ICLR_LORA, RWKV_WIDTH), 0.3 * ICLR_LORA ** -0.5)
    gate_g2 = nrm(11, (L, GATE_LORA, RWKV_WIDTH), GATE_LORA ** -0.5)
    k_k = 0.85 + nrm(12, (L, RWKV_WIDTH), 0.05)
    k_a = 1.0 + nrm(13, (L, RWKV_WIDTH), 0.05)
    r_k = nrm(14, (L, RWKV_HEADS, RWKV_HEAD_DIM), 0.1)
    lnx_w = 1.0 + nrm(15, (L, RWKV_WIDTH), 0.02)
    lnx_b = nrm(16, (L, RWKV_WIDTH), 0.02)
    w_up_attn = nrm(17, (L, ATT_WIDTH, D), ATT_WIDTH ** -0.5)
    w_up_rwkv = nrm(18, (L, RWKV_WIDTH, D), RWKV_WIDTH ** -0.5)
    w_o = nrm(19, (L, D, D), D ** -0.5)
    norm_ffn = 1.0 + nrm(20, (L, D), 0.02)
    w_ffn_up = nrm(21, (L, D, 2 * D_FF), D ** -0.5)
    ffn_conv = nrm(22, (L, 3, 2 * D_FF), 3 ** -0.5)
    ffn_conv_b = nrm(23, (L, 2 * D_FF), 0.02)
    w_ffn_down = nrm(24, (L, D_FF, D), D_FF ** -0.5)
    return {"x_prompt": x_prompt, "x_sample": x_sample, "norm_mix": norm_mix, "w_in": w_in,
            "q_gain": q_gain, "k_gain": k_gain, "rwkv_conv": rwkv_conv, "decay_w0": decay_w0,
            "decay_w2": decay_w2, "iclr_a0": iclr_a0, "iclr_a2": iclr_a2, "gate_g2": gate_g2,
            "k_k": k_k, "k_a": k_a, "r_k": r_k, "lnx_w": lnx_w, "lnx_b": lnx_b,
            "w_up_attn": w_up_attn, "w_up_rwkv": w_up_rwkv, "w_o": w_o, "norm_ffn": norm_ffn,
            "w_ffn_up": w_ffn_up, "ffn_conv": ffn_conv, "ffn_conv_b": ffn_conv_b,
            "w_ffn_down": w_ffn_down}


def reference(x_prompt, x_sample, norm_mix, w_in, q_gain, k_gain, rwkv_conv, decay_w0, decay_w2,
              iclr_a0, iclr_a2, gate_g2, k_k, k_a, r_k, lnx_w, lnx_b, w_up_attn, w_up_rwkv, w_o,
              norm_ffn, w_ffn_up, ffn_conv, ffn_conv_b, w_ffn_down):
    weights = (norm_mix, w_in, q_gain, k_gain, rwkv_conv, decay_w0, decay_w2, iclr_a0, iclr_a2,
               gate_g2, k_k, k_a, r_k, lnx_w, lnx_b, w_up_attn, w_up_rwkv, w_o, norm_ffn,
               w_ffn_up, ffn_conv, ffn_conv_b, w_ffn_down)
    y_prompt = run_trunk(x_prompt, weights)
    y_sample = run_trunk(x_sample, weights)
    return (y_prompt, y_sample)
```

```python
import math
from contextlib import ExitStack
import numpy as np
import ml_dtypes
import concourse.bass as bass
import concourse.mybir as mybir
from concourse.bass_utils import run_bass_kernel_spmd

F32 = mybir.dt.float32
BF16 = mybir.dt.bfloat16
AF = mybir.ActivationFunctionType
ALU = mybir.AluOpType
NP_BF16 = ml_dtypes.bfloat16


class Cfg:
    def __init__(s, D=4096, SEQ=2048, BATCH=4, DEC_BATCH=2, L=2):
        s.D = D; s.SEQ = SEQ; s.BATCH = BATCH; s.DEC_BATCH = DEC_BATCH; s.L = L
        s.N = 2 * SEQ
        s.AW = D // 2; s.QH = s.AW // 128; s.KVH = s.QH // 4; s.KVW = s.KVH * 128
        s.RW = D // 2; s.RH = s.RW // 64
        s.DL = 128; s.IL = 128; s.GL = 480
        s.DFF = 256 * (-(-(8 * D) // (3 * 256)))
        s.SPL = (s.AW, s.KVW, s.KVW, 3 * s.RW, 2 * s.DL, 2 * s.IL, s.GL, D, D)
        s.INW = sum(s.SPL)
        s.C = 64; s.NCH = s.N // 64
        s.NB = s.RW // 128
        s.FB = 2 * s.DFF // 128
        o = 0
        s.po = {}
        for nm, w in (("g1", D // 128), ("g2", D // 128), ("qg", 1), ("kg", 1), ("conv", 3 * 3 * s.NB),
                      ("w0", 2 * s.NB), ("a0", 2 * s.NB), ("kk", s.NB), ("ka", s.NB), ("rk", s.NB),
                      ("lw", s.NB), ("lb", s.NB), ("fc", 3 * s.FB), ("fb", s.FB)):
            s.po[nm] = o; o += w
        s.PC = o


class KB:
    R = 8

    def __init__(s, nc):
        s.nc = nc
        s.es = ExitStack()
        s.eng = {"pe": nc.tensor, "act": nc.scalar, "dve": nc.vector, "pool": nc.gpsimd, "sp": nc.sync}
        s.csem = {e: s.es.enter_context(nc.semaphore("c_" + e)) for e in ("pe", "act", "dve", "pool")}
        s.ccnt = {e: 0 for e in s.csem}
        s.ring = {q: [s.es.enter_context(nc.semaphore("d_%s%d" % (q, i))) for i in range(s.R)]
                  for q in ("sp", "pool", "act")}
        s.dcnt = {q: 0 for q in s.ring}
        s.seen = {e: {} for e in s.eng}
        s.lastw = {}
        s.rd_c = {}
        s.rd_d = {}
        s.pending = {e: [] for e in s.csem}
        s.ps_i = 0

    def _wait(s, e, t):
        if t["eng"] == "pe" and e == "pe":
            return
        assert t["val"] is not None, "unresolved ticket"
        k = id(t["sem"])
        if s.seen[e].get(k, 0) >= t["val"]:
            return
        s.eng[e].wait_ge(t["sem"], t["val"])
        s.seen[e][k] = t["val"]

    def _deps(s, e, reads, writes):
        for k in reads:
            t = s.lastw.get(k)
            if t is not None:
                s._wait(e, t)
        for k in writes:
            t = s.lastw.get(k)
            if t is not None:
                s._wait(e, t)
            for t in s.rd_c.get(k, {}).values():
                s._wait(e, t)
            for t in s.rd_d.get(k, ()):
                s._wait(e, t)

    def _record(s, tk, reads, writes, isdma):
        for k in reads:
            if isdma:
                s.rd_d.setdefault(k, []).append(tk)
            else:
                s.rd_c.setdefault(k, {})[tk["eng"]] = tk
        for k in writes:
            s.lastw[k] = tk
            s.rd_c[k] = {}
            s.rd_d[k] = []

    def op(s, e, fn, reads=(), writes=(), signal=True):
        s._deps(e, reads, writes)
        ins = fn(s.eng[e])
        tk = {"sem": s.csem[e], "val": None, "eng": e}
        if signal:
            s.ccnt[e] += 1
            ins.then_inc(s.csem[e], 1)
            tk["val"] = s.ccnt[e]
            for p in s.pending[e]:
                p["val"] = s.ccnt[e]
            s.pending[e] = []
        else:
            s.pending[e].append(tk)
        s._record(tk, reads, writes, False)
        return tk

    def dma(s, q, out, in_, reads=(), writes=()):
        s._deps(q, reads, writes)
        j = s.dcnt[q]
        slot = j % s.R
        sem = s.ring[q][slot]
        if j >= s.R:
            s._wait(q, {"sem": sem, "val": 16 * (j // s.R), "eng": "dma"})
        s.eng[q].dma_start(out=out, in_=in_).then_inc(sem, 16)
        s.dcnt[q] += 1
        tk = {"sem": sem, "val": 16 * (j // s.R + 1), "eng": "dma"}
        s._record(tk, reads, writes, True)
        return tk

    def barrier(s):
        tks = []
        for e in s.csem:
            assert not s.pending[e], "pending unsignaled op at barrier on " + e
            if s.ccnt[e]:
                tks.append({"sem": s.csem[e], "val": s.ccnt[e], "eng": "x"})
        for q in s.ring:
            for i in range(s.R):
                if s.dcnt[q] > i:
                    uses = (s.dcnt[q] - i + s.R - 1) // s.R
                    tks.append({"sem": s.ring[q][i], "val": 16 * uses, "eng": "dma"})
        for e in s.eng:
            for t in tks:
                s._wait(e, t)
        s.lastw = {}
        s.rd_c = {}
        s.rd_d = {}


def build(cfg, debug_outs=()):
    c = cfg
    nc = bass.Bass("TRN2", target_bir_lowering=False)
    D, N, L = c.D, c.N, c.L
    NT = N // 512
    HALF = N // 2

    def din(name, shape, dt=F32):
        return nc.dram_tensor(name, list(shape), dt, kind="ExternalInput").ap()

    def dsc(name, shape, dt):
        kind = "ExternalOutput" if name in debug_outs else "Internal"
        return nc.dram_tensor(name, list(shape), dt, kind=kind).ap()

    x_in = din("x", [N, D])
    w_in = din("w_in", [L, D, c.INW])
    w_ua = din("w_up_attn", [L, c.AW, D])
    w_ur = din("w_up_rwkv", [L, c.RW, D])
    w_o = din("w_o", [L, D, D])
    w_fu = din("w_ffn_up", [L, D, 2 * c.DFF])
    w_fd = din("w_ffn_down", [L, c.DFF, D])
    dw2 = din("decay_w2", [L, 2, c.DL, c.RW])
    ia2 = din("iclr_a2", [L, 2, c.IL, c.RW])
    g2 = din("gate_g2", [L, c.GL, c.RW])
    pv = din("pv", [L, 128, c.PC])
    c_attb = din("c_attb", [128, (N // 128) * 2])
    c_carry = din("c_carry", [64, 2 * c.NCH])
    c_edge = din("c_edge", [128, 1])
    c_rope = din("c_rope", [128, 2, N])
    c_mats = din("c_mats", [128, 4, 128], BF16)
    c_tri = din("c_tri", [64, 5, 8, 64], BF16)
    c_reset = din("c_reset", [128, 512])
    y_out = nc.dram_tensor("y", [N, D], F32, kind="ExternalOutput").ap()

    hT = dsc("hT", [D, N], BF16)
    qraw = dsc("qraw", [c.AW + c.KVW, N], F32)
    vT = dsc("vT", [c.KVW, N], BF16)
    rkvraw = dsc("rkvraw", [3 * c.RW, N], F32)
    wlowT = dsc("wlowT", [2 * c.DL, N], BF16)
    alowT = dsc("alowT", [2 * c.IL, N], BF16)
    glowT = dsc("glowT", [c.GL, N], BF16)
    gA = dsc("gA", [D, N], BF16)
    gR = dsc("gR", [D, N], BF16)
    QKT = dsc("QKT", [c.AW + c.KVW, N], BF16)
    Vtm = dsc("Vtm", [N, c.KVW], BF16)
    attT = dsc("attT", [c.AW, N], BF16)
    sA = [dsc("sA%d" % d, [c.RW, N], BF16) for d in range(2)]
    sB = [dsc("sB%d" % d, [c.RW, N], BF16) for d in range(2)]
    sK = [dsc("sK%d" % d, [c.RW, N], BF16) for d in range(2)]
    sR = [dsc("sR%d" % d, [c.RW, N], BF16) for d in range(2)]
    sBh = [dsc("sBh%d" % d, [N, c.RW], BF16) for d in range(2)]
    sKh = [dsc("sKh%d" % d, [N, c.RW], BF16) for d in range(2)]
    sV = dsc("sV", [N, c.RW], BF16)
    sG = [dsc("sG%d" % d, [c.RW, c.NCH], F32) for d in range(2)]
    bonus = dsc("bonus", [c.RW, N], F32)
    gout = dsc("gout", [c.RW, N], F32)
    yT = [dsc("yT%d" % d, [c.RW, N], F32) for d in range(2)]
    rwT = dsc("rwT", [c.RW, N], BF16)
    mixT = dsc("mixT", [D, N], BF16)
    x1 = dsc("x1", [N, D], F32)
    x2 = dsc("x2", [N, D], F32)
    uT = dsc("uT", [2 * c.DFF, N], BF16)
    actT = dsc("actT", [c.DFF, N], BF16)

    _uid = [0]

    def SBT(name, shape, dt):
        _uid[0] += 1
        return nc.sbuf_tensor("%s_u%d" % (name, _uid[0]), shape, dt)

    kb = KB(nc)
    with kb.es:
        es0 = kb.es
        PS = [es0.enter_context(nc.psum_tensor("ps%d" % i, [128, 512], F32)) for i in range(6)]
        PSB = [es0.enter_context(nc.psum_tensor("psb%d" % i, [128, 1024], BF16)) for i in range(2)]
        mats = es0.enter_context(SBT("mats", [128, 4, 128], BF16))
        tri = es0.enter_context(SBT("tri", [64, 5, 8, 64], BF16))
        pvs = es0.enter_context(SBT("pvs", [128, c.PC], F32))
        edge = es0.enter_context(SBT("edge", [128, 1], F32))
        kb.dma("sp", mats[:], c_mats, writes=["mats"])
        kb.dma("sp", tri[:], c_tri, writes=["tri"])
        kb.dma("sp", edge[:], c_edge, writes=["edge"])
        ident = mats[:, 0, :]
        ones = mats[:, 1, :]
        blk = mats[:, 2, :]
        permT = mats[:, 3, :]
        ps_state = {"i": 0, "b": 0}

        def next_ps(lo=0, hi=6):
            i = lo + ps_state["i"] % (hi - lo)
            ps_state["i"] += 1
            return PS[i], ("ps", i)

        def next_psb():
            i = ps_state["b"] % 2
            ps_state["b"] += 1
            return PSB[i], ("psb", i)

        def rsqrt_to(dst, dkey, src, skey, scale, bias):
            kb.op("act", lambda e: e.activation(out=dst, in_=src, func=AF.Sqrt, bias=float(bias), scale=float(scale)),
                  reads=[skey], writes=[dkey])
            kb.op("dve", lambda e: e.reciprocal(out=dst, in_=dst), reads=[dkey], writes=[dkey])

        def pcol(name, j=0, n=1):
            o = c.po[name] + j
            return pvs[:, o:o + n]

        def phase_norm(xsrc, gname):
            with ExitStack() as es:
                KC = D // 128
                xt = [es.enter_context(SBT("n_xt%d" % i, [128, D], F32)) for i in range(2)]
                xb = [es.enter_context(SBT("n_xb%d" % i, [128, D], BF16)) for i in range(2)]
                junk = es.enter_context(SBT("n_junk", [128, D], BF16))
                st = es.enter_context(SBT("n_st", [128, 8], F32))
                hb = [es.enter_context(SBT("n_hb%d" % i, [128, KC, 512], BF16)) for i in range(2)]
                for ti in range(N // 128):
                    b = ti % 2
                    g = ti // 4
                    hbb = hb[g % 2]
                    kb.dma("sp", xt[b][:], xsrc[ti * 128:(ti + 1) * 128, :], writes=[("xt", b)])
                    sc = st[:, b * 4:b * 4 + 1]
                    rs = st[:, b * 4 + 1:b * 4 + 2]
                    kb.op("act", lambda e: e.activation(out=junk[:], in_=xt[b][:], func=AF.Square, accum_out=sc),
                          reads=[("xt", b)], writes=["junk", ("ss", b)])
                    rsqrt_to(rs, ("rs", b), sc, ("ss", b), 1.0 / D, 1e-6)
                    kb.op("act", lambda e: e.activation(out=xb[b][:], in_=xt[b][:], func=AF.Copy, scale=rs),
                          reads=[("xt", b), ("rs", b)], writes=[("xb", b)])
                    for q in range(KC // 8):
                        pt, pk = next_psb()
                        for j in range(8):
                            cc = q * 8 + j
                            kb.op("pe", lambda e: e.transpose(pt[:, j * 128:(j + 1) * 128],
                                                              xb[b][:, cc * 128:(cc + 1) * 128], ident),
                                  reads=[("xb", b), "mats"], writes=[pk], signal=(j == 7))
                        for j in range(8):
                            cc = q * 8 + j
                            eng = "act" if j % 2 == 0 else "dve"
                            dst = hbb[:, cc, (ti % 4) * 128:(ti % 4 + 1) * 128]
                            src = pt[:, j * 128:(j + 1) * 128]
                            gcol = pcol(gname, cc)
                            if eng == "act":
                                kb.op("act", lambda e: e.activation(out=dst, in_=src, func=AF.Copy, scale=gcol),
                                      reads=[pk, "pvs"], writes=[("hb", g % 2)])
                            else:
                                kb.op("dve", lambda e: e.tensor_scalar(out=dst, in0=src, scalar1=gcol, scalar2=None,
                                                                      op0=ALU.mult),
                                      reads=[pk, "pvs"], writes=[("hb", g % 2)])
                    if ti % 4 == 3:
                        kb.dma("sp", hT.rearrange("(c p) n -> p c n", p=128)[:, :, g * 512:(g + 1) * 512], hbb[:],
                               reads=[("hb", g % 2)])
                kb.barrier()

        def gemm_fm(pairs, groups, TT, epi):
            with ExitStack() as es:
                KCs = [(K + 127) // 128 for (_, _, K) in pairs]
                Xs = [es.enter_context(SBT("g_x%d" % i, [128, KCs[i], TT], BF16))
                      for i in range(len(pairs))]
                Ws = [[es.enter_context(SBT("g_w%d_%d" % (i, b), [128, KCs[i], 512], BF16))
                       for b in range(2)] for i in range(len(pairs))]
                wi = 0
                for st in range(N // TT):
                    for i, (X, W, K) in enumerate(pairs):
                        for kc in range(KCs[i]):
                            r = min(128, K - kc * 128)
                            kb.dma("sp", Xs[i][0:r, kc, :], X[kc * 128:kc * 128 + r, st * TT:(st + 1) * TT],
                                   writes=[("gx", i)])
                    for (c0, gw, blocks) in groups:
                        wb = wi % 2
                        wi += 1
                        for i, (X, W, K) in enumerate(pairs):
                            if K % 128 == 0:
                                kb.dma("pool", Ws[i][wb][:, :, 0:gw],
                                       W.rearrange("(c p) m -> p c m", p=128)[:, :, c0:c0 + gw],
                                       writes=[("gw", i, wb)])
                            else:
                                for kc in range(KCs[i]):
                                    r = min(128, K - kc * 128)
                                    kb.dma("pool", Ws[i][wb][0:r, kc, 0:gw], W[kc * 128:kc * 128 + r, c0:c0 + gw],
                                           writes=[("gw", i, wb)])
                        for (off, w, tag) in blocks:
                            for tg in range(TT // 512):
                                psl, pkl = [], []
                                for i, (X, W, K) in enumerate(pairs):
                                    pt, pk = next_ps()
                                    psl.append(pt[0:w, :])
                                    pkl.append(pk)
                                    for kc in range(KCs[i]):
                                        r = min(128, K - kc * 128)
                                        kb.op("pe", lambda e: e.matmul(
                                            pt[0:w, :], lhsT=Ws[i][wb][0:r, kc, off:off + w],
                                            rhs=Xs[i][0:r, kc, tg * 512:(tg + 1) * 512],
                                            start=(kc == 0), stop=(kc == KCs[i] - 1)),
                                            reads=[("gw", i, wb), ("gx", i)], writes=[pk],
                                            signal=(kc == KCs[i] - 1))
                                epi(psl, pkl, tag, c0 + off, w, st * TT + tg * 512)
                kb.barrier()

        def gemm_tm(X, W, K, xres, xdst):
            with ExitStack() as es:
                KC = (K + 127) // 128
                nwb = 2 if KC <= 32 else 1
                Wb = [es.enter_context(SBT("t_w%d" % b, [128, KC, 512], BF16)) for b in range(nwb)]
                Xb = [es.enter_context(SBT("t_x%d" % b, [128, KC, 128], BF16)) for b in range(3)]
                Rb = [es.enter_context(SBT("t_r%d" % b, [128, 512], F32)) for b in range(3)]
                xi = 0
                for cb in range(D // 512):
                    wb = cb % nwb
                    for kc0 in range(0, KC, 16):
                        kc1 = min(KC, kc0 + 16)
                        kb.dma("pool", Wb[wb][:, kc0:kc1, :],
                               W.rearrange("(c p) m -> p c m", p=128)[:, kc0:kc1, cb * 512:(cb + 1) * 512],
                               writes=[("tw", wb)])
                    for tt in range(N // 128):
                        b = xi % 3
                        xi += 1
                        kb.dma("sp", Xb[b][:], X.rearrange("(c p) n -> p c n", p=128)[:, :, tt * 128:(tt + 1) * 128],
                               writes=[("tx", b)])
                        kb.dma("sp", Rb[b][:], xres[tt * 128:(tt + 1) * 128, cb * 512:(cb + 1) * 512],
                               writes=[("tr", b)])
                        pt, pk = next_ps()
                        for kc in range(KC):
                            kb.op("pe", lambda e: e.matmul(pt[:, :], lhsT=Xb[b][:, kc, :], rhs=Wb[wb][:, kc, :],
                                                           start=(kc == 0), stop=(kc == KC - 1)),
                                  reads=[("tx", b), ("tw", wb)], writes=[pk], signal=(kc == KC - 1))
                        kb.op("dve", lambda e: e.tensor_tensor(out=Rb[b][:], in0=pt[:, :], in1=Rb[b][:], op=ALU.add),
                              reads=[pk, ("tr", b)], writes=[("tr", b)])
                        kb.dma("sp", xdst[tt * 128:(tt + 1) * 128, cb * 512:(cb + 1) * 512], Rb[b][:],
                               reads=[("tr", b)])
                kb.barrier()

        class Evac:
            def __init__(s, es, name, dt, nbuf=3):
                s.bufs = [es.enter_context(SBT("%s%d" % (name, i), [128, 512], dt)) for i in range(nbuf)]
                s.i = 0
                s.name = name

            def get(s):
                b = s.i % len(s.bufs)
                s.i += 1
                return s.bufs[b], (s.name, b)

        def phase_inproj(l):
            blocks = []
            o = 0
            segs = [("q", c.AW + c.KVW), ("v", c.KVW), ("rkv", 3 * c.RW), ("wl", 2 * c.DL), ("al", 2 * c.IL),
                    ("gl", c.GL), ("ga", D), ("gr", D)]
            for tag, wd in segs:
                so = 0
                while so < wd:
                    w = min(128, wd - so)
                    blocks.append((o + so, w, (tag, so)))
                    so += w
                o += wd
            groups = []
            cur = None
            for (a, w, tag) in blocks:
                if cur is None or (a + w - cur[0]) > 512:
                    cur = [a, 0, []]
                    groups.append(cur)
                cur[2].append((a - cur[0], w, tag))
                cur[1] = a + w - cur[0]
            with ExitStack() as es:
                ev32 = Evac(es, "ip_f", F32)
                ev16 = Evac(es, "ip_b", BF16)
                dst = {"q": (qraw, F32, AF.Copy), "v": (vT, BF16, AF.Copy), "rkv": (rkvraw, F32, AF.Copy),
                       "wl": (wlowT, BF16, AF.Tanh), "al": (alowT, BF16, AF.Copy), "gl": (glowT, BF16, AF.Sigmoid),
                       "ga": (gA, BF16, AF.Sigmoid), "gr": (gR, BF16, AF.Sigmoid)}
                cnt = [0]

                def epi(psl, pkl, tag, offabs, w, tok0):
                    dten, dt, fn = dst[tag[0]]
                    buf, bk = (ev32 if dt == F32 else ev16).get()
                    cnt[0] += 1
                    if fn == AF.Copy and cnt[0] % 2 == 0:
                        kb.op("dve", lambda e: e.tensor_copy(out=buf[0:w, :], in_=psl[0]),
                              reads=[pkl[0]], writes=[bk])
                    else:
                        kb.op("act", lambda e: e.activation(out=buf[0:w, :], in_=psl[0], func=fn),
                              reads=[pkl[0]], writes=[bk])
                    kb.dma("sp", dten[tag[1]:tag[1] + w, tok0:tok0 + 512], buf[0:w, :], reads=[bk])

                gemm_fm([(hT, w_in[l], D)], [tuple(g) for g in groups], min(N, 1024), epi)

        def phase_qkprep():
            with ExitStack() as es:
                rope = es.enter_context(SBT("rope", [128, 2, N], F32))
                kb.dma("sp", rope[:], c_rope, writes=["rope"])
                nb = 2
                T = lambda nm, dt: [es.enter_context(SBT("%s%d" % (nm, i), [128, 512], dt))
                                    for i in range(nb)]
                raw, rg, sq, xbf = T("q_raw", F32), T("q_rg", F32), T("q_sq", BF16), T("q_xb", BF16)
                rstd, t1, t2, ob = T("q_rs", F32), T("q_t1", F32), T("q_t2", F32), T("q_ob", BF16)
                it = 0
                for hd in range(c.QH + c.KVH):
                    gcol = pcol("qg") if hd < c.QH else pcol("kg")
                    for tg in range(NT):
                        b = it % nb
                        it += 1
                        ts = slice(tg * 512, (tg + 1) * 512)
                        kb.dma("sp", raw[b][:], qraw[hd * 128:(hd + 1) * 128, ts], writes=[("raw", b)])
                        kb.op("act", lambda e: e.activation(out=sq[b][:], in_=raw[b][:], func=AF.Square),
                              reads=[("raw", b)], writes=[("sq", b)])
                        kb.op("act", lambda e: e.activation(out=rg[b][:], in_=raw[b][:], func=AF.Copy, scale=gcol),
                              reads=[("raw", b), "pvs"], writes=[("rg", b)])
                        kb.op("dve", lambda e: e.tensor_copy(out=xbf[b][:], in_=rg[b][:]),
                              reads=[("rg", b)], writes=[("xbf", b)])
                        p1, k1 = next_ps()
                        kb.op("pe", lambda e: e.matmul(p1[:, :], lhsT=ones, rhs=sq[b][:], start=True, stop=True),
                              reads=[("sq", b), "mats"], writes=[k1])
                        p2, k2 = next_ps()
                        kb.op("pe", lambda e: e.matmul(p2[:, :], lhsT=permT, rhs=xbf[b][:], start=True, stop=True),
                              reads=[("xbf", b), "mats"], writes=[k2])
                        rsqrt_to(rstd[b][:], ("rstd", b), p1[:, :], k1, 1.0 / 128, 1e-6)
                        kb.op("dve", lambda e: e.tensor_tensor(out=t1[b][:], in0=rg[b][:], in1=rope[:, 0, ts],
                                                              op=ALU.mult),
                              reads=[("rg", b), "rope"], writes=[("t1", b)])
                        kb.op("dve", lambda e: e.tensor_tensor(out=t2[b][:], in0=p2[:, :], in1=rope[:, 1, ts],
                                                              op=ALU.mult),
                              reads=[k2, "rope"], writes=[("t2", b)])
                        kb.op("dve", lambda e: e.tensor_tensor(out=t1[b][:], in0=t1[b][:], in1=t2[b][:], op=ALU.add),
                              reads=[("t1", b), ("t2", b)], writes=[("t1", b)])
                        kb.op("dve", lambda e: e.tensor_tensor(out=ob[b][:], in0=t1[b][:], in1=rstd[b][:],
                                                              op=ALU.mult),
                              reads=[("t1", b), ("rstd", b)], writes=[("ob", b)])
                        kb.dma("sp", QKT[hd * 128:(hd + 1) * 128, ts], ob[b][:], reads=[("ob", b)])
                vin = T("q_vin", BF16)
                vo = [es.enter_context(SBT("q_vo%d" % i, [128, 4, 128], BF16)) for i in range(2)]
                it = 0
                for h in range(c.KVH):
                    for tg in range(NT):
                        b = it % 2
                        it += 1
                        kb.dma("sp", vin[b][:], vT[h * 128:(h + 1) * 128, tg * 512:(tg + 1) * 512],
                               writes=[("vin", b)])
                        pt, pk = next_psb()
                        for j in range(4):
                            kb.op("pe", lambda e: e.transpose(pt[:, j * 128:(j + 1) * 128],
                                                              vin[b][:, j * 128:(j + 1) * 128], ident),
                                  reads=[("vin", b), "mats"], writes=[pk], signal=(j == 3))
                        kb.op("dve", lambda e: e.tensor_copy(out=vo[b][:].rearrange("p a b -> p (a b)"),
                                                            in_=pt[:, 0:512]),
                              reads=[pk], writes=[("vo", b)])
                        kb.dma("sp", Vtm.rearrange("(a p) f -> p a f", p=128)[:, tg * 4:(tg + 1) * 4,
                                                                             h * 128:(h + 1) * 128],
                               vo[b][:], reads=[("vo", b)])
                kb.barrier()

        def phase_attn():
            with ExitStack() as es:
                NKT = N // 128
                attb = es.enter_context(SBT("attb", [128, NKT * 2], F32))
                kb.dma("sp", attb[:], c_attb, writes=["attb"])
                Kt = es.enter_context(SBT("a_K", [128, N], BF16))
                Vt = es.enter_context(SBT("a_V", [128, NKT, 128], BF16))
                Qt = [es.enter_context(SBT("a_Q%d" % i, [128, N], BF16)) for i in range(2)]
                Pb = [es.enter_context(SBT("a_P%d" % i, [128, 512], BF16)) for i in range(3)]
                rz = [es.enter_context(SBT("a_rz%d" % i, [128, 512], F32)) for i in range(2)]
                ob = [es.enter_context(SBT("a_o%d" % i, [128, 512], BF16)) for i in range(2)]
                scale = 128.0 ** -0.5
                qi = 0
                pi = 0
                oi = 0
                for h in range(c.KVH):
                    kb.dma("sp", Kt[:], QKT[c.AW + h * 128:c.AW + (h + 1) * 128, :], writes=["aK"])
                    kb.dma("sp", Vt[:], Vtm.rearrange("(a p) f -> p a f", p=128)[:, :, h * 128:(h + 1) * 128],
                           writes=["aV"])
                    for g in range(4):
                        qh = h * 4 + g
                        qb = qi % 2
                        qi += 1
                        kb.dma("sp", Qt[qb][:], QKT[qh * 128:(qh + 1) * 128, :], writes=[("aQ", qb)])
                        for qt in range(NT):
                            a = oi % 2
                            oi += 1
                            po, ko = PS[a * 2], ("ps", a * 2)
                            pz, kz = PS[a * 2 + 1], ("ps", a * 2 + 1)
                            qhalf = 0 if (qt * 512) < HALF else 1
                            for kt in range(NKT):
                                pst, kst = next_ps(4, 6)
                                kb.op("pe", lambda e: e.matmul(pst[:, :], lhsT=Kt[:, kt * 128:(kt + 1) * 128],
                                                               rhs=Qt[qb][:, qt * 512:(qt + 1) * 512],
                                                               start=True, stop=True),
                                      reads=["aK", ("aQ", qb)], writes=[kst])
                                pb = pi % 3
                                pi += 1
                                kb.op("act", lambda e: e.activation(out=Pb[pb][:], in_=pst[:, :], func=AF.Exp,
                                                                    bias=attb[:, kt * 2 + qhalf:kt * 2 + qhalf + 1],
                                                                    scale=scale),
                                      reads=[kst, "attb"], writes=[("aP", pb)])
                                kb.op("pe", lambda e: e.matmul(po[:, :], lhsT=Vt[:, kt, :], rhs=Pb[pb][:],
                                                               start=(kt == 0), stop=(kt == NKT - 1)),
                                      reads=["aV", ("aP", pb)], writes=[ko], signal=False)
                                kb.op("pe", lambda e: e.matmul(pz[:, :], lhsT=ones, rhs=Pb[pb][:],
                                                               start=(kt == 0), stop=(kt == NKT - 1)),
                                      reads=["mats", ("aP", pb)], writes=[kz], signal=True)
                            kb.op("dve", lambda e: e.reciprocal(out=rz[a][:], in_=pz[:, :]),
                                  reads=[kz], writes=[("rz", a)])
                            kb.op("dve", lambda e: e.tensor_tensor(out=ob[a][:], in0=po[:, :], in1=rz[a][:],
                                                                  op=ALU.mult),
                                  reads=[ko, ("rz", a)], writes=[("ao", a)])
                            kb.dma("sp", attT[qh * 128:(qh + 1) * 128, qt * 512:(qt + 1) * 512], ob[a][:],
                                   reads=[("ao", a)])
                kb.barrier()

        def phase_rwkvprep(l):
            with ExitStack() as es:
                NB = c.NB
                rst = es.enter_context(SBT("rp_rst", [128, 512], F32))
                kb.dma("sp", rst[:], c_reset, writes=["rst"])
                w2s = es.enter_context(SBT("rp_w2", [128, 2, c.RW], BF16))
                a2s = es.enter_context(SBT("rp_a2", [128, 2, c.RW], BF16))
                g2s = es.enter_context(SBT("rp_g2", [128, 4, c.RW], BF16))
                for d in range(2):
                    kb.dma("pool", w2s[:, d, :], dw2[l, d], writes=["w2s"])
                    kb.dma("pool", a2s[:, d, :], ia2[l, d], writes=["a2s"])
                for kc in range(4):
                    r = min(128, c.GL - kc * 128)
                    kb.dma("pool", g2s[0:r, kc, :], g2[l, kc * 128:kc * 128 + r, :], writes=["g2s"])
                names32 = ["rr", "kr", "vr", "r", "k", "v", "kku", "kk", "ag", "lw", "cum", "exc", "kd", "bb", "ee",
                           "tmp", "tmp2", "gg", "bon"]
                S = {nm: es.enter_context(SBT("rp_" + nm, [128, 514 if nm in ("rr", "kr", "vr") else 512],
                                                         F32)) for nm in names32}
                names16 = ["wl0", "wl1", "al0", "al1", "sqb", "rkb", "oA", "oB", "oK", "oR", "oBh", "oKh", "vb"]
                Sb = {nm: es.enter_context(SBT("rp_" + nm, [128, 512], BF16)) for nm in names16}
                glb = es.enter_context(SBT("rp_gl", [128, 4, 512], BF16))
                tot = es.enter_context(SBT("rp_tot", [128, 32], F32))
                otm = [es.enter_context(SBT("rp_otm%d" % i, [128, 4, 128], BF16)) for i in range(3)]
                nchk = 512 // 64

                def V(eng, fn, r, w):
                    kb.op(eng, fn, reads=r, writes=w)

                for tg in range(NT):
                    t0 = tg * 512
                    ts = slice(t0, t0 + 512)
                    for d in range(2):
                        kb.dma("sp", Sb["wl%d" % d][:], wlowT[d * c.DL:(d + 1) * c.DL, ts], writes=["wl%d" % d])
                        kb.dma("sp", Sb["al%d" % d][:], alowT[d * c.IL:(d + 1) * c.IL, ts], writes=["al%d" % d])
                    for kc in range(4):
                        r = min(128, c.GL - kc * 128)
                        kb.dma("sp", glb[0:r, kc, :], glowT[kc * 128:kc * 128 + r, ts], writes=["glb"])
                    for cb in range(NB):
                        cs = slice(cb * 128, (cb + 1) * 128)
                        for i, nm in enumerate(("rr", "kr", "vr")):
                            lo = max(t0 - 1, 0)
                            hi = min(t0 + 513, N)
                            kb.dma("sp", S[nm][:, (lo - (t0 - 1)):(hi - (t0 - 1))],
                                   rkvraw[i * c.RW + cb * 128:i * c.RW + (cb + 1) * 128, lo:hi], writes=[nm])
                            if lo == 0 and t0 == 0:
                                V("dve", lambda e: e.memset(S[nm][:, 0:1], 0.0), [], [nm])
                            if hi == N and t0 + 512 == N:
                                V("dve", lambda e: e.memset(S[nm][:, 513:514], 0.0), [], [nm])
                            on = ("r", "k", "v")[i]
                            cw = lambda tap: pcol("conv", (tap * 3 + i) * NB + cb)
                            src = S[nm]
                            V("dve", lambda e: e.tensor_scalar(out=S[on][:], in0=src[:, 1:513], scalar1=cw(1),
                                                               scalar2=None, op0=ALU.mult), [nm, "pvs"], [on])
                            V("dve", lambda e: e.scalar_tensor_tensor(out=S[on][:], in0=src[:, 0:512], scalar=cw(0),
                                                                      in1=S[on][:], op0=ALU.mult, op1=ALU.add),
                              [nm, on, "pvs"], [on])
                            V("dve", lambda e: e.scalar_tensor_tensor(out=S[on][:], in0=src[:, 2:514], scalar=cw(2),
                                                                      in1=S[on][:], op0=ALU.mult, op1=ALU.add),
                              [nm, on, "pvs"], [on])
                            if t0 == HALF:
                                V("dve", lambda e: e.tensor_scalar(out=S["tmp"][:, 0:1], in0=src[:, 0:1],
                                                                   scalar1=cw(0), scalar2=edge[:, 0:1],
                                                                   op0=ALU.mult, op1=ALU.mult),
                                  [nm, "pvs", "edge"], ["tmp"])
                                V("dve", lambda e: e.tensor_tensor(out=S[on][:, 0:1], in0=S[on][:, 0:1],
                                                                   in1=S["tmp"][:, 0:1], op=ALU.add),
                                  ["tmp", on], [on])
                            if t0 + 512 == HALF:
                                V("dve", lambda e: e.tensor_scalar(out=S["tmp"][:, 0:1], in0=src[:, 513:514],
                                                                   scalar1=cw(2), scalar2=edge[:, 0:1],
                                                                   op0=ALU.mult, op1=ALU.mult),
                                  [nm, "pvs", "edge"], ["tmp"])
                                V("dve", lambda e: e.tensor_tensor(out=S[on][:, 511:512], in0=S[on][:, 511:512],
                                                                   in1=S["tmp"][:, 0:1], op=ALU.add),
                                  ["tmp", on], [on])
                        V("dve", lambda e: e.tensor_scalar(out=S["kku"][:], in0=S["k"][:], scalar1=pcol("kk", cb),
                                                           scalar2=None, op0=ALU.mult), ["k", "pvs"], ["kku"])
                        V("act", lambda e: e.activation(out=Sb["sqb"][:], in_=S["kku"][:], func=AF.Square),
                          ["kku"], ["sqb"])
                        p1, k1 = next_ps()
                        kb.op("pe", lambda e: e.matmul(p1[:, :], lhsT=blk, rhs=Sb["sqb"][:], start=True, stop=True),
                              reads=["sqb", "mats"], writes=[k1])
                        rsqrt_to(S["tmp"][:], "tmp", p1[:, :], k1, 1.0, 1e-24)
                        V("dve", lambda e: e.tensor_tensor(out=S["kk"][:], in0=S["kku"][:], in1=S["tmp"][:],
                                                           op=ALU.mult), ["kku", "tmp"], ["kk"])
                        V("act", lambda e: e.activation(out=Sb["vb"][:], in_=S["v"][:], func=AF.Copy), ["v"], ["vb"])
                        pg, kg_ = next_ps()
                        for kc in range(4):
                            r = min(128, c.GL - kc * 128)
                            kb.op("pe", lambda e: e.matmul(pg[:, :], lhsT=g2s[0:r, kc, cs], rhs=glb[0:r, kc, :],
                                                           start=(kc == 0), stop=(kc == 3)),
                                  reads=["g2s", "glb"], writes=[kg_], signal=(kc == 3))
                        V("act", lambda e: e.activation(out=S["gg"][:], in_=pg[:, :], func=AF.Copy), [kg_], ["gg"])
                        kb.dma("sp", gout[cs, ts], S["gg"][:], reads=["gg"])
                        pbn, kbn = next_ps()
                        for d in range(2):
                            pw, kw = next_ps()
                            kb.op("pe", lambda e: e.matmul(pw[:, :], lhsT=w2s[:, d, cs], rhs=Sb["wl%d" % d][:],
                                                           start=True, stop=True),
                                  reads=["w2s", "wl%d" % d], writes=[kw])
                            pa, ka = next_ps()
                            kb.op("pe", lambda e: e.matmul(pa[:, :], lhsT=a2s[:, d, cs], rhs=Sb["al%d" % d][:],
                                                           start=True, stop=True),
                                  reads=["a2s", "al%d" % d], writes=[ka])
                            V("act", lambda e: e.activation(out=S["lw"][:], in_=pw[:, :], func=AF.Sigmoid,
                                                            bias=pcol("w0", d * NB + cb), scale=1.0),
                              [kw, "pvs"], ["lw"])
                            V("act", lambda e: e.activation(out=S["ag"][:], in_=pa[:, :], func=AF.Sigmoid,
                                                            bias=pcol("a0", d * NB + cb), scale=1.0),
                              [ka, "pvs"], ["ag"])
                            V("dve", lambda e: e.tensor_scalar(out=S["lw"][:], in0=S["lw"][:],
                                                               scalar1=-math.exp(-0.5), scalar2=None, op0=ALU.mult),
                              ["lw"], ["lw"])
                            V("dve", lambda e: e.tensor_tensor_scan(out=S["cum"][:], data0=rst[:], data1=S["lw"][:],
                                                                    initial=0.0, op0=ALU.mult, op1=ALU.add),
                              ["lw", "rst"], ["cum"])
                            V("dve", lambda e: e.tensor_tensor(out=S["exc"][:], in0=S["cum"][:], in1=S["lw"][:],
                                                               op=ALU.subtract), ["cum", "lw"], ["exc"])
                            cum3 = S["cum"][:].rearrange("p (a b) -> p a b", b=64)
                            to = d * 16
                            V("dve", lambda e: e.tensor_copy(out=tot[:, to:to + nchk], in_=cum3[:, :, 63]),
                              ["cum"], ["tot"])
                            V("dve", lambda e: e.tensor_scalar(out=tot[:, to + 8:to + 8 + nchk], in0=tot[:, to:to + nchk],
                                                               scalar1=-1.0, scalar2=None, op0=ALU.mult),
                              ["tot"], ["tot"])
                            V("act", lambda e: e.activation(out=S["tmp"][:, 0:nchk], in_=tot[:, to:to + nchk],
                                                            func=AF.Exp), ["tot"], ["tmp"])
                            kb.dma("sp", sG[d][cs, tg * nchk:(tg + 1) * nchk], S["tmp"][:, 0:nchk], reads=["tmp"])
                            V("dve", lambda e: e.tensor_scalar(out=S["kd"][:], in0=S["ag"][:], scalar1=-1.0,
                                                               scalar2=pcol("ka", cb), op0=ALU.add, op1=ALU.mult),
                              ["ag", "pvs"], ["kd"])
                            V("dve", lambda e: e.scalar_tensor_tensor(out=S["kd"][:], in0=S["kd"][:], scalar=1.0,
                                                                      in1=S["k"][:], op0=ALU.add, op1=ALU.mult),
                              ["kd", "k"], ["kd"])
                            V("dve", lambda e: e.tensor_tensor(out=S["bb"][:], in0=S["ag"][:], in1=S["kk"][:],
                                                               op=ALU.mult), ["ag", "kk"], ["bb"])
                            V("dve", lambda e: e.scalar_tensor_tensor(out=Sb["rkb"][:], in0=S["r"][:],
                                                                      scalar=pcol("rk", cb), in1=S["kd"][:],
                                                                      op0=ALU.mult, op1=ALU.mult),
                              ["r", "kd", "pvs"], ["rkb"])
                            kb.op("pe", lambda e: e.matmul(pbn[:, :], lhsT=blk, rhs=Sb["rkb"][:], start=(d == 0),
                                                           stop=(d == 1)),
                                  reads=["rkb", "mats"], writes=[kbn], signal=True)
                            def expo(srcn, sgn, sb):
                                if sb == 0:
                                    V("act", lambda e: e.activation(out=S["ee"][:], in_=S[srcn][:], func=AF.Exp,
                                                                    scale=float(sgn)), [srcn], ["ee"])
                                else:
                                    base = to if sb > 0 else to + 8
                                    for ch in range(nchk):
                                        V("act", lambda e: e.activation(out=S["ee"][:, ch * 64:(ch + 1) * 64],
                                                                        in_=S[srcn][:, ch * 64:(ch + 1) * 64],
                                                                        func=AF.Identity,
                                                                        bias=tot[:, base + ch:base + ch + 1],
                                                                        scale=float(sgn)), [srcn, "tot"], ["ee"])
                                    V("act", lambda e: e.activation(out=S["ee"][:], in_=S["ee"][:], func=AF.Exp),
                                      ["ee"], ["ee"])

                            def prod(onm, anm, neg=False):
                                if neg:
                                    V("dve", lambda e: e.scalar_tensor_tensor(out=Sb[onm][:], in0=S[anm][:],
                                                                              scalar=-1.0, in1=S["ee"][:],
                                                                              op0=ALU.mult, op1=ALU.mult),
                                      [anm, "ee"], [onm])
                                else:
                                    V("dve", lambda e: e.tensor_tensor(out=Sb[onm][:], in0=S[anm][:], in1=S["ee"][:],
                                                                       op=ALU.mult), [anm, "ee"], [onm])

                            if d == 0:
                                specs = [("cum", 1, 0, [("oR", "r", False)]), ("exc", 1, 0, [("oA", "kk", True)]),
                                         ("cum", -1, 0, [("oB", "bb", False), ("oK", "kd", False)]),
                                         ("cum", -1, 1, [("oBh", "bb", False), ("oKh", "kd", False)])]
                            else:
                                specs = [("exc", -1, 1, [("oR", "r", False)]), ("cum", -1, 1, [("oA", "kk", True)]),
                                         ("exc", 1, -1, [("oB", "bb", False), ("oK", "kd", False)]),
                                         ("exc", 1, 0, [("oBh", "bb", False), ("oKh", "kd", False)])]
                            for (srcn, sgn, sb, outs) in specs:
                                expo(srcn, sgn, sb)
                                for (onm, anm, neg) in outs:
                                    prod(onm, anm, neg)
                            for onm, dten in (("oA", sA[d]), ("oB", sB[d]), ("oK", sK[d]), ("oR", sR[d])):
                                kb.dma("sp", dten[cs, ts], Sb[onm][:], reads=[onm])
                            for onm, dten in (("oBh", sBh[d]), ("oKh", sKh[d])) + ((("vb", sV),) if d == 0 else ()):
                                pt, pk = next_psb()
                                for j in range(4):
                                    kb.op("pe", lambda e: e.transpose(pt[:, j * 128:(j + 1) * 128],
                                                                      Sb[onm][:, j * 128:(j + 1) * 128], ident),
                                          reads=[onm, "mats"], writes=[pk], signal=(j == 3))
                                ob_i = ps_state.setdefault("otm", 0) % 3
                                ps_state["otm"] += 1
                                V("act", lambda e: e.activation(out=otm[ob_i][:].rearrange("p a b -> p (a b)"),
                                                                in_=pt[:, 0:512], func=AF.Copy),
                                  [pk], [("otm", ob_i)])
                                kb.dma("sp", dten.rearrange("(a p) f -> p a f", p=128)[:, tg * 4:(tg + 1) * 4, cs],
                                       otm[ob_i][:], reads=[("otm", ob_i)])
                        V("dve", lambda e: e.tensor_tensor(out=S["bon"][:], in0=pbn[:, :], in1=S["v"][:],
                                                           op=ALU.mult), [kbn, "v"], ["bon"])
                        V("dve", lambda e: e.tensor_scalar(out=S["bon"][:], in0=S["bon"][:], scalar1=pcol("lb", cb),
                                                           scalar2=None, op0=ALU.add), ["bon", "pvs"], ["bon"])
                        kb.dma("sp", bonus[cs, ts], S["bon"][:], reads=["bon"])
                kb.barrier()

        def phase_scan():
            with ExitStack() as es:
                carry = es.enter_context(SBT("sc_carry", [64, 2 * c.NCH], F32))
                kb.dma("sp", carry[:], c_carry, writes=["carry"])
                NHG = c.RH // 8
                chains = [(d, hg) for hg in range(NHG) for d in range(2)]
                CPR = 3
                for r0 in range(0, len(chains), CPR):
                    with ExitStack() as es2:
                        act_ch = chains[r0:r0 + CPR]
                        st = {}
                        for ci, (d, hg) in enumerate(act_ch):
                            A = lambda nm, shp, dt: es2.enter_context(
                                SBT("sc%d_%s" % (ci, nm), shp, dt))
                            s_ = {"fm": [A("fm%d" % b, [64, 4, 8, 128], BF16) for b in range(2)],
                                  "tm": [A("tm%d" % b, [64, 3, 512], BF16) for b in range(2)],
                                  "H32": A("H32", [64, 512], F32), "Hb": A("Hb", [64, 512], BF16),
                                  "gt": A("gt", [64, 8, c.NCH], F32),
                                  "yb": [A("yb%d" % b, [64, 8, 128], F32) for b in range(2)]}
                            for nm in ("Nn", "NT", "AK", "RB", "RK", "X", "U"):
                                s_[nm] = A(nm, [64, 512], BF16)
                            for nm in ("Q", "QT", "P"):
                                s_[nm] = [A("%s%d" % (nm, b), [64, 512], BF16) for b in range(2)]
                            st[ci] = s_
                            kb.op("dve", lambda e: e.memset(s_["H32"][:], 0.0), [], [("H32", ci)])
                            kb.op("dve", lambda e: e.memset(s_["Hb"][:], 0.0), [], [("Hb", ci)])
                            kb.dma("sp", s_["gt"][:],
                                   sG[d].rearrange("(h c) n -> c h n", c=64)[:, hg * 8:(hg + 1) * 8, :],
                                   writes=[("gt", ci)])

                        def unit(ci, d, hg, step):
                            s_ = st[ci]
                            ch = step if d == 0 else c.NCH - 1 - step
                            pair = step // 2
                            pb = pair % 2
                            sub = (ch % 2)
                            K = lambda nm: (nm, ci)
                            if step % 2 == 0:
                                c2 = (ch // 2) * 2
                                tsl = slice(c2 * 64, c2 * 64 + 128)
                                for i, ten in enumerate((sA[d], sB[d], sK[d], sR[d])):
                                    kb.dma("sp", s_["fm"][pb][:, i, :, :],
                                           ten.rearrange("(h c) n -> c h n", c=64)[:, hg * 8:(hg + 1) * 8, tsl],
                                           writes=[("fm", ci, pb)])
                            tmb = step % 2
                            for i, ten in enumerate((sV, sBh[d], sKh[d])):
                                kb.dma("sp", s_["tm"][tmb][:, i, :], ten[ch * 64:(ch + 1) * 64, hg * 512:(hg + 1) * 512],
                                       writes=[("tm", ci, tmb)])
                            fm = s_["fm"][pb]
                            tsub = slice(sub * 64, (sub + 1) * 64)
                            At = lambda h: fm[:, 0, h, tsub]
                            Bt = lambda h: fm[:, 1, h, tsub]
                            Kt = lambda h: fm[:, 2, h, tsub]
                            Rt = lambda h: fm[:, 3, h, tsub]
                            tm = s_["tm"][tmb]
                            Vh = lambda h: tm[:, 0, h * 64:(h + 1) * 64]
                            Bh = lambda h: tm[:, 1, h * 64:(h + 1) * 64]
                            Kh = lambda h: tm[:, 2, h * 64:(h + 1) * 64]
                            hs = lambda t, h: t[0:64, h * 64:(h + 1) * 64]
                            FMK = ("fm", ci, pb)
                            TMK = ("tm", ci, tmb)
                            ms, mi = (0, 1) if d == 0 else (2, 3)
                            mt = 2 if d == 0 else 0

                            def mm8(lh, rh, reads, masks, outs):
                                pt, pk = next_ps()
                                for h in range(8):
                                    kb.op("pe", lambda e: e.matmul(hs(pt, h), lhsT=lh(h), rhs=rh(h), start=True,
                                                                   stop=True),
                                          reads=reads, writes=[pk], signal=(h == 7))
                                return pt, pk

                            def evm(pt, pk, mask, dst, dk, eng="dve"):
                                kb.op(eng, lambda e: e.tensor_tensor(
                                    out=dst[:], in0=pt[0:64, :], in1=tri[:, mask, :, :].rearrange("p a b -> p (a b)"),
                                    op=ALU.mult), reads=[pk, "tri"], writes=[dk])

                            def evc(pt, pk, dst, dk, eng):
                                if eng == "act":
                                    kb.op("act", lambda e: e.activation(out=dst[:], in_=pt[0:64, :], func=AF.Copy),
                                          reads=[pk], writes=[dk])
                                else:
                                    kb.op("dve", lambda e: e.tensor_copy(out=dst[:], in_=pt[0:64, :]),
                                          reads=[pk], writes=[dk])
                            pt, pk = mm8(Bt, At, [FMK], None, None)
                            evm(pt, pk, ms, s_["Nn"], K("Nn"))
                            pt, pk = mm8(At, Bt, [FMK], None, None)
                            evm(pt, pk, mt, s_["NT"], K("NT"))
                            yield
                            pt, pk = mm8(Kt, At, [FMK], None, None)
                            evm(pt, pk, ms, s_["AK"], K("AK"))
                            pt, pk = mm8(Bt, Rt, [FMK], None, None)
                            evm(pt, pk, mi, s_["RB"], K("RB"))
                            pt, pk = mm8(Kt, Rt, [FMK], None, None)
                            evm(pt, pk, mi, s_["RK"], K("RK"))
                            yield
                            kb.op("dve", lambda e: e.tensor_tensor(
                                out=s_["P"][0][:], in0=s_["Nn"][:],
                                in1=tri[:, 4, :, :].rearrange("p a b -> p (a b)"), op=ALU.add),
                                reads=[K("Nn"), "tri"], writes=[K("P0")])
                            Qc, QTc, Pc = s_["Nn"], s_["NT"], s_["P"][0]
                            Qk, QTk, Pk_ = K("Nn"), K("NT"), K("P0")
                            nlev = 5
                            for j in range(1, nlev + 1):
                                jb = j % 2
                                last = (j == nlev)
                                if not last:
                                    pt, pk = mm8(lambda h: hs(QTc, h), lambda h: hs(Qc, h), [Qk, QTk], None, None)
                                    Qn, Qnk = s_["Q"][jb], K("Q%d" % jb)
                                    evc(pt, pk, Qn, Qnk, "act")
                                pt, pk = mm8(lambda h: hs(Qc, h), lambda h: hs(QTc, h), [Qk, QTk], None, None)
                                QTn, QTnk = s_["QT"][jb], K("QT%d" % jb)
                                evc(pt, pk, QTn, QTnk, "dve")
                                yield
                                pt, pk = next_ps()
                                for h in range(8):
                                    kb.op("pe", lambda e: e.matmul(hs(pt, h), lhsT=hs(QTn, h), rhs=hs(Pc, h),
                                                                   start=True, stop=False),
                                          reads=[QTnk, Pk_], writes=[pk], signal=False)
                                    kb.op("pe", lambda e: e.matmul(hs(pt, h), lhsT=ident[0:64, 0:64], rhs=hs(Pc, h),
                                                                   start=False, stop=True),
                                          reads=["mats", Pk_], writes=[pk], signal=(h == 7))
                                Pn, Pnk = s_["P"][jb], K("P%d" % jb)
                                evc(pt, pk, Pn, Pnk, "act")
                                Pc, Pk_ = Pn, Pnk
                                if not last:
                                    Qc, Qk = Qn, Qnk
                                QTc, QTk = QTn, QTnk
                                yield
                            Hb = s_["Hb"]
                            pt, pk = next_ps()
                            for h in range(8):
                                kb.op("pe", lambda e: e.matmul(hs(pt, h), lhsT=At(h), rhs=hs(Hb, h), start=True,
                                                               stop=False),
                                      reads=[FMK, K("Hb")], writes=[pk], signal=False)
                                kb.op("pe", lambda e: e.matmul(hs(pt, h), lhsT=hs(s_["AK"], h), rhs=Vh(h),
                                                               start=False, stop=True),
                                      reads=[K("AK"), TMK], writes=[pk], signal=(h == 7))
                            evc(pt, pk, s_["X"], K("X"), "dve")
                            yield
                            pt, pk = mm8(lambda h: hs(Pc, h), lambda h: hs(s_["X"], h), [Pk_, K("X")], None, None)
                            evc(pt, pk, s_["U"], K("U"), "act")
                            yield
                            pt, pk = next_ps()
                            for h in range(8):
                                kb.op("pe", lambda e: e.matmul(hs(pt, h), lhsT=hs(Hb, h), rhs=Rt(h), start=True,
                                                               stop=False),
                                      reads=[FMK, K("Hb")], writes=[pk], signal=False)
                                kb.op("pe", lambda e: e.matmul(hs(pt, h), lhsT=hs(s_["U"], h), rhs=hs(s_["RB"], h),
                                                               start=False, stop=False),
                                      reads=[K("U"), K("RB")], writes=[pk], signal=False)
                                kb.op("pe", lambda e: e.matmul(hs(pt, h), lhsT=Vh(h), rhs=hs(s_["RK"], h),
                                                               start=False, stop=True),
                                      reads=[TMK, K("RK")], writes=[pk], signal=(h == 7))
                            yb = s_["yb"][pb]
                            kb.op("dve", lambda e: e.tensor_copy(
                                out=yb[:, :, tsub], in_=pt[0:64, :].rearrange("p (a b) -> p a b", b=64)),
                                reads=[pk], writes=[("yb", ci, pb)])
                            if step % 2 == 1:
                                c2 = (ch // 2) * 2
                                kb.dma("sp", yT[d].rearrange("(h c) n -> c h n", c=64)[:, hg * 8:(hg + 1) * 8,
                                                                                      c2 * 64:c2 * 64 + 128],
                                       yb[:], reads=[("yb", ci, pb)])
                            pt, pk = next_ps()
                            for h in range(8):
                                kb.op("pe", lambda e: e.matmul(hs(pt, h), lhsT=Bh(h), rhs=hs(s_["U"], h), start=True,
                                                               stop=False),
                                      reads=[TMK, K("U")], writes=[pk], signal=False)
                                kb.op("pe", lambda e: e.matmul(hs(pt, h), lhsT=Kh(h), rhs=Vh(h), start=False,
                                                               stop=True),
                                      reads=[TMK], writes=[pk], signal=(h == 7))
                            H32 = s_["H32"]
                            for h in range(8):
                                kb.op("dve", lambda e: e.scalar_tensor_tensor(
                                    out=hs(H32, h), in0=hs(H32, h), scalar=s_["gt"][:, h, ch:ch + 1],
                                    in1=hs(pt, h), op0=ALU.mult, op1=ALU.add),
                                    reads=[pk, K("H32"), K("gt")], writes=[K("H32")])
                            ccol = carry[:, d * c.NCH + ch:d * c.NCH + ch + 1]
                            kb.op("dve", lambda e: e.tensor_scalar(out=H32[:], in0=H32[:], scalar1=ccol, scalar2=None,
                                                                  op0=ALU.mult),
                                  reads=[K("H32"), "carry"], writes=[K("H32")])
                            kb.op("act", lambda e: e.activation(out=Hb[:], in_=H32[:], func=AF.Copy),
                                  reads=[K("H32")], writes=[K("Hb")])
                            yield

                        for step in range(c.NCH):
                            gens = [unit(ci, d, hg, step) for ci, (d, hg) in enumerate(act_ch)]
                            while gens:
                                for g in list(gens):
                                    try:
                                        next(g)
                                    except StopIteration:
                                        gens.remove(g)
                        kb.barrier()

        def phase_rwkvpost():
            with ExitStack() as es:
                nb = 2
                T = lambda nm, dt: [es.enter_context(SBT("po_%s%d" % (nm, i), [128, 512], dt))
                                    for i in range(nb)]
                y0, y1, bn, gg, dd, t1 = T("y0", F32), T("y1", F32), T("bn", F32), T("gg", F32), T("dd", F32), T("t1", F32)
                yb16, sq16, ob = T("yb", BF16), T("sq", BF16), T("ob", BF16)
                it = 0
                for cb in range(c.NB):
                    cs = slice(cb * 128, (cb + 1) * 128)
                    for tg in range(NT):
                        b = it % nb
                        it += 1
                        ts = slice(tg * 512, (tg + 1) * 512)
                        kb.dma("sp", y0[b][:], yT[0][cs, ts], writes=[("y0", b)])
                        kb.dma("sp", y1[b][:], yT[1][cs, ts], writes=[("y1", b)])
                        kb.dma("sp", bn[b][:], bonus[cs, ts], writes=[("bn", b)])
                        kb.dma("sp", gg[b][:], gout[cs, ts], writes=[("gg", b)])
                        kb.op("dve", lambda e: e.tensor_tensor(out=y0[b][:], in0=y0[b][:], in1=y1[b][:], op=ALU.add),
                              reads=[("y0", b), ("y1", b)], writes=[("y0", b)])
                        kb.op("act", lambda e: e.activation(out=yb16[b][:], in_=y0[b][:], func=AF.Copy),
                              reads=[("y0", b)], writes=[("yb", b)])
                        p1, k1 = next_ps()
                        kb.op("pe", lambda e: e.matmul(p1[:, :], lhsT=blk, rhs=yb16[b][:], start=True, stop=True),
                              reads=[("yb", b), "mats"], writes=[k1])
                        kb.op("dve", lambda e: e.scalar_tensor_tensor(out=dd[b][:], in0=p1[:, :], scalar=-1.0 / 64,
                                                                     in1=y0[b][:], op0=ALU.mult, op1=ALU.add),
                              reads=[k1, ("y0", b)], writes=[("dd", b)])
                        kb.op("act", lambda e: e.activation(out=sq16[b][:], in_=dd[b][:], func=AF.Square),
                              reads=[("dd", b)], writes=[("sq", b)])
                        p2, k2 = next_ps()
                        kb.op("pe", lambda e: e.matmul(p2[:, :], lhsT=blk, rhs=sq16[b][:], start=True, stop=True),
                              reads=[("sq", b), "mats"], writes=[k2])
                        rsqrt_to(t1[b][:], ("t1", b), p2[:, :], k2, 1.0 / 64, 64e-5)
                        kb.op("dve", lambda e: e.tensor_scalar(out=t1[b][:], in0=t1[b][:], scalar1=pcol("lw", cb),
                                                              scalar2=None, op0=ALU.mult),
                              reads=[("t1", b), "pvs"], writes=[("t1", b)])
                        kb.op("dve", lambda e: e.tensor_tensor(out=dd[b][:], in0=dd[b][:], in1=t1[b][:], op=ALU.mult),
                              reads=[("dd", b), ("t1", b)], writes=[("dd", b)])
                        kb.op("dve", lambda e: e.tensor_tensor(out=dd[b][:], in0=dd[b][:], in1=bn[b][:], op=ALU.add),
                              reads=[("dd", b), ("bn", b)], writes=[("dd", b)])
                        kb.op("dve", lambda e: e.tensor_tensor(out=ob[b][:], in0=dd[b][:], in1=gg[b][:], op=ALU.mult),
                              reads=[("dd", b), ("gg", b)], writes=[("ob", b)])
                        kb.dma("sp", rwT[cs, ts], ob[b][:], reads=[("ob", b)])
                kb.barrier()

        def phase_merge(l):
            with ExitStack() as es:
                ga_b = [es.enter_context(SBT("m_ga%d" % i, [128, 512], BF16)) for i in range(3)]
                gr_b = [es.enter_context(SBT("m_gr%d" % i, [128, 512], BF16)) for i in range(3)]
                t1 = [es.enter_context(SBT("m_t%d" % i, [128, 512], F32)) for i in range(3)]
                ob = [es.enter_context(SBT("m_o%d" % i, [128, 512], BF16)) for i in range(3)]
                cnt = [0]

                def epi(psl, pkl, tag, offabs, w, tok0):
                    b = cnt[0] % 3
                    cnt[0] += 1
                    ts = slice(tok0, tok0 + 512)
                    kb.dma("sp", ga_b[b][:], gA[offabs:offabs + 128, ts], writes=[("mga", b)])
                    kb.dma("sp", gr_b[b][:], gR[offabs:offabs + 128, ts], writes=[("mgr", b)])
                    kb.op("dve", lambda e: e.tensor_tensor(out=t1[b][:], in0=psl[0], in1=ga_b[b][:], op=ALU.mult),
                          reads=[pkl[0], ("mga", b)], writes=[("mt", b)])
                    kb.op("dve", lambda e: e.tensor_tensor(out=ob[b][:], in0=psl[1], in1=gr_b[b][:], op=ALU.mult),
                          reads=[pkl[1], ("mgr", b)], writes=[("mo", b)])
                    kb.op("dve", lambda e: e.tensor_tensor(out=ob[b][:], in0=ob[b][:], in1=t1[b][:], op=ALU.add),
                          reads=[("mo", b), ("mt", b)], writes=[("mo", b)])
                    kb.dma("sp", mixT[offabs:offabs + 128, ts], ob[b][:], reads=[("mo", b)])

                groups = [(g * 512, 512, [(j * 128, 128, None) for j in range(4)]) for g in range(D // 512)]
                gemm_fm([(attT, w_ua[l], c.AW), (rwT, w_ur[l], c.RW)], groups, min(N, 1024), epi)

        def phase_ffnup(l):
            with ExitStack() as es:
                ev16 = Evac(es, "fu_b", BF16)
                cnt = [0]

                def epi(psl, pkl, tag, offabs, w, tok0):
                    buf, bk = ev16.get()
                    cnt[0] += 1
                    if cnt[0] % 2 == 0:
                        kb.op("dve", lambda e: e.tensor_copy(out=buf[0:w, :], in_=psl[0]), reads=[pkl[0]], writes=[bk])
                    else:
                        kb.op("act", lambda e: e.activation(out=buf[0:w, :], in_=psl[0], func=AF.Copy),
                              reads=[pkl[0]], writes=[bk])
                    kb.dma("sp", uT[offabs:offabs + w, tok0:tok0 + 512], buf[0:w, :], reads=[bk])

                M = 2 * c.DFF
                groups = []
                o = 0
                while o < M:
                    gw = min(512, M - o)
                    groups.append((o, gw, [(j * 128, 128, None) for j in range(gw // 128)]))
                    o += gw
                gemm_fm([(hT, w_fu[l], D)], groups, min(N, 1024), epi)

        def phase_ffnact():
            with ExitStack() as es:
                nb = 2
                uv = [es.enter_context(SBT("fa_uv%d" % i, [128, N + 2], BF16)) for i in range(nb)]
                ug = [es.enter_context(SBT("fa_ug%d" % i, [128, N + 2], BF16)) for i in range(nb)]
                cv = [es.enter_context(SBT("fa_cv%d" % i, [128, N], F32)) for i in range(nb)]
                cg = [es.enter_context(SBT("fa_cg%d" % i, [128, N], F32)) for i in range(nb)]
                ob = [es.enter_context(SBT("fa_o%d" % i, [128, N], BF16)) for i in range(nb)]
                tmp = es.enter_context(SBT("fa_tmp", [128, 4], F32))
                for i in range(nb):
                    for t in (uv[i], ug[i]):
                        kb.op("dve", lambda e: e.memset(t[:, 0:1], 0.0), [], [("fau", i)])
                        kb.op("dve", lambda e: e.memset(t[:, N + 1:N + 2], 0.0), [], [("fau", i)])
                FBh = c.FB // 2
                for j in range(FBh):
                    b = j % nb
                    kb.dma("sp", uv[b][:, 1:N + 1], uT[j * 128:(j + 1) * 128, :], writes=[("fau", b)])
                    kb.dma("sp", ug[b][:, 1:N + 1], uT[(FBh + j) * 128:(FBh + j + 1) * 128, :], writes=[("fau", b)])
                    for (src, dst, fbk, dk) in ((uv[b], cv[b], j, ("cv", b)), (ug[b], cg[b], FBh + j, ("cg", b))):
                        fc = lambda tap: pcol("fc", tap * c.FB + fbk)
                        kb.op("dve", lambda e: e.tensor_scalar(out=dst[:], in0=src[:, 1:N + 1], scalar1=fc(1),
                                                              scalar2=pcol("fb", fbk), op0=ALU.mult, op1=ALU.add),
                              reads=[("fau", b), "pvs"], writes=[dk])
                        kb.op("dve", lambda e: e.scalar_tensor_tensor(out=dst[:], in0=src[:, 0:N], scalar=fc(0),
                                                                     in1=dst[:], op0=ALU.mult, op1=ALU.add),
                              reads=[("fau", b), dk, "pvs"], writes=[dk])
                        kb.op("dve", lambda e: e.scalar_tensor_tensor(out=dst[:], in0=src[:, 2:N + 2], scalar=fc(2),
                                                                     in1=dst[:], op0=ALU.mult, op1=ALU.add),
                              reads=[("fau", b), dk, "pvs"], writes=[dk])
                        kb.op("dve", lambda e: e.tensor_scalar(out=tmp[:, 0:1], in0=src[:, HALF:HALF + 1], scalar1=fc(0),
                                                              scalar2=edge[:, 0:1], op0=ALU.mult, op1=ALU.mult),
                              reads=[("fau", b), "pvs", "edge"], writes=["fatmp"])
                        kb.op("dve", lambda e: e.tensor_tensor(out=dst[:, HALF:HALF + 1], in0=dst[:, HALF:HALF + 1],
                                                              in1=tmp[:, 0:1], op=ALU.add),
                              reads=["fatmp", dk], writes=[dk])
                        kb.op("dve", lambda e: e.tensor_scalar(out=tmp[:, 1:2], in0=src[:, HALF + 1:HALF + 2],
                                                              scalar1=fc(2), scalar2=edge[:, 0:1], op0=ALU.mult,
                                                              op1=ALU.mult),
                              reads=[("fau", b), "pvs", "edge"], writes=["fatmp"])
                        kb.op("dve", lambda e: e.tensor_tensor(out=dst[:, HALF - 1:HALF], in0=dst[:, HALF - 1:HALF],
                                                              in1=tmp[:, 1:2], op=ALU.add),
                              reads=["fatmp", dk], writes=[dk])
                    kb.op("act", lambda e: e.activation(out=cg[b][:], in_=cg[b][:], func=AF.Silu),
                          reads=[("cg", b)], writes=[("cg", b)])
                    kb.op("dve", lambda e: e.tensor_tensor(out=ob[b][:], in0=cg[b][:], in1=cv[b][:], op=ALU.mult),
                          reads=[("cg", b), ("cv", b)], writes=[("fao", b)])
                    kb.dma("sp", actT[j * 128:(j + 1) * 128, :], ob[b][:], reads=[("fao", b)])
                kb.barrier()

        xcur = x_in
        for l in range(L):
            kb.dma("sp", pvs[:], pv[l], writes=["pvs"])
            kb.barrier()
            phase_norm(xcur, "g1")
            phase_inproj(l)
            phase_qkprep()
            phase_attn()
            phase_rwkvprep(l)
            phase_scan()
            phase_rwkvpost()
            phase_merge(l)
            gemm_tm(mixT, w_o[l], D, xcur, x1)
            phase_norm(x1, "g2")
            phase_ffnup(l)
            phase_ffnact()
            xnext = y_out if l == L - 1 else x2
            gemm_tm(actT, w_fd[l], c.DFF, x1, xnext)
            xcur = xnext
        kb.barrier()
    return nc


def _cols(v):
    v = np.asarray(v, np.float32).reshape(-1, 128)
    return np.ascontiguousarray(v.T)


def host_consts(cfg, packed):
    c = cfg
    N = c.N
    HALF = N // 2
    NKT = N // 128
    attb = np.zeros((128, NKT, 2), np.float32)
    if packed:
        for kt in range(NKT):
            kh = 0 if kt * 128 < HALF else 1
            attb[:, kt, 1 - kh] = -30000.0
    carry = np.ones((64, 2, c.NCH), np.float32)
    if packed:
        carry[:, 0, c.NCH // 2 - 1] = 0.0
        carry[:, 1, c.NCH // 2] = 0.0
    edge = np.full((128, 1), -1.0 if packed else 0.0, np.float32)
    T = HALF if packed else N
    pos = np.arange(N) % T
    row = (pos // 64).astype(np.float32)
    col = (pos % 64).astype(np.float32)
    inv = (10000.0 ** (-np.arange(0, 64, 2, dtype=np.float32) / 64)).astype(np.float32)
    rope = np.zeros((128, 2, N), np.float32)
    for p in range(128):
        ax, j = p // 64, p % 64
        half, f = j // 32, j % 32
        ang = (row if ax == 0 else col) * inv[f]
        rope[p, 0] = np.cos(ang.astype(np.float32))
        rope[p, 1] = (-np.sin(ang.astype(np.float32))) if half == 0 else np.sin(ang.astype(np.float32))
    mats = np.zeros((128, 4, 128), np.float32)
    mats[:, 0, :] = np.eye(128)
    mats[:, 1, :] = 1.0
    mats[0:64, 2, 0:64] = 1.0
    mats[64:128, 2, 64:128] = 1.0
    for p in range(128):
        j = p % 64
        partner = p + 32 if (j // 32) == 0 else p - 32
        mats[partner, 3, p] = 1.0
    s = np.arange(64)[:, None]
    t = np.arange(64)[None, :]
    tri = np.zeros((64, 5, 8, 64), np.float32)
    for i, m in enumerate((s < t, s <= t, s > t, s >= t, s == t)):
        tri[:, i, :, :] = m.astype(np.float32)[:, None, :]
    reset = np.ones((128, 512), np.float32)
    reset[:, 0::64] = 0.0
    return {"c_attb": attb.reshape(128, NKT * 2), "c_carry": carry.reshape(64, 2 * c.NCH), "c_edge": edge,
            "c_rope": rope, "c_mats": mats.astype(NP_BF16), "c_tri": tri.astype(NP_BF16), "c_reset": reset}


def host_pv(cfg, inp):
    c = cfg
    pv = np.zeros((c.L, 128, c.PC), np.float32)
    NB, FB = c.NB, c.FB
    for l in range(c.L):
        def put(nm, arr):
            pv[l, :, c.po[nm]:c.po[nm] + arr.shape[1]] = arr
        put("g1", _cols(inp["norm_mix"][l]))
        put("g2", _cols(inp["norm_ffn"][l]))
        put("qg", _cols(inp["q_gain"][l]))
        put("kg", _cols(inp["k_gain"][l]))
        cv = np.asarray(inp["rwkv_conv"][l], np.float32)
        put("conv", np.concatenate([_cols(cv[tap, i * c.RW:(i + 1) * c.RW]) for tap in range(3) for i in range(3)], 1))
        put("w0", np.concatenate([_cols(inp["decay_w0"][l][d]) for d in range(2)], 1))
        put("a0", np.concatenate([_cols(inp["iclr_a0"][l][d]) for d in range(2)], 1))
        put("kk", _cols(inp["k_k"][l]))
        put("ka", _cols(inp["k_a"][l]))
        put("rk", _cols(np.asarray(inp["r_k"][l]).reshape(-1)))
        put("lw", _cols(inp["lnx_w"][l]))
        put("lb", _cols(inp["lnx_b"][l]))
        fc = np.asarray(inp["ffn_conv"][l], np.float32)
        put("fc", np.concatenate([_cols(fc[tap]) for tap in range(3)], 1))
        put("fb", _cols(inp["ffn_conv_b"][l]))
    return pv


_NC_CACHE = {}


def run(cfg, inputs, debug_outs=()):
    c = cfg
    key = (c.D, c.SEQ, c.L, tuple(debug_outs))
    if key not in _NC_CACHE:
        _NC_CACHE[key] = build(c, debug_outs)
    nc = _NC_CACHE[key]
    xp = np.asarray(inputs["x_prompt"], np.float32)
    xs = np.asarray(inputs["x_sample"], np.float32)
    D = c.D
    groups = {}
    npk = c.BATCH // 2
    act_cores = [0, 1, 4, 5, 2, 3, 6, 7]
    gi = 0
    for g in range(npk):
        groups[act_cores[gi]] = ("p", g); gi += 1
    for g in range(c.DEC_BATCH):
        groups[act_cores[gi]] = ("s", g); gi += 1
    pvh = host_pv(c, inputs)
    wkeys = {"w_in": "w_in", "w_up_attn": "w_up_attn", "w_up_rwkv": "w_up_rwkv", "w_o": "w_o",
             "w_ffn_up": "w_ffn_up", "w_ffn_down": "w_ffn_down", "decay_w2": "decay_w2", "iclr_a2": "iclr_a2",
             "gate_g2": "gate_g2"}
    shared = {k: np.asarray(inputs[v], np.float32) for k, v in wkeys.items()}
    shared["pv"] = pvh
    cp = host_consts(c, True)
    cs = host_consts(c, False)
    in_maps = []
    zx = np.zeros((c.N, D), np.float32)
    for core in range(8):
        m = dict(shared)
        gk = groups.get(core)
        if gk is None:
            m["x"] = zx
            m.update(cs)
        elif gk[0] == "p":
            m["x"] = np.ascontiguousarray(xp[2 * gk[1]:2 * gk[1] + 2].reshape(c.N, D))
            m.update(cp)
        else:
            m["x"] = np.ascontiguousarray(xs[gk[1]])
            m.update(cs)
        in_maps.append(m)
    res = run_bass_kernel_spmd(nc, in_maps, core_ids=list(range(8)))
    yp = np.zeros((c.BATCH, c.SEQ, D), np.float32)
    ys = np.zeros((c.DEC_BATCH, 2 * c.SEQ, D), np.float32)
    for core, gk in groups.items():
        y = np.asarray(res.results[core]["y"])
        if gk[0] == "p":
            yp[2 * gk[1]:2 * gk[1] + 2] = y.reshape(2, c.SEQ, D)
        else:
            ys[gk[1]] = y
    return (yp, ys), res


def kernel(**inputs):
    cfg = Cfg()
    (yp, ys), _ = run(cfg, inputs)
    return (yp, ys)
```

```python
import math
from contextlib import ExitStack
import numpy as np
import ml_dtypes
import concourse.bass as bass
import concourse.mybir as mybir
from concourse.bass_utils import run_bass_kernel_spmd

F32 = mybir.dt.float32
BF16 = mybir.dt.bfloat16
AF = mybir.ActivationFunctionType
ALU = mybir.AluOpType
NP_BF16 = ml_dtypes.bfloat16


class Cfg:
    def __init__(s, D=4096, SEQ=2048, BATCH=4, DEC_BATCH=2, L=2):
        s.D = D; s.SEQ = SEQ; s.BATCH = BATCH; s.DEC_BATCH = DEC_BATCH; s.L = L
        s.N = 2 * SEQ
        s.AW = D // 2; s.QH = s.AW // 128; s.KVH = s.QH // 4; s.KVW = s.KVH * 128
        s.RW = D // 2; s.RH = s.RW // 64
        s.DL = 128; s.IL = 128; s.GL = 480
        s.DFF = 256 * (-(-(8 * D) // (3 * 256)))
        s.SPL = (s.AW, s.KVW, s.KVW, 3 * s.RW, 2 * s.DL, 2 * s.IL, s.GL, D, D)
        s.INW = sum(s.SPL)
        s.C = 64; s.NCH = s.N // 64
        s.NB = s.RW // 128
        s.FB = 2 * s.DFF // 128
        o = 0
        s.po = {}
        for nm, w in (("g1", D // 128), ("g2", D // 128), ("qg", 1), ("kg", 1), ("conv", 3 * 3 * s.NB),
                      ("w0", 2 * s.NB), ("a0", 2 * s.NB), ("kk", s.NB), ("ka", s.NB), ("rk", s.NB),
                      ("lw", s.NB), ("lb", s.NB), ("fc", 3 * s.FB), ("fb", s.FB)):
            s.po[nm] = o; o += w
        s.PC = o


class KB:
    R = 8

    def __init__(s, nc):
        s.nc = nc
        s.es = ExitStack()
        s.eng = {"pe": nc.tensor, "act": nc.scalar, "dve": nc.vector, "pool": nc.gpsimd, "sp": nc.sync}
        s.csem = {e: s.es.enter_context(nc.semaphore("c_" + e)) for e in ("pe", "act", "dve", "pool")}
        s.ccnt = {e: 0 for e in s.csem}
        s.ring = {q: [s.es.enter_context(nc.semaphore("d_%s%d" % (q, i))) for i in range(s.R)]
                  for q in ("sp", "pool", "act")}
        s.dcnt = {q: 0 for q in s.ring}
        s.seen = {e: {} for e in s.eng}
        s.lastw = {}
        s.rd_c = {}
        s.rd_d = {}
        s.pending = {e: [] for e in s.csem}
        s.ps_i = 0

    def _wait(s, e, t):
        if t["eng"] == "pe" and e == "pe":
            return
        assert t["val"] is not None, "unresolved ticket"
        k = id(t["sem"])
        if s.seen[e].get(k, 0) >= t["val"]:
            return
        s.eng[e].wait_ge(t["sem"], t["val"])
        s.seen[e][k] = t["val"]

    def _deps(s, e, reads, writes):
        for k in reads:
            t = s.lastw.get(k)
            if t is not None:
                s._wait(e, t)
        for k in writes:
            t = s.lastw.get(k)
            if t is not None:
                s._wait(e, t)
            for t in s.rd_c.get(k, {}).values():
                s._wait(e, t)
            for t in s.rd_d.get(k, ()):
                s._wait(e, t)

    def _record(s, tk, reads, writes, isdma):
        for k in reads:
            if isdma:
                s.rd_d.setdefault(k, []).append(tk)
            else:
                s.rd_c.setdefault(k, {})[tk["eng"]] = tk
        for k in writes:
            s.lastw[k] = tk
            s.rd_c[k] = {}
            s.rd_d[k] = []

    def op(s, e, fn, reads=(), writes=(), signal=True):
        s._deps(e, reads, writes)
        ins = fn(s.eng[e])
        tk = {"sem": s.csem[e], "val": None, "eng": e}
        if signal:
            s.ccnt[e] += 1
            ins.then_inc(s.csem[e], 1)
            tk["val"] = s.ccnt[e]
            for p in s.pending[e]:
                p["val"] = s.ccnt[e]
            s.pending[e] = []
        else:
            s.pending[e].append(tk)
        s._record(tk, reads, writes, False)
        return tk

    def dma(s, q, out, in_, reads=(), writes=()):
        s._deps(q, reads, writes)
        j = s.dcnt[q]
        slot = j % s.R
        sem = s.ring[q][slot]
        if j >= s.R:
            s._wait(q, {"sem": sem, "val": 16 * (j // s.R), "eng": "dma"})
        s.eng[q].dma_start(out=out, in_=in_).then_inc(sem, 16)
        s.dcnt[q] += 1
        tk = {"sem": sem, "val": 16 * (j // s.R + 1), "eng": "dma"}
        s._record(tk, reads, writes, True)
        return tk

    def barrier(s):
        tks = []
        for e in s.csem:
            assert not s.pending[e], "pending unsignaled op at barrier on " + e
            if s.ccnt[e]:
                tks.append({"sem": s.csem[e], "val": s.ccnt[e], "eng": "x"})
        for q in s.ring:
            for i in range(s.R):
                if s.dcnt[q] > i:
                    uses = (s.dcnt[q] - i + s.R - 1) // s.R
                    tks.append({"sem": s.ring[q][i], "val": 16 * uses, "eng": "dma"})
        for e in s.eng:
            for t in tks:
                s._wait(e, t)
        s.lastw = {}
        s.rd_c = {}
        s.rd_d = {}


def build(cfg, debug_outs=()):
    c = cfg
    nc = bass.Bass("TRN2", target_bir_lowering=False)
    D, N, L = c.D, c.N, c.L
    NT = N // 512
    HALF = N // 2

    def din(name, shape, dt=F32):
        return nc.dram_tensor(name, list(shape), dt, kind="ExternalInput").ap()

    def dsc(name, shape, dt):
        kind = "ExternalOutput" if name in debug_outs else "Internal"
        return nc.dram_tensor(name, list(shape), dt, kind=kind).ap()

    x_in = din("x", [N, D])
    w_in = din("w_in", [L, D, c.INW])
    w_ua = din("w_up_attn", [L, c.AW, D])
    w_ur = din("w_up_rwkv", [L, c.RW, D])
    w_o = din("w_o", [L, D, D])
    w_fu = din("w_ffn_up", [L, D, 2 * c.DFF])
    w_fd = din("w_ffn_down", [L, c.DFF, D])
    dw2 = din("decay_w2", [L, 2, c.DL, c.RW])
    ia2 = din("iclr_a2", [L, 2, c.IL, c.RW])
    g2 = din("gate_g2", [L, c.GL, c.RW])
    pv = din("pv", [L, 128, c.PC])
    c_attb = din("c_attb", [128, (N // 128) * 2])
    c_carry = din("c_carry", [64, 2 * c.NCH])
    c_edge = din("c_edge", [128, 1])
    c_rope = din("c_rope", [128, 2, N])
    c_mats = din("c_mats", [128, 4, 128], BF16)
    c_tri = din("c_tri", [64, 5, 8, 64], BF16)
    c_reset = din("c_reset", [128, 512])
    y_out = nc.dram_tensor("y", [N, D], F32, kind="ExternalOutput").ap()

    hT = dsc("hT", [D, N], BF16)
    qraw = dsc("qraw", [c.AW + c.KVW, N], F32)
    vT = dsc("vT", [c.KVW, N], BF16)
    rkvraw = dsc("rkvraw", [3 * c.RW, N], F32)
    wlowT = dsc("wlowT", [2 * c.DL, N], BF16)
    alowT = dsc("alowT", [2 * c.IL, N], BF16)
    glowT = dsc("glowT", [c.GL, N], BF16)
    gA = dsc("gA", [D, N], BF16)
    gR = dsc("gR", [D, N], BF16)
    QKT = dsc("QKT", [c.AW + c.KVW, N], BF16)
    Vtm = dsc("Vtm", [N, c.KVW], BF16)
    attT = dsc("attT", [c.AW, N], BF16)
    sA = [dsc("sA%d" % d, [c.RW, N], BF16) for d in range(2)]
    sB = [dsc("sB%d" % d, [c.RW, N], BF16) for d in range(2)]
    sK = [dsc("sK%d" % d, [c.RW, N], BF16) for d in range(2)]
    sR = [dsc("sR%d" % d, [c.RW, N], BF16) for d in range(2)]
    sBh = [dsc("sBh%d" % d, [N, c.RW], BF16) for d in range(2)]
    sKh = [dsc("sKh%d" % d, [N, c.RW], BF16) for d in range(2)]
    sV = dsc("sV", [N, c.RW], BF16)
    sG = [dsc("sG%d" % d, [c.RW, c.NCH], F32) for d in range(2)]
    bonus = dsc("bonus", [c.RW, N], F32)
    gout = dsc("gout", [c.RW, N], F32)
    yT = [dsc("yT%d" % d, [c.RW, N], F32) for d in range(2)]
    rwT = dsc("rwT", [c.RW, N], BF16)
    mixT = dsc("mixT", [N // 128, 128, D // 128, 128], BF16)
    x1 = dsc("x1", [N, D], F32)
    x2 = dsc("x2", [N, D], F32)
    uT = dsc("uT", [2 * c.DFF, N], BF16)
    actT = dsc("actT", [N // 128, 128, c.DFF // 128, 128], BF16)

    _uid = [0]

    def SBT(name, shape, dt):
        _uid[0] += 1
        return nc.sbuf_tensor("%s_u%d" % (name, _uid[0]), shape, dt)

    kb = KB(nc)
    with kb.es:
        es0 = kb.es
        PS = [es0.enter_context(nc.psum_tensor("ps%d" % i, [128, 512], F32)) for i in range(6)]
        PSB = [es0.enter_context(nc.psum_tensor("psb%d" % i, [128, 1024], BF16)) for i in range(2)]
        mats = es0.enter_context(SBT("mats", [128, 4, 128], BF16))
        tri = es0.enter_context(SBT("tri", [64, 5, 8, 64], BF16))
        pvs = es0.enter_context(SBT("pvs", [128, c.PC], F32))
        edge = es0.enter_context(SBT("edge", [128, 1], F32))
        kb.dma("sp", mats[:], c_mats, writes=["mats"])
        kb.dma("sp", tri[:], c_tri, writes=["tri"])
        kb.dma("sp", edge[:], c_edge, writes=["edge"])
        ident = mats[:, 0, :]
        ones = mats[:, 1, :]
        blk = mats[:, 2, :]
        permT = mats[:, 3, :]
        ps_state = {"i": 0, "b": 0}

        def next_ps(lo=0, hi=6):
            i = lo + ps_state["i"] % (hi - lo)
            ps_state["i"] += 1
            return PS[i], ("ps", i)

        def next_psb():
            i = ps_state["b"] % 2
            ps_state["b"] += 1
            return PSB[i], ("psb", i)

        def rsqrt_to(dst, dkey, src, skey, scale, bias):
            kb.op("act", lambda e: e.activation(out=dst, in_=src, func=AF.Sqrt, bias=float(bias), scale=float(scale)),
                  reads=[skey], writes=[dkey])
            kb.op("dve", lambda e: e.reciprocal(out=dst, in_=dst), reads=[dkey], writes=[dkey])

        def pcol(name, j=0, n=1):
            o = c.po[name] + j
            return pvs[:, o:o + n]

        def phase_norm(xsrc, gname):
            with ExitStack() as es:
                KC = D // 128
                xt = [es.enter_context(SBT("n_xt%d" % i, [128, D], F32)) for i in range(2)]
                xb = [es.enter_context(SBT("n_xb%d" % i, [128, D], BF16)) for i in range(2)]
                junk = es.enter_context(SBT("n_junk", [128, D], BF16))
                st = es.enter_context(SBT("n_st", [128, 8], F32))
                hb = [es.enter_context(SBT("n_hb%d" % i, [128, KC, 512], BF16)) for i in range(2)]
                for ti in range(N // 128):
                    b = ti % 2
                    g = ti // 4
                    hbb = hb[g % 2]
                    kb.dma("sp", xt[b][:], xsrc[ti * 128:(ti + 1) * 128, :], writes=[("xt", b)])
                    sc = st[:, b * 4:b * 4 + 1]
                    rs = st[:, b * 4 + 1:b * 4 + 2]
                    kb.op("act", lambda e: e.activation(out=junk[:], in_=xt[b][:], func=AF.Square, accum_out=sc),
                          reads=[("xt", b)], writes=["junk", ("ss", b)])
                    rsqrt_to(rs, ("rs", b), sc, ("ss", b), 1.0 / D, 1e-6)
                    kb.op("act", lambda e: e.activation(out=xb[b][:], in_=xt[b][:], func=AF.Copy, scale=rs),
                          reads=[("xt", b), ("rs", b)], writes=[("xb", b)])
                    for q in range(KC // 8):
                        pt, pk = next_psb()
                        for j in range(8):
                            cc = q * 8 + j
                            kb.op("pe", lambda e: e.transpose(pt[:, j * 128:(j + 1) * 128],
                                                              xb[b][:, cc * 128:(cc + 1) * 128], ident),
                                  reads=[("xb", b), "mats"], writes=[pk], signal=(j == 7))
                        for j in range(8):
                            cc = q * 8 + j
                            eng = "act" if j % 2 == 0 else "dve"
                            dst = hbb[:, cc, (ti % 4) * 128:(ti % 4 + 1) * 128]
                            src = pt[:, j * 128:(j + 1) * 128]
                            gcol = pcol(gname, cc)
                            if eng == "act":
                                kb.op("act", lambda e: e.activation(out=dst, in_=src, func=AF.Copy, scale=gcol),
                                      reads=[pk, "pvs"], writes=[("hb", g % 2)])
                            else:
                                kb.op("dve", lambda e: e.tensor_scalar(out=dst, in0=src, scalar1=gcol, scalar2=None,
                                                                      op0=ALU.mult),
                                      reads=[pk, "pvs"], writes=[("hb", g % 2)])
                    if ti % 4 == 3:
                        kb.dma("sp", hT.rearrange("(c p) n -> p c n", p=128)[:, :, g * 512:(g + 1) * 512], hbb[:],
                               reads=[("hb", g % 2)])
                kb.barrier()

        def gemm_fm(pairs, groups, TT, epi):
            with ExitStack() as es:
                KCs = [(K + 127) // 128 for (_, _, K) in pairs]
                Xs = [es.enter_context(SBT("g_x%d" % i, [128, KCs[i], TT], BF16))
                      for i in range(len(pairs))]
                Ws = [[es.enter_context(SBT("g_w%d_%d" % (i, b), [128, KCs[i], 512], BF16))
                       for b in range(2)] for i in range(len(pairs))]
                wi = 0
                for st in range(N // TT):
                    for i, (X, W, K) in enumerate(pairs):
                        for kc in range(KCs[i]):
                            r = min(128, K - kc * 128)
                            kb.dma("sp", Xs[i][0:r, kc, :], X[kc * 128:kc * 128 + r, st * TT:(st + 1) * TT],
                                   writes=[("gx", i)])
                    for (c0, gw, blocks) in groups:
                        wb = wi % 2
                        wi += 1
                        for i, (X, W, K) in enumerate(pairs):
                            if K % 128 == 0:
                                kb.dma("pool", Ws[i][wb][:, :, 0:gw],
                                       W.rearrange("(c p) m -> p c m", p=128)[:, :, c0:c0 + gw],
                                       writes=[("gw", i, wb)])
                            else:
                                for kc in range(KCs[i]):
                                    r = min(128, K - kc * 128)
                                    kb.dma("pool", Ws[i][wb][0:r, kc, 0:gw], W[kc * 128:kc * 128 + r, c0:c0 + gw],
                                           writes=[("gw", i, wb)])
                        for (off, w, tag) in blocks:
                            for tg in range(TT // 512):
                                psl, pkl = [], []
                                for i, (X, W, K) in enumerate(pairs):
                                    pt, pk = next_ps()
                                    psl.append(pt[0:w, :])
                                    pkl.append(pk)
                                    for kc in range(KCs[i]):
                                        r = min(128, K - kc * 128)
                                        kb.op("pe", lambda e: e.matmul(
                                            pt[0:w, :], lhsT=Ws[i][wb][0:r, kc, off:off + w],
                                            rhs=Xs[i][0:r, kc, tg * 512:(tg + 1) * 512],
                                            start=(kc == 0), stop=(kc == KCs[i] - 1)),
                                            reads=[("gw", i, wb), ("gx", i)], writes=[pk],
                                            signal=(kc == KCs[i] - 1))
                                epi(psl, pkl, tag, c0 + off, w, st * TT + tg * 512)
                kb.barrier()

        def gemm_tm(X, W, K, xres, xdst):
            with ExitStack() as es:
                KC = (K + 127) // 128
                nwb = 2 if KC <= 32 else 1
                Wb = [es.enter_context(SBT("t_w%d" % b, [128, KC, 512], BF16)) for b in range(nwb)]
                Xb = [es.enter_context(SBT("t_x%d" % b, [128, KC, 128], BF16)) for b in range(3)]
                Rb = [es.enter_context(SBT("t_r%d" % b, [128, 512], F32)) for b in range(3)]
                xi = 0
                for cb in range(D // 512):
                    wb = cb % nwb
                    for kc0 in range(0, KC, 16):
                        kc1 = min(KC, kc0 + 16)
                        kb.dma("pool", Wb[wb][:, kc0:kc1, :],
                               W.rearrange("(c p) m -> p c m", p=128)[:, kc0:kc1, cb * 512:(cb + 1) * 512],
                               writes=[("tw", wb)])
                    for tt in range(N // 128):
                        b = xi % 3
                        xi += 1
                        kb.dma("sp", Xb[b][:], X[tt], writes=[("tx", b)])
                        kb.dma("sp", Rb[b][:], xres[tt * 128:(tt + 1) * 128, cb * 512:(cb + 1) * 512],
                               writes=[("tr", b)])
                        pt, pk = next_ps()
                        for kc in range(KC):
                            kb.op("pe", lambda e: e.matmul(pt[:, :], lhsT=Xb[b][:, kc, :], rhs=Wb[wb][:, kc, :],
                                                           start=(kc == 0), stop=(kc == KC - 1)),
                                  reads=[("tx", b), ("tw", wb)], writes=[pk], signal=(kc == KC - 1))
                        kb.op("dve", lambda e: e.tensor_tensor(out=Rb[b][:], in0=pt[:, :], in1=Rb[b][:], op=ALU.add),
                              reads=[pk, ("tr", b)], writes=[("tr", b)])
                        kb.dma("sp", xdst[tt * 128:(tt + 1) * 128, cb * 512:(cb + 1) * 512], Rb[b][:],
                               reads=[("tr", b)])
                kb.barrier()

        class Evac:
            def __init__(s, es, name, dt, nbuf=3):
                s.bufs = [es.enter_context(SBT("%s%d" % (name, i), [128, 512], dt)) for i in range(nbuf)]
                s.i = 0
                s.name = name

            def get(s):
                b = s.i % len(s.bufs)
                s.i += 1
                return s.bufs[b], (s.name, b)

        def phase_inproj(l):
            blocks = []
            o = 0
            segs = [("q", c.AW + c.KVW), ("v", c.KVW), ("rkv", 3 * c.RW), ("wl", 2 * c.DL), ("al", 2 * c.IL),
                    ("gl", c.GL), ("ga", D), ("gr", D)]
            for tag, wd in segs:
                so = 0
                while so < wd:
                    w = min(128, wd - so)
                    blocks.append((o + so, w, (tag, so)))
                    so += w
                o += wd
            groups = []
            cur = None
            for (a, w, tag) in blocks:
                if cur is None or (a + w - cur[0]) > 512:
                    cur = [a, 0, []]
                    groups.append(cur)
                cur[2].append((a - cur[0], w, tag))
                cur[1] = a + w - cur[0]
            with ExitStack() as es:
                ev32 = Evac(es, "ip_f", F32)
                ev16 = Evac(es, "ip_b", BF16)
                dst = {"q": (qraw, F32, AF.Copy), "v": (vT, BF16, AF.Copy), "rkv": (rkvraw, F32, AF.Copy),
                       "wl": (wlowT, BF16, AF.Tanh), "al": (alowT, BF16, AF.Copy), "gl": (glowT, BF16, AF.Sigmoid),
                       "ga": (gA, BF16, AF.Sigmoid), "gr": (gR, BF16, AF.Sigmoid)}
                cnt = [0]

                def epi(psl, pkl, tag, offabs, w, tok0):
                    dten, dt, fn = dst[tag[0]]
                    buf, bk = (ev32 if dt == F32 else ev16).get()
                    cnt[0] += 1
                    if fn == AF.Copy and cnt[0] % 2 == 0:
                        kb.op("dve", lambda e: e.tensor_copy(out=buf[0:w, :], in_=psl[0]),
                              reads=[pkl[0]], writes=[bk])
                    else:
                        kb.op("act", lambda e: e.activation(out=buf[0:w, :], in_=psl[0], func=fn),
                              reads=[pkl[0]], writes=[bk])
                    kb.dma("sp", dten[tag[1]:tag[1] + w, tok0:tok0 + 512], buf[0:w, :], reads=[bk])

                gemm_fm([(hT, w_in[l], D)], [tuple(g) for g in groups], min(N, 1024), epi)

        def phase_qkprep():
            with ExitStack() as es:
                rope = es.enter_context(SBT("rope", [128, 2, N], F32))
                kb.dma("sp", rope[:], c_rope, writes=["rope"])
                nb = 2
                T = lambda nm, dt: [es.enter_context(SBT("%s%d" % (nm, i), [128, 512], dt))
                                    for i in range(nb)]
                raw, rg, sq, xbf = T("q_raw", F32), T("q_rg", F32), T("q_sq", BF16), T("q_xb", BF16)
                rstd, t1, t2, ob = T("q_rs", F32), T("q_t1", F32), T("q_t2", F32), T("q_ob", BF16)
                it = 0
                for hd in range(c.QH + c.KVH):
                    gcol = pcol("qg") if hd < c.QH else pcol("kg")
                    for tg in range(NT):
                        b = it % nb
                        it += 1
                        ts = slice(tg * 512, (tg + 1) * 512)
                        kb.dma("sp", raw[b][:], qraw[hd * 128:(hd + 1) * 128, ts], writes=[("raw", b)])
                        kb.op("act", lambda e: e.activation(out=sq[b][:], in_=raw[b][:], func=AF.Square),
                              reads=[("raw", b)], writes=[("sq", b)])
                        kb.op("act", lambda e: e.activation(out=rg[b][:], in_=raw[b][:], func=AF.Copy, scale=gcol),
                              reads=[("raw", b), "pvs"], writes=[("rg", b)])
                        kb.op("dve", lambda e: e.tensor_copy(out=xbf[b][:], in_=rg[b][:]),
                              reads=[("rg", b)], writes=[("xbf", b)])
                        p1, k1 = next_ps()
                        kb.op("pe", lambda e: e.matmul(p1[:, :], lhsT=ones, rhs=sq[b][:], start=True, stop=True),
                              reads=[("sq", b), "mats"], writes=[k1])
                        p2, k2 = next_ps()
                        kb.op("pe", lambda e: e.matmul(p2[:, :], lhsT=permT, rhs=xbf[b][:], start=True, stop=True),
                              reads=[("xbf", b), "mats"], writes=[k2])
                        rsqrt_to(rstd[b][:], ("rstd", b), p1[:, :], k1, 1.0 / 128, 1e-6)
                        kb.op("dve", lambda e: e.tensor_tensor(out=t1[b][:], in0=rg[b][:], in1=rope[:, 0, ts],
                                                              op=ALU.mult),
                              reads=[("rg", b), "rope"], writes=[("t1", b)])
                        kb.op("dve", lambda e: e.tensor_tensor(out=t2[b][:], in0=p2[:, :], in1=rope[:, 1, ts],
                                                              op=ALU.mult),
                              reads=[k2, "rope"], writes=[("t2", b)])
                        kb.op("dve", lambda e: e.tensor_tensor(out=t1[b][:], in0=t1[b][:], in1=t2[b][:], op=ALU.add),
                              reads=[("t1", b), ("t2", b)], writes=[("t1", b)])
                        kb.op("dve", lambda e: e.tensor_tensor(out=ob[b][:], in0=t1[b][:], in1=rstd[b][:],
                                                              op=ALU.mult),
                              reads=[("t1", b), ("rstd", b)], writes=[("ob", b)])
                        kb.dma("sp", QKT[hd * 128:(hd + 1) * 128, ts], ob[b][:], reads=[("ob", b)])
                vin = T("q_vin", BF16)
                vo = [es.enter_context(SBT("q_vo%d" % i, [128, 4, 128], BF16)) for i in range(2)]
                it = 0
                for h in range(c.KVH):
                    for tg in range(NT):
                        b = it % 2
                        it += 1
                        kb.dma("sp", vin[b][:], vT[h * 128:(h + 1) * 128, tg * 512:(tg + 1) * 512],
                               writes=[("vin", b)])
                        pt, pk = next_psb()
                        for j in range(4):
                            kb.op("pe", lambda e: e.transpose(pt[:, j * 128:(j + 1) * 128],
                                                              vin[b][:, j * 128:(j + 1) * 128], ident),
                                  reads=[("vin", b), "mats"], writes=[pk], signal=(j == 3))
                        kb.op("dve", lambda e: e.tensor_copy(out=vo[b][:].rearrange("p a b -> p (a b)"),
                                                            in_=pt[:, 0:512]),
                              reads=[pk], writes=[("vo", b)])
                        kb.dma("sp", Vtm.rearrange("(a p) f -> p a f", p=128)[:, tg * 4:(tg + 1) * 4,
                                                                             h * 128:(h + 1) * 128],
                               vo[b][:], reads=[("vo", b)])
                kb.barrier()

        def phase_attn():
            with ExitStack() as es:
                NKT = N // 128
                attb = es.enter_context(SBT("attb", [128, NKT * 2], F32))
                kb.dma("sp", attb[:], c_attb, writes=["attb"])
                Kt = es.enter_context(SBT("a_K", [128, N], BF16))
                Vt = es.enter_context(SBT("a_V", [128, NKT, 128], BF16))
                Qt = [es.enter_context(SBT("a_Q%d" % i, [128, N], BF16)) for i in range(2)]
                Pb = [es.enter_context(SBT("a_P%d" % i, [128, 512], BF16)) for i in range(4)]
                rz = [es.enter_context(SBT("a_rz%d" % i, [128, 512], F32)) for i in range(2)]
                ob = [es.enter_context(SBT("a_o%d" % i, [128, 512], BF16)) for i in range(2)]
                scale = 128.0 ** -0.5
                qi = 0
                pi = 0
                oi = 0
                for h in range(c.KVH):
                    kb.dma("sp", Kt[:], QKT[c.AW + h * 128:c.AW + (h + 1) * 128, :], writes=["aK"])
                    kb.dma("sp", Vt[:], Vtm.rearrange("(a p) f -> p a f", p=128)[:, :, h * 128:(h + 1) * 128],
                           writes=["aV"])
                    for g in range(4):
                        qh = h * 4 + g
                        qb = qi % 2
                        qi += 1
                        kb.dma("sp", Qt[qb][:], QKT[qh * 128:(qh + 1) * 128, :], writes=[("aQ", qb)])
                        for qt in range(NT):
                            a = oi % 2
                            oi += 1
                            po, ko = PS[0], ("ps", 0)
                            pz, kz = PS[1], ("ps", 1)
                            qhalf = 0 if (qt * 512) < HALF else 1
                            sts = {}

                            def issue_S(kt):
                                pst, kst = next_ps(2, 6)
                                kb.op("pe", lambda e: e.matmul(pst[:, :], lhsT=Kt[:, kt * 128:(kt + 1) * 128],
                                                               rhs=Qt[qb][:, qt * 512:(qt + 1) * 512],
                                                               start=True, stop=True),
                                      reads=["aK", ("aQ", qb)], writes=[kst])
                                sts[kt] = (pst, kst)

                            issue_S(0)
                            if NKT > 1:
                                issue_S(1)
                            for kt in range(NKT):
                                if kt + 2 < NKT:
                                    issue_S(kt + 2)
                                pst, kst = sts.pop(kt)
                                pb = pi % 4
                                pi += 1
                                kb.op("act", lambda e: e.activation(out=Pb[pb][:], in_=pst[:, :], func=AF.Exp,
                                                                    bias=attb[:, kt * 2 + qhalf:kt * 2 + qhalf + 1],
                                                                    scale=scale),
                                      reads=[kst, "attb"], writes=[("aP", pb)])
                                kb.op("pe", lambda e: e.matmul(po[:, :], lhsT=Vt[:, kt, :], rhs=Pb[pb][:],
                                                               start=(kt == 0), stop=(kt == NKT - 1)),
                                      reads=["aV", ("aP", pb)], writes=[ko], signal=False)
                                kb.op("pe", lambda e: e.matmul(pz[:, :], lhsT=ones, rhs=Pb[pb][:],
                                                               start=(kt == 0), stop=(kt == NKT - 1)),
                                      reads=["mats", ("aP", pb)], writes=[kz], signal=True)
                            kb.op("dve", lambda e: e.reciprocal(out=rz[a][:], in_=pz[:, :]),
                                  reads=[kz], writes=[("rz", a)])
                            kb.op("dve", lambda e: e.tensor_tensor(out=ob[a][:], in0=po[:, :], in1=rz[a][:],
                                                                  op=ALU.mult),
                                  reads=[ko, ("rz", a)], writes=[("ao", a)])
                            kb.dma("sp", attT[qh * 128:(qh + 1) * 128, qt * 512:(qt + 1) * 512], ob[a][:],
                                   reads=[("ao", a)])
                kb.barrier()

        def phase_rwkvprep(l):
            with ExitStack() as es:
                NB = c.NB
                rst = es.enter_context(SBT("rp_rst", [128, 512], F32))
                kb.dma("sp", rst[:], c_reset, writes=["rst"])
                w2s = es.enter_context(SBT("rp_w2", [128, 2, c.RW], BF16))
                a2s = es.enter_context(SBT("rp_a2", [128, 2, c.RW], BF16))
                g2s = es.enter_context(SBT("rp_g2", [128, 4, c.RW], BF16))
                for d in range(2):
                    kb.dma("pool", w2s[:, d, :], dw2[l, d], writes=["w2s"])
                    kb.dma("pool", a2s[:, d, :], ia2[l, d], writes=["a2s"])
                for kc in range(4):
                    r = min(128, c.GL - kc * 128)
                    kb.dma("pool", g2s[0:r, kc, :], g2[l, kc * 128:kc * 128 + r, :], writes=["g2s"])
                names32 = ["rr", "kr", "vr", "r", "k", "v", "kku", "kk", "ag", "lw", "cum", "exc", "kd", "bb", "ee",
                           "tmp", "tmp2", "gg", "bon"]
                S2 = [{nm: es.enter_context(SBT("rp_" + nm, [128, 514 if nm in ("rr", "kr", "vr") else 512],
                                                          F32)) for nm in names32} for _ in range(2)]
                names16 = ["sqb", "rkb", "oA", "oB", "oK", "oR", "oBh", "oKh", "vb"]
                Sb2 = [{nm: es.enter_context(SBT("rp_" + nm, [128, 512], BF16)) for nm in names16} for _ in range(2)]
                Lw = {nm: es.enter_context(SBT("rp_" + nm, [128, 512], BF16)) for nm in ("wl0", "wl1", "al0", "al1")}
                dbl = set(names32) | set(names16) | {"tot"}
                cur = {"par": 0}

                def km(keys):
                    return [((k, cur["par"]) if (isinstance(k, str) and k in dbl) else k) for k in keys]
                glb = es.enter_context(SBT("rp_gl", [128, 4, 512], BF16))
                tot = es.enter_context(SBT("rp_tot", [128, 64], F32))
                otm = [es.enter_context(SBT("rp_otm%d" % i, [128, 4, 128], BF16)) for i in range(3)]
                nchk = 512 // 64

                def V(eng, fn, r, w, signal=True):
                    kb.op(eng, fn, reads=km(r), writes=km(w), signal=signal)

                def Dm(out, in_, r=(), w=()):
                    kb.dma("sp", out, in_, reads=km(r), writes=km(w))

                for tg in range(NT):
                    t0 = tg * 512
                    ts = slice(t0, t0 + 512)
                    for d in range(2):
                        kb.dma("sp", Lw["wl%d" % d][:], wlowT[d * c.DL:(d + 1) * c.DL, ts], writes=["wl%d" % d])
                        kb.dma("sp", Lw["al%d" % d][:], alowT[d * c.IL:(d + 1) * c.IL, ts], writes=["al%d" % d])
                    for kc in range(4):
                        r = min(128, c.GL - kc * 128)
                        kb.dma("sp", glb[0:r, kc, :], glowT[kc * 128:kc * 128 + r, ts], writes=["glb"])
                    for cb in range(NB):
                        cs = slice(cb * 128, (cb + 1) * 128)
                        par = cb % 2
                        cur["par"] = par
                        S = S2[par]
                        Sb = Sb2[par]
                        for i, nm in enumerate(("rr", "kr", "vr")):
                            lo = max(t0 - 1, 0)
                            hi = min(t0 + 513, N)
                            Dm(S[nm][:, (lo - (t0 - 1)):(hi - (t0 - 1))],
                                   rkvraw[i * c.RW + cb * 128:i * c.RW + (cb + 1) * 128, lo:hi], w=[nm])
                            if lo == 0 and t0 == 0:
                                V("dve", lambda e: e.memset(S[nm][:, 0:1], 0.0), [], [nm])
                            if hi == N and t0 + 512 == N:
                                V("dve", lambda e: e.memset(S[nm][:, 513:514], 0.0), [], [nm])
                            on = ("r", "k", "v")[i]
                            cw = lambda tap: pcol("conv", (tap * 3 + i) * NB + cb)
                            src = S[nm]
                            V("dve", lambda e: e.tensor_scalar(out=S[on][:], in0=src[:, 1:513], scalar1=cw(1),
                                                               scalar2=None, op0=ALU.mult), [nm, "pvs"], [on])
                            V("dve", lambda e: e.scalar_tensor_tensor(out=S[on][:], in0=src[:, 0:512], scalar=cw(0),
                                                                      in1=S[on][:], op0=ALU.mult, op1=ALU.add),
                              [nm, on, "pvs"], [on])
                            V("dve", lambda e: e.scalar_tensor_tensor(out=S[on][:], in0=src[:, 2:514], scalar=cw(2),
                                                                      in1=S[on][:], op0=ALU.mult, op1=ALU.add),
                              [nm, on, "pvs"], [on])
                            if t0 == HALF:
                                V("dve", lambda e: e.tensor_scalar(out=S["tmp"][:, 0:1], in0=src[:, 0:1],
                                                                   scalar1=cw(0), scalar2=edge[:, 0:1],
                                                                   op0=ALU.mult, op1=ALU.mult),
                                  [nm, "pvs", "edge"], ["tmp"])
                                V("dve", lambda e: e.tensor_tensor(out=S[on][:, 0:1], in0=S[on][:, 0:1],
                                                                   in1=S["tmp"][:, 0:1], op=ALU.add),
                                  ["tmp", on], [on])
                            if t0 + 512 == HALF:
                                V("dve", lambda e: e.tensor_scalar(out=S["tmp"][:, 0:1], in0=src[:, 513:514],
                                                                   scalar1=cw(2), scalar2=edge[:, 0:1],
                                                                   op0=ALU.mult, op1=ALU.mult),
                                  [nm, "pvs", "edge"], ["tmp"])
                                V("dve", lambda e: e.tensor_tensor(out=S[on][:, 511:512], in0=S[on][:, 511:512],
                                                                   in1=S["tmp"][:, 0:1], op=ALU.add),
                                  ["tmp", on], [on])
                        V("dve", lambda e: e.tensor_scalar(out=S["kku"][:], in0=S["k"][:], scalar1=pcol("kk", cb),
                                                           scalar2=None, op0=ALU.mult), ["k", "pvs"], ["kku"])
                        V("act", lambda e: e.activation(out=Sb["sqb"][:], in_=S["kku"][:], func=AF.Square),
                          ["kku"], ["sqb"])
                        p1, k1 = next_ps()
                        V("pe", lambda e: e.matmul(p1[:, :], lhsT=blk, rhs=Sb["sqb"][:], start=True, stop=True),
                              r=["sqb", "mats"], w=[k1])
                        rsqrt_to(S["tmp"][:], "tmp", p1[:, :], k1, 1.0, 1e-24)
                        V("dve", lambda e: e.tensor_tensor(out=S["kk"][:], in0=S["kku"][:], in1=S["tmp"][:],
                                                           op=ALU.mult), ["kku", "tmp"], ["kk"])
                        V("act", lambda e: e.activation(out=Sb["vb"][:], in_=S["v"][:], func=AF.Copy), ["v"], ["vb"])
                        pg, kg_ = next_ps()
                        for kc in range(4):
                            r = min(128, c.GL - kc * 128)
                            V("pe", lambda e: e.matmul(pg[:, :], lhsT=g2s[0:r, kc, cs], rhs=glb[0:r, kc, :],
                                                           start=(kc == 0), stop=(kc == 3)),
                                  r=["g2s", "glb"], w=[kg_], signal=(kc == 3))
                        V("act", lambda e: e.activation(out=S["gg"][:], in_=pg[:, :], func=AF.Copy), [kg_], ["gg"])
                        Dm(gout[cs, ts], S["gg"][:], r=["gg"])
                        pbn, kbn = next_ps()
                        for d in range(2):
                            pw, kw = next_ps()
                            V("pe", lambda e: e.matmul(pw[:, :], lhsT=w2s[:, d, cs], rhs=Lw["wl%d" % d][:],
                                                           start=True, stop=True),
                                  r=["w2s", "wl%d" % d], w=[kw])
                            pa, ka = next_ps()
                            V("pe", lambda e: e.matmul(pa[:, :], lhsT=a2s[:, d, cs], rhs=Lw["al%d" % d][:],
                                                           start=True, stop=True),
                                  r=["a2s", "al%d" % d], w=[ka])
                            V("act", lambda e: e.activation(out=S["lw"][:], in_=pw[:, :], func=AF.Sigmoid,
                                                            bias=pcol("w0", d * NB + cb), scale=1.0),
                              [kw, "pvs"], ["lw"])
                            V("act", lambda e: e.activation(out=S["ag"][:], in_=pa[:, :], func=AF.Sigmoid,
                                                            bias=pcol("a0", d * NB + cb), scale=1.0),
                              [ka, "pvs"], ["ag"])
                            V("dve", lambda e: e.tensor_scalar(out=S["lw"][:], in0=S["lw"][:],
                                                               scalar1=-math.exp(-0.5), scalar2=None, op0=ALU.mult),
                              ["lw"], ["lw"])
                            V("dve", lambda e: e.tensor_tensor_scan(out=S["cum"][:], data0=rst[:], data1=S["lw"][:],
                                                                    initial=0.0, op0=ALU.mult, op1=ALU.add),
                              ["lw", "rst"], ["cum"])
                            V("dve", lambda e: e.tensor_tensor(out=S["exc"][:], in0=S["cum"][:], in1=S["lw"][:],
                                                               op=ALU.subtract), ["cum", "lw"], ["exc"])
                            cum3 = S["cum"][:].rearrange("p (a b) -> p a b", b=64)
                            to = par * 32 + d * 16
                            V("dve", lambda e: e.tensor_copy(out=tot[:, to:to + nchk], in_=cum3[:, :, 63]),
                              ["cum"], ["tot"])
                            V("dve", lambda e: e.tensor_scalar(out=tot[:, to + 8:to + 8 + nchk], in0=tot[:, to:to + nchk],
                                                               scalar1=-1.0, scalar2=None, op0=ALU.mult),
                              ["tot"], ["tot"])
                            V("act", lambda e: e.activation(out=S["tmp"][:, 0:nchk], in_=tot[:, to:to + nchk],
                                                            func=AF.Exp), ["tot"], ["tmp"])
                            Dm(sG[d][cs, tg * nchk:(tg + 1) * nchk], S["tmp"][:, 0:nchk], r=["tmp"])
                            V("dve", lambda e: e.tensor_scalar(out=S["kd"][:], in0=S["ag"][:], scalar1=-1.0,
                                                               scalar2=pcol("ka", cb), op0=ALU.add, op1=ALU.mult),
                              ["ag", "pvs"], ["kd"])
                            V("dve", lambda e: e.scalar_tensor_tensor(out=S["kd"][:], in0=S["kd"][:], scalar=1.0,
                                                                      in1=S["k"][:], op0=ALU.add, op1=ALU.mult),
                              ["kd", "k"], ["kd"])
                            V("dve", lambda e: e.tensor_tensor(out=S["bb"][:], in0=S["ag"][:], in1=S["kk"][:],
                                                               op=ALU.mult), ["ag", "kk"], ["bb"])
                            V("dve", lambda e: e.scalar_tensor_tensor(out=Sb["rkb"][:], in0=S["r"][:],
                                                                      scalar=pcol("rk", cb), in1=S["kd"][:],
                                                                      op0=ALU.mult, op1=ALU.mult),
                              ["r", "kd", "pvs"], ["rkb"])
                            V("pe", lambda e: e.matmul(pbn[:, :], lhsT=blk, rhs=Sb["rkb"][:], start=(d == 0),
                                                           stop=(d == 1)),
                                  r=["rkb", "mats"], w=[kbn], signal=True)
                            def expo(srcn, sgn, sb):
                                if sb == 0:
                                    V("act", lambda e: e.activation(out=S["ee"][:], in_=S[srcn][:], func=AF.Exp,
                                                                    scale=float(sgn)), [srcn], ["ee"])
                                else:
                                    base = to if sb > 0 else to + 8
                                    for ch in range(nchk):
                                        V("act", lambda e: e.activation(out=S["ee"][:, ch * 64:(ch + 1) * 64],
                                                                        in_=S[srcn][:, ch * 64:(ch + 1) * 64],
                                                                        func=AF.Identity,
                                                                        bias=tot[:, base + ch:base + ch + 1],
                                                                        scale=float(sgn)), [srcn, "tot"], ["ee"])
                                    V("act", lambda e: e.activation(out=S["ee"][:], in_=S["ee"][:], func=AF.Exp),
                                      ["ee"], ["ee"])

                            def prod(onm, anm, neg=False):
                                if neg:
                                    V("dve", lambda e: e.scalar_tensor_tensor(out=Sb[onm][:], in0=S[anm][:],
                                                                              scalar=-1.0, in1=S["ee"][:],
                                                                              op0=ALU.mult, op1=ALU.mult),
                                      [anm, "ee"], [onm])
                                else:
                                    V("dve", lambda e: e.tensor_tensor(out=Sb[onm][:], in0=S[anm][:], in1=S["ee"][:],
                                                                       op=ALU.mult), [anm, "ee"], [onm])

                            if d == 0:
                                specs = [("cum", 1, 0, [("oR", "r", False)]), ("exc", 1, 0, [("oA", "kk", True)]),
                                         ("cum", -1, 0, [("oB", "bb", False), ("oK", "kd", False)]),
                                         ("cum", -1, 1, [("oBh", "bb", False), ("oKh", "kd", False)])]
                            else:
                                specs = [("exc", -1, 1, [("oR", "r", False)]), ("cum", -1, 1, [("oA", "kk", True)]),
                                         ("exc", 1, -1, [("oB", "bb", False), ("oK", "kd", False)]),
                                         ("exc", 1, 0, [("oBh", "bb", False), ("oKh", "kd", False)])]
                            for (srcn, sgn, sb, outs) in specs:
                                expo(srcn, sgn, sb)
                                for (onm, anm, neg) in outs:
                                    prod(onm, anm, neg)
                            for onm, dten in (("oA", sA[d]), ("oB", sB[d]), ("oK", sK[d]), ("oR", sR[d])):
                                Dm(dten[cs, ts], Sb[onm][:], r=[onm])
                            for onm, dten in (("oBh", sBh[d]), ("oKh", sKh[d])) + ((("vb", sV),) if d == 0 else ()):
                                pt, pk = next_psb()
                                for j in range(4):
                                    V("pe", lambda e: e.transpose(pt[:, j * 128:(j + 1) * 128],
                                                                      Sb[onm][:, j * 128:(j + 1) * 128], ident),
                                          r=[onm, "mats"], w=[pk], signal=(j == 3))
                                ob_i = ps_state.setdefault("otm", 0) % 3
                                ps_state["otm"] += 1
                                V("act", lambda e: e.activation(out=otm[ob_i][:].rearrange("p a b -> p (a b)"),
                                                                in_=pt[:, 0:512], func=AF.Copy),
                                  [pk], [("otm", ob_i)])
                                Dm(dten.rearrange("(a p) f -> p a f", p=128)[:, tg * 4:(tg + 1) * 4, cs],
                                       otm[ob_i][:], r=[("otm", ob_i)])
                        V("dve", lambda e: e.tensor_tensor(out=S["bon"][:], in0=pbn[:, :], in1=S["v"][:],
                                                           op=ALU.mult), [kbn, "v"], ["bon"])
                        V("dve", lambda e: e.tensor_scalar(out=S["bon"][:], in0=S["bon"][:], scalar1=pcol("lb", cb),
                                                           scalar2=None, op0=ALU.add), ["bon", "pvs"], ["bon"])
                        Dm(bonus[cs, ts], S["bon"][:], r=["bon"])
                kb.barrier()

        def phase_scan():
            with ExitStack() as es:
                carry = es.enter_context(SBT("sc_carry", [64, 2 * c.NCH], F32))
                kb.dma("sp", carry[:], c_carry, writes=["carry"])
                NHG = c.RH // 8
                chains = [(d, hg) for hg in range(NHG) for d in range(2)]
                CPR = 3
                for r0 in range(0, len(chains), CPR):
                    with ExitStack() as es2:
                        act_ch = chains[r0:r0 + CPR]
                        st = {}
                        for ci, (d, hg) in enumerate(act_ch):
                            A = lambda nm, shp, dt: es2.enter_context(
                                SBT("sc%d_%s" % (ci, nm), shp, dt))
                            s_ = {"fm": [A("fm%d" % b, [64, 4, 8, 128], BF16) for b in range(2)],
                                  "tm": [A("tm%d" % b, [64, 3, 512], BF16) for b in range(2)],
                                  "H32": A("H32", [64, 512], F32), "Hb": A("Hb", [64, 512], BF16),
                                  "gt": A("gt", [64, 8, c.NCH], F32),
                                  "yb": [A("yb%d" % b, [64, 8, 128], F32) for b in range(2)]}
                            for nm in ("Nn", "NT", "AK", "RB", "RK", "X", "U"):
                                s_[nm] = A(nm, [64, 512], BF16)
                            for nm in ("Q", "QT", "P"):
                                s_[nm] = [A("%s%d" % (nm, b), [64, 512], BF16) for b in range(2)]
                            st[ci] = s_
                            kb.op("dve", lambda e: e.memset(s_["H32"][:], 0.0), [], [("H32", ci)])
                            kb.op("dve", lambda e: e.memset(s_["Hb"][:], 0.0), [], [("Hb", ci)])
                            kb.dma("sp", s_["gt"][:],
                                   sG[d].rearrange("(h c) n -> c h n", c=64)[:, hg * 8:(hg + 1) * 8, :],
                                   writes=[("gt", ci)])

                        def unit(ci, d, hg, step):
                            s_ = st[ci]
                            ch = step if d == 0 else c.NCH - 1 - step
                            pair = step // 2
                            pb = pair % 2
                            sub = (ch % 2)
                            K = lambda nm: (nm, ci)
                            if step % 2 == 0:
                                c2 = (ch // 2) * 2
                                tsl = slice(c2 * 64, c2 * 64 + 128)
                                for i, ten in enumerate((sA[d], sB[d], sK[d], sR[d])):
                                    kb.dma("sp", s_["fm"][pb][:, i, :, :],
                                           ten.rearrange("(h c) n -> c h n", c=64)[:, hg * 8:(hg + 1) * 8, tsl],
                                           writes=[("fm", ci, pb)])
                            tmb = step % 2
                            for i, ten in enumerate((sV, sBh[d], sKh[d])):
                                kb.dma("sp", s_["tm"][tmb][:, i, :], ten[ch * 64:(ch + 1) * 64, hg * 512:(hg + 1) * 512],
                                       writes=[("tm", ci, tmb)])
                            fm = s_["fm"][pb]
                            tsub = slice(sub * 64, (sub + 1) * 64)
                            At = lambda h: fm[:, 0, h, tsub]
                            Bt = lambda h: fm[:, 1, h, tsub]
                            Kt = lambda h: fm[:, 2, h, tsub]
                            Rt = lambda h: fm[:, 3, h, tsub]
                            tm = s_["tm"][tmb]
                            Vh = lambda h: tm[:, 0, h * 64:(h + 1) * 64]
                            Bh = lambda h: tm[:, 1, h * 64:(h + 1) * 64]
                            Kh = lambda h: tm[:, 2, h * 64:(h + 1) * 64]
                            hs = lambda t, h: t[0:64, h * 64:(h + 1) * 64]
                            FMK = ("fm", ci, pb)
                            TMK = ("tm", ci, tmb)
                            ms, mi = (0, 1) if d == 0 else (2, 3)
                            mt = 2 if d == 0 else 0

                            def mm8(lh, rh, reads, masks, outs):
                                pt, pk = next_ps()
                                for h in range(8):
                                    kb.op("pe", lambda e: e.matmul(hs(pt, h), lhsT=lh(h), rhs=rh(h), start=True,
                                                                   stop=True),
                                          reads=reads, writes=[pk], signal=(h == 7))
                                return pt, pk

                            def evm(pt, pk, mask, dst, dk, eng="dve"):
                                kb.op(eng, lambda e: e.tensor_tensor(
                                    out=dst[:], in0=pt[0:64, :], in1=tri[:, mask, :, :].rearrange("p a b -> p (a b)"),
                                    op=ALU.mult), reads=[pk, "tri"], writes=[dk])

                            def evc(pt, pk, dst, dk, eng):
                                if eng == "act":
                                    kb.op("act", lambda e: e.activation(out=dst[:], in_=pt[0:64, :], func=AF.Copy),
                                          reads=[pk], writes=[dk])
                                else:
                                    kb.op("dve", lambda e: e.tensor_copy(out=dst[:], in_=pt[0:64, :]),
                                          reads=[pk], writes=[dk])
                            pt, pk = mm8(Bt, At, [FMK], None, None)
                            evm(pt, pk, ms, s_["Nn"], K("Nn"))
                            pt, pk = mm8(At, Bt, [FMK], None, None)
                            evm(pt, pk, mt, s_["NT"], K("NT"))
                            yield
                            pt, pk = mm8(Kt, At, [FMK], None, None)
                            evm(pt, pk, ms, s_["AK"], K("AK"))
                            pt, pk = mm8(Bt, Rt, [FMK], None, None)
                            evm(pt, pk, mi, s_["RB"], K("RB"))
                            pt, pk = mm8(Kt, Rt, [FMK], None, None)
                            evm(pt, pk, mi, s_["RK"], K("RK"))
                            yield
                            kb.op("dve", lambda e: e.tensor_tensor(
                                out=s_["P"][0][:], in0=s_["Nn"][:],
                                in1=tri[:, 4, :, :].rearrange("p a b -> p (a b)"), op=ALU.add),
                                reads=[K("Nn"), "tri"], writes=[K("P0")])
                            Qc, QTc, Pc = s_["Nn"], s_["NT"], s_["P"][0]
                            Qk, QTk, Pk_ = K("Nn"), K("NT"), K("P0")
                            nlev = 5
                            for j in range(1, nlev + 1):
                                jb = j % 2
                                last = (j == nlev)
                                if not last:
                                    pt, pk = mm8(lambda h: hs(QTc, h), lambda h: hs(Qc, h), [Qk, QTk], None, None)
                                    Qn, Qnk = s_["Q"][jb], K("Q%d" % jb)
                                    evc(pt, pk, Qn, Qnk, "act")
                                pt, pk = mm8(lambda h: hs(Qc, h), lambda h: hs(QTc, h), [Qk, QTk], None, None)
                                QTn, QTnk = s_["QT"][jb], K("QT%d" % jb)
                                evc(pt, pk, QTn, QTnk, "act")
                                yield
                                pt, pk = mm8(lambda h: hs(QTn, h), lambda h: hs(Pc, h), [QTnk, Pk_], None, None)
                                Pn, Pnk = s_["P"][jb], K("P%d" % jb)
                                kb.op("dve", lambda e: e.tensor_tensor(out=Pn[:], in0=pt[0:64, :], in1=Pc[:], op=ALU.add),
                                      reads=[pk, Pk_], writes=[Pnk])
                                Pc, Pk_ = Pn, Pnk
                                if not last:
                                    Qc, Qk = Qn, Qnk
                                QTc, QTk = QTn, QTnk
                                yield
                            Hb = s_["Hb"]
                            pt, pk = next_ps()
                            for h in range(8):
                                kb.op("pe", lambda e: e.matmul(hs(pt, h), lhsT=At(h), rhs=hs(Hb, h), start=True,
                                                               stop=False),
                                      reads=[FMK, K("Hb")], writes=[pk], signal=False)
                                kb.op("pe", lambda e: e.matmul(hs(pt, h), lhsT=hs(s_["AK"], h), rhs=Vh(h),
                                                               start=False, stop=True),
                                      reads=[K("AK"), TMK], writes=[pk], signal=(h == 7))
                            evc(pt, pk, s_["X"], K("X"), "dve")
                            yield
                            pt, pk = mm8(lambda h: hs(Pc, h), lambda h: hs(s_["X"], h), [Pk_, K("X")], None, None)
                            evc(pt, pk, s_["U"], K("U"), "act")
                            yield
                            pt, pk = next_ps()
                            for h in range(8):
                                kb.op("pe", lambda e: e.matmul(hs(pt, h), lhsT=hs(Hb, h), rhs=Rt(h), start=True,
                                                               stop=False),
                                      reads=[FMK, K("Hb")], writes=[pk], signal=False)
                                kb.op("pe", lambda e: e.matmul(hs(pt, h), lhsT=hs(s_["U"], h), rhs=hs(s_["RB"], h),
                                                               start=False, stop=False),
                                      reads=[K("U"), K("RB")], writes=[pk], signal=False)
                                kb.op("pe", lambda e: e.matmul(hs(pt, h), lhsT=Vh(h), rhs=hs(s_["RK"], h),
                                                               start=False, stop=True),
                                      reads=[TMK, K("RK")], writes=[pk], signal=(h == 7))
                            yb = s_["yb"][pb]
                            kb.op("dve", lambda e: e.tensor_copy(
                                out=yb[:, :, tsub], in_=pt[0:64, :].rearrange("p (a b) -> p a b", b=64)),
                                reads=[pk], writes=[("yb", ci, pb)])
                            if step % 2 == 1:
                                c2 = (ch // 2) * 2
                                kb.dma("sp", yT[d].rearrange("(h c) n -> c h n", c=64)[:, hg * 8:(hg + 1) * 8,
                                                                                      c2 * 64:c2 * 64 + 128],
                                       yb[:], reads=[("yb", ci, pb)])
                            pt, pk = next_ps()
                            for h in range(8):
                                kb.op("pe", lambda e: e.matmul(hs(pt, h), lhsT=Bh(h), rhs=hs(s_["U"], h), start=True,
                                                               stop=False),
                                      reads=[TMK, K("U")], writes=[pk], signal=False)
                                kb.op("pe", lambda e: e.matmul(hs(pt, h), lhsT=Kh(h), rhs=Vh(h), start=False,
                                                               stop=True),
                                      reads=[TMK], writes=[pk], signal=(h == 7))
                            H32 = s_["H32"]
                            for h in range(8):
                                kb.op("dve", lambda e: e.scalar_tensor_tensor(
                                    out=hs(H32, h), in0=hs(H32, h), scalar=s_["gt"][:, h, ch:ch + 1],
                                    in1=hs(pt, h), op0=ALU.mult, op1=ALU.add),
                                    reads=[pk, K("H32"), K("gt")], writes=[K("H32")])
                            ccol = carry[:, d * c.NCH + ch:d * c.NCH + ch + 1]
                            kb.op("dve", lambda e: e.tensor_scalar(out=H32[:], in0=H32[:], scalar1=ccol, scalar2=None,
                                                                  op0=ALU.mult),
                                  reads=[K("H32"), "carry"], writes=[K("H32")])
                            kb.op("act", lambda e: e.activation(out=Hb[:], in_=H32[:], func=AF.Copy),
                                  reads=[K("H32")], writes=[K("Hb")])
                            yield

                        for step in range(c.NCH):
                            gens = [unit(ci, d, hg, step) for ci, (d, hg) in enumerate(act_ch)]
                            while gens:
                                for g in list(gens):
                                    try:
                                        next(g)
                                    except StopIteration:
                                        gens.remove(g)
                        kb.barrier()

        def phase_rwkvpost():
            with ExitStack() as es:
                nb = 2
                T = lambda nm, dt: [es.enter_context(SBT("po_%s%d" % (nm, i), [128, 512], dt))
                                    for i in range(nb)]
                y0, y1, bn, gg, dd, t1 = T("y0", F32), T("y1", F32), T("bn", F32), T("gg", F32), T("dd", F32), T("t1", F32)
                yb16, sq16, ob = T("yb", BF16), T("sq", BF16), T("ob", BF16)
                it = 0
                for cb in range(c.NB):
                    cs = slice(cb * 128, (cb + 1) * 128)
                    for tg in range(NT):
                        b = it % nb
                        it += 1
                        ts = slice(tg * 512, (tg + 1) * 512)
                        kb.dma("sp", y0[b][:], yT[0][cs, ts], writes=[("y0", b)])
                        kb.dma("sp", y1[b][:], yT[1][cs, ts], writes=[("y1", b)])
                        kb.dma("sp", bn[b][:], bonus[cs, ts], writes=[("bn", b)])
                        kb.dma("sp", gg[b][:], gout[cs, ts], writes=[("gg", b)])
                        kb.op("dve", lambda e: e.tensor_tensor(out=y0[b][:], in0=y0[b][:], in1=y1[b][:], op=ALU.add),
                              reads=[("y0", b), ("y1", b)], writes=[("y0", b)])
                        kb.op("act", lambda e: e.activation(out=yb16[b][:], in_=y0[b][:], func=AF.Copy),
                              reads=[("y0", b)], writes=[("yb", b)])
                        p1, k1 = next_ps()
                        kb.op("pe", lambda e: e.matmul(p1[:, :], lhsT=blk, rhs=yb16[b][:], start=True, stop=True),
                              reads=[("yb", b), "mats"], writes=[k1])
                        kb.op("dve", lambda e: e.scalar_tensor_tensor(out=dd[b][:], in0=p1[:, :], scalar=-1.0 / 64,
                                                                     in1=y0[b][:], op0=ALU.mult, op1=ALU.add),
                              reads=[k1, ("y0", b)], writes=[("dd", b)])
                        kb.op("act", lambda e: e.activation(out=sq16[b][:], in_=dd[b][:], func=AF.Square),
                              reads=[("dd", b)], writes=[("sq", b)])
                        p2, k2 = next_ps()
                        kb.op("pe", lambda e: e.matmul(p2[:, :], lhsT=blk, rhs=sq16[b][:], start=True, stop=True),
                              reads=[("sq", b), "mats"], writes=[k2])
                        rsqrt_to(t1[b][:], ("t1", b), p2[:, :], k2, 1.0 / 64, 64e-5)
                        kb.op("dve", lambda e: e.tensor_scalar(out=t1[b][:], in0=t1[b][:], scalar1=pcol("lw", cb),
                                                              scalar2=None, op0=ALU.mult),
                              reads=[("t1", b), "pvs"], writes=[("t1", b)])
                        kb.op("dve", lambda e: e.tensor_tensor(out=dd[b][:], in0=dd[b][:], in1=t1[b][:], op=ALU.mult),
                              reads=[("dd", b), ("t1", b)], writes=[("dd", b)])
                        kb.op("dve", lambda e: e.tensor_tensor(out=dd[b][:], in0=dd[b][:], in1=bn[b][:], op=ALU.add),
                              reads=[("dd", b), ("bn", b)], writes=[("dd", b)])
                        kb.op("dve", lambda e: e.tensor_tensor(out=ob[b][:], in0=dd[b][:], in1=gg[b][:], op=ALU.mult),
                              reads=[("dd", b), ("gg", b)], writes=[("ob", b)])
                        kb.dma("sp", rwT[cs, ts], ob[b][:], reads=[("ob", b)])
                kb.barrier()

        def phase_merge(l):
            with ExitStack() as es:
                ga_b = [es.enter_context(SBT("m_ga%d" % i, [128, 512], BF16)) for i in range(3)]
                gr_b = [es.enter_context(SBT("m_gr%d" % i, [128, 512], BF16)) for i in range(3)]
                t1 = [es.enter_context(SBT("m_t%d" % i, [128, 512], F32)) for i in range(3)]
                ob = [es.enter_context(SBT("m_o%d" % i, [128, 512], BF16)) for i in range(3)]
                cnt = [0]

                def epi(psl, pkl, tag, offabs, w, tok0):
                    b = cnt[0] % 3
                    cnt[0] += 1
                    ts = slice(tok0, tok0 + 512)
                    kb.dma("sp", ga_b[b][:], gA[offabs:offabs + 128, ts], writes=[("mga", b)])
                    kb.dma("sp", gr_b[b][:], gR[offabs:offabs + 128, ts], writes=[("mgr", b)])
                    kb.op("dve", lambda e: e.tensor_tensor(out=t1[b][:], in0=psl[0], in1=ga_b[b][:], op=ALU.mult),
                          reads=[pkl[0], ("mga", b)], writes=[("mt", b)])
                    kb.op("dve", lambda e: e.tensor_tensor(out=ob[b][:], in0=psl[1], in1=gr_b[b][:], op=ALU.mult),
                          reads=[pkl[1], ("mgr", b)], writes=[("mo", b)])
                    kb.op("dve", lambda e: e.tensor_tensor(out=ob[b][:], in0=ob[b][:], in1=t1[b][:], op=ALU.add),
                          reads=[("mo", b), ("mt", b)], writes=[("mo", b)])
                    kb.dma("sp", mixT[tok0 // 128:tok0 // 128 + 4, :, offabs // 128, :].rearrange("t p k -> p t k"),
                           ob[b][:].rearrange("p (t k) -> p t k", k=128), reads=[("mo", b)])

                groups = [(g * 512, 512, [(j * 128, 128, None) for j in range(4)]) for g in range(D // 512)]
                gemm_fm([(attT, w_ua[l], c.AW), (rwT, w_ur[l], c.RW)], groups, min(N, 1024), epi)

        def phase_ffnup(l):
            with ExitStack() as es:
                ev16 = Evac(es, "fu_b", BF16)
                cnt = [0]

                def epi(psl, pkl, tag, offabs, w, tok0):
                    buf, bk = ev16.get()
                    cnt[0] += 1
                    if cnt[0] % 2 == 0:
                        kb.op("dve", lambda e: e.tensor_copy(out=buf[0:w, :], in_=psl[0]), reads=[pkl[0]], writes=[bk])
                    else:
                        kb.op("act", lambda e: e.activation(out=buf[0:w, :], in_=psl[0], func=AF.Copy),
                              reads=[pkl[0]], writes=[bk])
                    kb.dma("sp", uT[offabs:offabs + w, tok0:tok0 + 512], buf[0:w, :], reads=[bk])

                M = 2 * c.DFF
                groups = []
                o = 0
                while o < M:
                    gw = min(512, M - o)
                    groups.append((o, gw, [(j * 128, 128, None) for j in range(gw // 128)]))
                    o += gw
                gemm_fm([(hT, w_fu[l], D)], groups, min(N, 1024), epi)

        def phase_ffnact():
            with ExitStack() as es:
                nb = 2
                uv = [es.enter_context(SBT("fa_uv%d" % i, [128, N + 2], BF16)) for i in range(nb)]
                ug = [es.enter_context(SBT("fa_ug%d" % i, [128, N + 2], BF16)) for i in range(nb)]
                cv = [es.enter_context(SBT("fa_cv%d" % i, [128, N], F32)) for i in range(nb)]
                cg = [es.enter_context(SBT("fa_cg%d" % i, [128, N], F32)) for i in range(nb)]
                ob = [es.enter_context(SBT("fa_o%d" % i, [128, N], BF16)) for i in range(nb)]
                tmp = es.enter_context(SBT("fa_tmp", [128, 4], F32))
                for i in range(nb):
                    for t in (uv[i], ug[i]):
                        kb.op("dve", lambda e: e.memset(t[:, 0:1], 0.0), [], [("fau", i)])
                        kb.op("dve", lambda e: e.memset(t[:, N + 1:N + 2], 0.0), [], [("fau", i)])
                FBh = c.FB // 2
                for j in range(FBh):
                    b = j % nb
                    kb.dma("sp", uv[b][:, 1:N + 1], uT[j * 128:(j + 1) * 128, :], writes=[("fau", b)])
                    kb.dma("sp", ug[b][:, 1:N + 1], uT[(FBh + j) * 128:(FBh + j + 1) * 128, :], writes=[("fau", b)])
                    for (src, dst, fbk, dk) in ((uv[b], cv[b], j, ("cv", b)), (ug[b], cg[b], FBh + j, ("cg", b))):
                        fc = lambda tap: pcol("fc", tap * c.FB + fbk)
                        kb.op("act", lambda e: e.activation(out=dst[:], in_=src[:, 1:N + 1], func=AF.Identity,
                                                            bias=pcol("fb", fbk), scale=fc(1)),
                              reads=[("fau", b), "pvs"], writes=[dk])
                        kb.op("dve", lambda e: e.scalar_tensor_tensor(out=dst[:], in0=src[:, 0:N], scalar=fc(0),
                                                                     in1=dst[:], op0=ALU.mult, op1=ALU.add),
                              reads=[("fau", b), dk, "pvs"], writes=[dk])
                        kb.op("dve", lambda e: e.scalar_tensor_tensor(out=dst[:], in0=src[:, 2:N + 2], scalar=fc(2),
                                                                     in1=dst[:], op0=ALU.mult, op1=ALU.add),
                              reads=[("fau", b), dk, "pvs"], writes=[dk])
                        kb.op("dve", lambda e: e.tensor_scalar(out=tmp[:, 0:1], in0=src[:, HALF:HALF + 1], scalar1=fc(0),
                                                              scalar2=edge[:, 0:1], op0=ALU.mult, op1=ALU.mult),
                              reads=[("fau", b), "pvs", "edge"], writes=["fatmp"])
                        kb.op("dve", lambda e: e.tensor_tensor(out=dst[:, HALF:HALF + 1], in0=dst[:, HALF:HALF + 1],
                                                              in1=tmp[:, 0:1], op=ALU.add),
                              reads=["fatmp", dk], writes=[dk])
                        kb.op("dve", lambda e: e.tensor_scalar(out=tmp[:, 1:2], in0=src[:, HALF + 1:HALF + 2],
                                                              scalar1=fc(2), scalar2=edge[:, 0:1], op0=ALU.mult,
                                                              op1=ALU.mult),
                              reads=[("fau", b), "pvs", "edge"], writes=["fatmp"])
                        kb.op("dve", lambda e: e.tensor_tensor(out=dst[:, HALF - 1:HALF], in0=dst[:, HALF - 1:HALF],
                                                              in1=tmp[:, 1:2], op=ALU.add),
                              reads=["fatmp", dk], writes=[dk])
                    kb.op("act", lambda e: e.activation(out=cg[b][:], in_=cg[b][:], func=AF.Silu),
                          reads=[("cg", b)], writes=[("cg", b)])
                    kb.op("dve", lambda e: e.tensor_tensor(out=ob[b][:], in0=cg[b][:], in1=cv[b][:], op=ALU.mult),
                          reads=[("cg", b), ("cv", b)], writes=[("fao", b)])
                    kb.dma("sp", actT[:, :, j, :].rearrange("t p k -> p t k"),
                           ob[b][:].rearrange("p (t k) -> p t k", k=128), reads=[("fao", b)])
                kb.barrier()

        xcur = x_in
        for l in range(L):
            kb.dma("sp", pvs[:], pv[l], writes=["pvs"])
            kb.barrier()
            phase_norm(xcur, "g1")
            phase_inproj(l)
            phase_qkprep()
            phase_attn()
            phase_rwkvprep(l)
            phase_scan()
            phase_rwkvpost()
            phase_merge(l)
            gemm_tm(mixT, w_o[l], D, xcur, x1)
            phase_norm(x1, "g2")
            phase_ffnup(l)
            phase_ffnact()
            xnext = y_out if l == L - 1 else x2
            gemm_tm(actT, w_fd[l], c.DFF, x1, xnext)
            xcur = xnext
        kb.barrier()
    return nc


def _cols(v):
    v = np.asarray(v, np.float32).reshape(-1, 128)
    return np.ascontiguousarray(v.T)


def host_consts(cfg, packed):
    c = cfg
    N = c.N
    HALF = N // 2
    NKT = N // 128
    attb = np.zeros((128, NKT, 2), np.float32)
    if packed:
        for kt in range(NKT):
            kh = 0 if kt * 128 < HALF else 1
            attb[:, kt, 1 - kh] = -30000.0
    carry = np.ones((64, 2, c.NCH), np.float32)
    if packed:
        carry[:, 0, c.NCH // 2 - 1] = 0.0
        carry[:, 1, c.NCH // 2] = 0.0
    edge = np.full((128, 1), -1.0 if packed else 0.0, np.float32)
    T = HALF if packed else N
    pos = np.arange(N) % T
    row = (pos // 64).astype(np.float32)
    col = (pos % 64).astype(np.float32)
    inv = (10000.0 ** (-np.arange(0, 64, 2, dtype=np.float32) / 64)).astype(np.float32)
    rope = np.zeros((128, 2, N), np.float32)
    for p in range(128):
        ax, j = p // 64, p % 64
        half, f = j // 32, j % 32
        ang = (row if ax == 0 else col) * inv[f]
        rope[p, 0] = np.cos(ang.astype(np.float32))
        rope[p, 1] = (-np.sin(ang.astype(np.float32))) if half == 0 else np.sin(ang.astype(np.float32))
    mats = np.zeros((128, 4, 128), np.float32)
    mats[:, 0, :] = np.eye(128)
    mats[:, 1, :] = 1.0
    mats[0:64, 2, 0:64] = 1.0
    mats[64:128, 2, 64:128] = 1.0
    for p in range(128):
        j = p % 64
        partner = p + 32 if (j // 32) == 0 else p - 32
        mats[partner, 3, p] = 1.0
    s = np.arange(64)[:, None]
    t = np.arange(64)[None, :]
    tri = np.zeros((64, 5, 8, 64), np.float32)
    for i, m in enumerate((s < t, s <= t, s > t, s >= t, s == t)):
        tri[:, i, :, :] = m.astype(np.float32)[:, None, :]
    reset = np.ones((128, 512), np.float32)
    reset[:, 0::64] = 0.0
    return {"c_attb": attb.reshape(128, NKT * 2), "c_carry": carry.reshape(64, 2 * c.NCH), "c_edge": edge,
            "c_rope": rope, "c_mats": mats.astype(NP_BF16), "c_tri": tri.astype(NP_BF16), "c_reset": reset}


def host_pv(cfg, inp):
    c = cfg
    pv = np.zeros((c.L, 128, c.PC), np.float32)
    NB, FB = c.NB, c.FB
    for l in range(c.L):
        def put(nm, arr):
            pv[l, :, c.po[nm]:c.po[nm] + arr.shape[1]] = arr
        put("g1", _cols(inp["norm_mix"][l]))
        put("g2", _cols(inp["norm_ffn"][l]))
        put("qg", _cols(inp["q_gain"][l]))
        put("kg", _cols(inp["k_gain"][l]))
        cv = np.asarray(inp["rwkv_conv"][l], np.float32)
        put("conv", np.concatenate([_cols(cv[tap, i * c.RW:(i + 1) * c.RW]) for tap in range(3) for i in range(3)], 1))
        put("w0", np.concatenate([_cols(inp["decay_w0"][l][d]) for d in range(2)], 1))
        put("a0", np.concatenate([_cols(inp["iclr_a0"][l][d]) for d in range(2)], 1))
        put("kk", _cols(inp["k_k"][l]))
        put("ka", _cols(inp["k_a"][l]))
        put("rk", _cols(np.asarray(inp["r_k"][l]).reshape(-1)))
        put("lw", _cols(inp["lnx_w"][l]))
        put("lb", _cols(inp["lnx_b"][l]))
        fc = np.asarray(inp["ffn_conv"][l], np.float32)
        put("fc", np.concatenate([_cols(fc[tap]) for tap in range(3)], 1))
        put("fb", _cols(inp["ffn_conv_b"][l]))
    return pv


_NC_CACHE = {}


def run(cfg, inputs, debug_outs=()):
    c = cfg
    key = (c.D, c.SEQ, c.L, tuple(debug_outs))
    if key not in _NC_CACHE:
        _NC_CACHE[key] = build(c, debug_outs)
    nc = _NC_CACHE[key]
    xp = np.asarray(inputs["x_prompt"], np.float32)
    xs = np.asarray(inputs["x_sample"], np.float32)
    D = c.D
    groups = {}
    npk = c.BATCH // 2
    act_cores = [0, 1, 4, 5, 2, 3, 6, 7]
    gi = 0
    for g in range(npk):
        groups[act_cores[gi]] = ("p", g); gi += 1
    for g in range(c.DEC_BATCH):
        groups[act_cores[gi]] = ("s", g); gi += 1
    pvh = host_pv(c, inputs)
    wkeys = {"w_in": "w_in", "w_up_attn": "w_up_attn", "w_up_rwkv": "w_up_rwkv", "w_o": "w_o",
             "w_ffn_up": "w_ffn_up", "w_ffn_down": "w_ffn_down", "decay_w2": "decay_w2", "iclr_a2": "iclr_a2",
             "gate_g2": "gate_g2"}
    shared = {k: np.asarray(inputs[v], np.float32) for k, v in wkeys.items()}
    shared["pv"] = pvh
    cp = host_consts(c, True)
    cs = host_consts(c, False)
    in_maps = []
    zx = np.zeros((c.N, D), np.float32)
    for core in range(8):
        m = dict(shared)
        gk = groups.get(core)
        if gk is None:
            m["x"] = zx
            m.update(cs)
        elif gk[0] == "p":
            m["x"] = np.ascontiguousarray(xp[2 * gk[1]:2 * gk[1] + 2].reshape(c.N, D))
            m.update(cp)
        else:
            m["x"] = np.ascontiguousarray(xs[gk[1]])
            m.update(cs)
        in_maps.append(m)
    res = run_bass_kernel_spmd(nc, in_maps, core_ids=list(range(8)))
    yp = np.zeros((c.BATCH, c.SEQ, D), np.float32)
    ys = np.zeros((c.DEC_BATCH, 2 * c.SEQ, D), np.float32)
    for core, gk in groups.items():
        y = np.asarray(res.results[core]["y"])
        if gk[0] == "p":
            yp[2 * gk[1]:2 * gk[1] + 2] = y.reshape(2, c.SEQ, D)
        else:
            ys[gk[1]] = y
    return (yp, ys), res


def kernel(**inputs):
    cfg = Cfg()
    (yp, ys), _ = run(cfg, inputs)
    return (yp, ys)
```

```python
import math
import threading
from contextlib import ExitStack
import numpy as np
import ml_dtypes
import concourse.bass as bass
import concourse.mybir as mybir
from concourse.bass_utils import run_bass_kernel_spmd

F32 = mybir.dt.float32
BF16 = mybir.dt.bfloat16
AF = mybir.ActivationFunctionType
ALU = mybir.AluOpType
NP_BF16 = ml_dtypes.bfloat16


class Cfg:
    def __init__(s, D=4096, SEQ=2048, BATCH=4, DEC_BATCH=2, L=2):
        s.D = D; s.SEQ = SEQ; s.BATCH = BATCH; s.DEC_BATCH = DEC_BATCH; s.L = L
        s.N = 2 * SEQ
        s.AW = D // 2; s.QH = s.AW // 128; s.KVH = s.QH // 4; s.KVW = s.KVH * 128
        s.RW = D // 2; s.RH = s.RW // 64
        s.DL = 128; s.IL = 128; s.GL = 480
        s.DFF = 256 * (-(-(8 * D) // (3 * 256)))
        s.SPL = (s.AW, s.KVW, s.KVW, 3 * s.RW, 2 * s.DL, 2 * s.IL, s.GL, D, D)
        s.INW = sum(s.SPL)
        s.C = 64; s.NCH = s.N // 64
        s.NB = s.RW // 128
        s.FB = 2 * s.DFF // 128
        o = 0
        s.po = {}
        for nm, w in (("g1", D // 128), ("g2", D // 128), ("qg", 1), ("kg", 1), ("conv", 3 * 3 * s.NB),
                      ("w0", 2 * s.NB), ("a0", 2 * s.NB), ("kk", s.NB), ("ka", s.NB), ("rk", s.NB),
                      ("lw", s.NB), ("lb", s.NB), ("fc", 3 * s.FB), ("fb", s.FB)):
            s.po[nm] = o; o += w
        s.PC = o


class KB:
    R = 8

    def __init__(s, nc):
        s.nc = nc
        s.es = ExitStack()
        s.eng = {"pe": nc.tensor, "act": nc.scalar, "dve": nc.vector, "pool": nc.gpsimd, "sp": nc.sync}
        s.csem = {e: s.es.enter_context(nc.semaphore("c_" + e)) for e in ("pe", "act", "dve", "pool")}
        s.ccnt = {e: 0 for e in s.csem}
        s.ring = {q: [s.es.enter_context(nc.semaphore("d_%s%d" % (q, i))) for i in range(s.R)]
                  for q in ("sp", "pool", "act")}
        s.dcnt = {q: 0 for q in s.ring}
        s.seen = {e: {} for e in s.eng}
        s.lastw = {}
        s.rd_c = {}
        s.rd_d = {}
        s.pending = {e: [] for e in s.csem}
        s.ps_i = 0

    def _wait(s, e, t):
        if t["eng"] == "pe" and e == "pe":
            return
        assert t["val"] is not None, "unresolved ticket"
        k = id(t["sem"])
        if s.seen[e].get(k, 0) >= t["val"]:
            return
        s.eng[e].wait_ge(t["sem"], t["val"])
        s.seen[e][k] = t["val"]

    def _deps(s, e, reads, writes):
        for k in reads:
            t = s.lastw.get(k)
            if t is not None:
                s._wait(e, t)
        for k in writes:
            t = s.lastw.get(k)
            if t is not None:
                s._wait(e, t)
            for t in s.rd_c.get(k, {}).values():
                s._wait(e, t)
            for t in s.rd_d.get(k, ()):
                s._wait(e, t)

    def _record(s, tk, reads, writes, isdma):
        for k in reads:
            if isdma:
                s.rd_d.setdefault(k, []).append(tk)
            else:
                s.rd_c.setdefault(k, {})[tk["eng"]] = tk
        for k in writes:
            s.lastw[k] = tk
            s.rd_c[k] = {}
            s.rd_d[k] = []

    def op(s, e, fn, reads=(), writes=(), signal=True):
        s._deps(e, reads, writes)
        ins = fn(s.eng[e])
        tk = {"sem": s.csem[e], "val": None, "eng": e}
        if signal:
            s.ccnt[e] += 1
            ins.then_inc(s.csem[e], 1)
            tk["val"] = s.ccnt[e]
            for p in s.pending[e]:
                p["val"] = s.ccnt[e]
            s.pending[e] = []
        else:
            s.pending[e].append(tk)
        s._record(tk, reads, writes, False)
        return tk

    def dma(s, q, out, in_, reads=(), writes=()):
        if q == "sp" and not writes:
            q = "act"
        s._deps(q, reads, writes)
        j = s.dcnt[q]
        slot = j % s.R
        sem = s.ring[q][slot]
        if j >= s.R:
            s._wait(q, {"sem": sem, "val": 16 * (j // s.R), "eng": "dma"})
        s.eng[q].dma_start(out=out, in_=in_).then_inc(sem, 16)
        s.dcnt[q] += 1
        tk = {"sem": sem, "val": 16 * (j // s.R + 1), "eng": "dma"}
        s._record(tk, reads, writes, True)
        return tk

    def barrier(s):
        tks = []
        for e in s.csem:
            assert not s.pending[e], "pending unsignaled op at barrier on " + e
            if s.ccnt[e]:
                tks.append({"sem": s.csem[e], "val": s.ccnt[e], "eng": "x"})
        for q in s.ring:
            for i in range(s.R):
                if s.dcnt[q] > i:
                    uses = (s.dcnt[q] - i + s.R - 1) // s.R
                    tks.append({"sem": s.ring[q][i], "val": 16 * uses, "eng": "dma"})
        for e in s.eng:
            for t in tks:
                s._wait(e, t)
        s.lastw = {}
        s.rd_c = {}
        s.rd_d = {}


def build(cfg, debug_outs=()):
    c = cfg
    nc = bass.Bass("TRN2", target_bir_lowering=False)
    D, N, L = c.D, c.N, c.L
    NT = N // 512
    HALF = N // 2

    def din(name, shape, dt=F32):
        return nc.dram_tensor(name, list(shape), dt, kind="ExternalInput").ap()

    def dsc(name, shape, dt):
        kind = "ExternalOutput" if name in debug_outs else "Internal"
        return nc.dram_tensor(name, list(shape), dt, kind=kind).ap()

    x_in = din("x", [N, D])
    w_in = din("w_in", [L, D, c.INW])
    w_ua = din("w_up_attn", [L, c.AW, D])
    w_ur = din("w_up_rwkv", [L, c.RW, D])
    w_o = din("w_o", [L, D, D])
    w_fu = din("w_ffn_up", [L, D, 2 * c.DFF])
    w_fd = din("w_ffn_down", [L, c.DFF, D])
    dw2 = din("decay_w2", [L, 2, c.DL, c.RW])
    ia2 = din("iclr_a2", [L, 2, c.IL, c.RW])
    g2 = din("gate_g2", [L, c.GL, c.RW])
    pv = din("pv", [L, 128, c.PC])
    c_attb = din("c_attb", [128, (N // 128) * 2])
    c_carry = din("c_carry", [64, 2 * c.NCH])
    c_edge = din("c_edge", [128, 1])
    c_rope = din("c_rope", [128, 2, N])
    c_mats = din("c_mats", [128, 4, 128], BF16)
    c_tri = din("c_tri", [64, 5, 8, 64], BF16)
    c_reset = din("c_reset", [128, 512])
    y_out = nc.dram_tensor("y", [N, D], F32, kind="ExternalOutput").ap()

    hT = dsc("hT", [D, N], BF16)
    qraw = dsc("qraw", [c.AW + c.KVW, N], F32)
    vT = dsc("vT", [c.KVW, N], BF16)
    rkvraw = dsc("rkvraw", [3 * c.RW, N], F32)
    wlowT = dsc("wlowT", [2 * c.DL, N], BF16)
    alowT = dsc("alowT", [2 * c.IL, N], BF16)
    glowT = dsc("glowT", [c.GL, N], BF16)
    gA = dsc("gA", [D, N], BF16)
    gR = dsc("gR", [D, N], BF16)
    QKT = dsc("QKT", [c.AW + c.KVW, N], BF16)
    Vtm = dsc("Vtm", [N, c.KVW], BF16)
    attT = dsc("attT", [c.AW, N], BF16)
    sA = [dsc("sA%d" % d, [c.RW, N], BF16) for d in range(2)]
    sB = [dsc("sB%d" % d, [c.RW, N], BF16) for d in range(2)]
    sK = [dsc("sK%d" % d, [c.RW, N], BF16) for d in range(2)]
    sR = [dsc("sR%d" % d, [c.RW, N], BF16) for d in range(2)]
    sBh = [dsc("sBh%d" % d, [N, c.RW], BF16) for d in range(2)]
    sKh = [dsc("sKh%d" % d, [N, c.RW], BF16) for d in range(2)]
    sV = dsc("sV", [N, c.RW], BF16)
    sG = [dsc("sG%d" % d, [c.RW, c.NCH], F32) for d in range(2)]
    bonus = dsc("bonus", [c.RW, N], F32)
    gout = dsc("gout", [c.RW, N], F32)
    yT = [dsc("yT%d" % d, [c.RW, N], F32) for d in range(2)]
    rwT = dsc("rwT", [c.RW, N], BF16)
    mixT = dsc("mixT", [N // 128, 128, D // 128, 128], BF16)
    x1 = dsc("x1", [N, D], F32)
    x2 = dsc("x2", [N, D], F32)
    uT = dsc("uT", [2 * c.DFF, N], BF16)
    actT = dsc("actT", [N // 128, 128, c.DFF // 128, 128], BF16)

    _uid = [0]

    def SBT(name, shape, dt):
        _uid[0] += 1
        return nc.sbuf_tensor("%s_u%d" % (name, _uid[0]), shape, dt)

    kb = KB(nc)
    with kb.es:
        es0 = kb.es
        PS = [es0.enter_context(nc.psum_tensor("ps%d" % i, [128, 512], F32)) for i in range(6)]
        PSB = [es0.enter_context(nc.psum_tensor("psb%d" % i, [128, 1024], BF16)) for i in range(2)]
        mats = es0.enter_context(SBT("mats", [128, 4, 128], BF16))
        tri = es0.enter_context(SBT("tri", [64, 5, 8, 64], BF16))
        pvs = es0.enter_context(SBT("pvs", [128, c.PC], F32))
        edge = es0.enter_context(SBT("edge", [128, 1], F32))
        kb.dma("sp", mats[:], c_mats, writes=["mats"])
        kb.dma("sp", tri[:], c_tri, writes=["tri"])
        kb.dma("sp", edge[:], c_edge, writes=["edge"])
        ident = mats[:, 0, :]
        ones = mats[:, 1, :]
        blk = mats[:, 2, :]
        permT = mats[:, 3, :]
        ps_state = {"i": 0, "b": 0}

        def next_ps(lo=None, hi=None):
            lo = ps_state.get("lo", 0) if lo is None else lo
            hi = ps_state.get("hi", 6) if hi is None else hi
            i = lo + ps_state["i"] % (hi - lo)
            ps_state["i"] += 1
            return PS[i], ("ps", i)

        def next_psb():
            i = ps_state["b"] % 2
            ps_state["b"] += 1
            return PSB[i], ("psb", i)

        def rsqrt_to(dst, dkey, src, skey, scale, bias):
            kb.op("act", lambda e: e.activation(out=dst, in_=src, func=AF.Sqrt, bias=float(bias), scale=float(scale)),
                  reads=[skey], writes=[dkey])
            kb.op("dve", lambda e: e.reciprocal(out=dst, in_=dst), reads=[dkey], writes=[dkey])

        def pcol(name, j=0, n=1):
            o = c.po[name] + j
            return pvs[:, o:o + n]

        def phase_norm(xsrc, gname):
            with ExitStack() as es:
                KC = D // 128
                xt = [es.enter_context(SBT("n_xt%d" % i, [128, D], F32)) for i in range(2)]
                xb = [es.enter_context(SBT("n_xb%d" % i, [128, D], BF16)) for i in range(2)]
                junk = es.enter_context(SBT("n_junk", [128, D], BF16))
                st = es.enter_context(SBT("n_st", [128, 8], F32))
                hb = [es.enter_context(SBT("n_hb%d" % i, [128, KC, 512], BF16)) for i in range(2)]
                for ti in range(N // 128):
                    b = ti % 2
                    g = ti // 4
                    hbb = hb[g % 2]
                    kb.dma("sp", xt[b][:], xsrc[ti * 128:(ti + 1) * 128, :], writes=[("xt", b)])
                    sc = st[:, b * 4:b * 4 + 1]
                    rs = st[:, b * 4 + 1:b * 4 + 2]
                    kb.op("act", lambda e: e.activation(out=junk[:], in_=xt[b][:], func=AF.Square, accum_out=sc),
                          reads=[("xt", b)], writes=["junk", ("ss", b)])
                    rsqrt_to(rs, ("rs", b), sc, ("ss", b), 1.0 / D, 1e-6)
                    kb.op("act", lambda e: e.activation(out=xb[b][:], in_=xt[b][:], func=AF.Copy, scale=rs),
                          reads=[("xt", b), ("rs", b)], writes=[("xb", b)])
                    for q in range(KC // 8):
                        pt, pk = next_psb()
                        for j in range(8):
                            cc = q * 8 + j
                            kb.op("pe", lambda e: e.transpose(pt[:, j * 128:(j + 1) * 128],
                                                              xb[b][:, cc * 128:(cc + 1) * 128], ident),
                                  reads=[("xb", b), "mats"], writes=[pk], signal=(j == 7))
                        for j in range(8):
                            cc = q * 8 + j
                            eng = "act" if j % 2 == 0 else "dve"
                            dst = hbb[:, cc, (ti % 4) * 128:(ti % 4 + 1) * 128]
                            src = pt[:, j * 128:(j + 1) * 128]
                            gcol = pcol(gname, cc)
                            if eng == "act":
                                kb.op("act", lambda e: e.activation(out=dst, in_=src, func=AF.Copy, scale=gcol),
                                      reads=[pk, "pvs"], writes=[("hb", g % 2)])
                            else:
                                kb.op("dve", lambda e: e.tensor_scalar(out=dst, in0=src, scalar1=gcol, scalar2=None,
                                                                      op0=ALU.mult),
                                      reads=[pk, "pvs"], writes=[("hb", g % 2)])
                    if ti % 4 == 3:
                        kb.dma("sp", hT.rearrange("(c p) n -> p c n", p=128)[:, :, g * 512:(g + 1) * 512], hbb[:],
                               reads=[("hb", g % 2)])
                kb.barrier()

        def gemm_fm(pairs, groups, TT, epi):
            with ExitStack() as es:
                KCs = [(K + 127) // 128 for (_, _, K) in pairs]
                Xs = [es.enter_context(SBT("g_x%d" % i, [128, KCs[i], TT], BF16))
                      for i in range(len(pairs))]
                Ws = [[es.enter_context(SBT("g_w%d_%d" % (i, b), [128, KCs[i], 512], BF16))
                       for b in range(2)] for i in range(len(pairs))]
                wi = 0
                for st in range(N // TT):
                    for i, (X, W, K) in enumerate(pairs):
                        for kc in range(KCs[i]):
                            r = min(128, K - kc * 128)
                            kb.dma("sp", Xs[i][0:r, kc, :], X[kc * 128:kc * 128 + r, st * TT:(st + 1) * TT],
                                   writes=[("gx", i)])
                    for (c0, gw, blocks) in groups:
                        wb = wi % 2
                        wi += 1
                        for i, (X, W, K) in enumerate(pairs):
                            if K % 128 == 0:
                                kb.dma("pool", Ws[i][wb][:, :, 0:gw],
                                       W.rearrange("(c p) m -> p c m", p=128)[:, :, c0:c0 + gw],
                                       writes=[("gw", i, wb)])
                            else:
                                for kc in range(KCs[i]):
                                    r = min(128, K - kc * 128)
                                    kb.dma("pool", Ws[i][wb][0:r, kc, 0:gw], W[kc * 128:kc * 128 + r, c0:c0 + gw],
                                           writes=[("gw", i, wb)])
                        for (off, w, tag) in blocks:
                            for tg in range(TT // 512):
                                psl, pkl = [], []
                                for i, (X, W, K) in enumerate(pairs):
                                    pt, pk = next_ps()
                                    psl.append(pt[0:w, :])
                                    pkl.append(pk)
                                    for kc in range(KCs[i]):
                                        r = min(128, K - kc * 128)
                                        kb.op("pe", lambda e: e.matmul(
                                            pt[0:w, :], lhsT=Ws[i][wb][0:r, kc, off:off + w],
                                            rhs=Xs[i][0:r, kc, tg * 512:(tg + 1) * 512],
                                            start=(kc == 0), stop=(kc == KCs[i] - 1)),
                                            reads=[("gw", i, wb), ("gx", i)], writes=[pk],
                                            signal=(kc == KCs[i] - 1))
                                epi(psl, pkl, tag, c0 + off, w, st * TT + tg * 512)
                kb.barrier()

        def gemm_tm(X, W, K, xres, xdst):
            with ExitStack() as es:
                KC = (K + 127) // 128
                nwb = 2 if KC <= 32 else 1
                Wb = [es.enter_context(SBT("t_w%d" % b, [128, KC, 512], BF16)) for b in range(nwb)]
                Xb = [es.enter_context(SBT("t_x%d" % b, [128, KC, 128], BF16)) for b in range(3)]
                Rb = [es.enter_context(SBT("t_r%d" % b, [128, 512], F32)) for b in range(3)]
                xi = 0
                for cb in range(D // 512):
                    wb = cb % nwb
                    for kc0 in range(0, KC, 16):
                        kc1 = min(KC, kc0 + 16)
                        kb.dma("pool", Wb[wb][:, kc0:kc1, :],
                               W.rearrange("(c p) m -> p c m", p=128)[:, kc0:kc1, cb * 512:(cb + 1) * 512],
                               writes=[("tw", wb)])
                    for tt in range(N // 128):
                        b = xi % 3
                        xi += 1
                        kb.dma("sp", Xb[b][:], X[tt], writes=[("tx", b)])
                        kb.dma("sp", Rb[b][:], xres[tt * 128:(tt + 1) * 128, cb * 512:(cb + 1) * 512],
                               writes=[("tr", b)])
                        pt, pk = next_ps()
                        for kc in range(KC):
                            kb.op("pe", lambda e: e.matmul(pt[:, :], lhsT=Xb[b][:, kc, :], rhs=Wb[wb][:, kc, :],
                                                           start=(kc == 0), stop=(kc == KC - 1)),
                                  reads=[("tx", b), ("tw", wb)], writes=[pk], signal=(kc == KC - 1))
                        kb.op("dve", lambda e: e.tensor_tensor(out=Rb[b][:], in0=pt[:, :], in1=Rb[b][:], op=ALU.add),
                              reads=[pk, ("tr", b)], writes=[("tr", b)])
                        kb.dma("sp", xdst[tt * 128:(tt + 1) * 128, cb * 512:(cb + 1) * 512], Rb[b][:],
                               reads=[("tr", b)])
                kb.barrier()

        class Evac:
            def __init__(s, es, name, dt, nbuf=3):
                s.bufs = [es.enter_context(SBT("%s%d" % (name, i), [128, 512], dt)) for i in range(nbuf)]
                s.i = 0
                s.name = name

            def get(s):
                b = s.i % len(s.bufs)
                s.i += 1
                return s.bufs[b], (s.name, b)

        def phase_inproj(l):
            blocks = []
            o = 0
            segs = [("q", c.AW + c.KVW), ("v", c.KVW), ("rkv", 3 * c.RW), ("wl", 2 * c.DL), ("al", 2 * c.IL),
                    ("gl", c.GL), ("ga", D), ("gr", D)]
            for tag, wd in segs:
                so = 0
                while so < wd:
                    w = min(128, wd - so)
                    blocks.append((o + so, w, (tag, so)))
                    so += w
                o += wd
            groups = []
            cur = None
            for (a, w, tag) in blocks:
                if cur is None or (a + w - cur[0]) > 512:
                    cur = [a, 0, []]
                    groups.append(cur)
                cur[2].append((a - cur[0], w, tag))
                cur[1] = a + w - cur[0]
            with ExitStack() as es:
                ev32 = Evac(es, "ip_f", F32)
                ev16 = Evac(es, "ip_b", BF16)
                dst = {"q": (qraw, F32, AF.Copy), "v": (vT, BF16, AF.Copy), "rkv": (rkvraw, F32, AF.Copy),
                       "wl": (wlowT, BF16, AF.Tanh), "al": (alowT, BF16, AF.Copy), "gl": (glowT, BF16, AF.Sigmoid),
                       "ga": (gA, BF16, AF.Sigmoid), "gr": (gR, BF16, AF.Sigmoid)}
                cnt = [0]

                def epi(psl, pkl, tag, offabs, w, tok0):
                    dten, dt, fn = dst[tag[0]]
                    buf, bk = (ev32 if dt == F32 else ev16).get()
                    cnt[0] += 1
                    if fn == AF.Copy and cnt[0] % 2 == 0:
                        kb.op("dve", lambda e: e.tensor_copy(out=buf[0:w, :], in_=psl[0]),
                              reads=[pkl[0]], writes=[bk])
                    else:
                        kb.op("act", lambda e: e.activation(out=buf[0:w, :], in_=psl[0], func=fn),
                              reads=[pkl[0]], writes=[bk])
                    kb.dma("sp", dten[tag[1]:tag[1] + w, tok0:tok0 + 512], buf[0:w, :], reads=[bk])

                gemm_fm([(hT, w_in[l], D)], [tuple(g) for g in groups], min(N, 1024), epi)

        def phase_qkprep():
            with ExitStack() as es:
                rope = es.enter_context(SBT("rope", [128, 2, N], F32))
                kb.dma("sp", rope[:], c_rope, writes=["rope"])
                nb = 2
                T = lambda nm, dt: [es.enter_context(SBT("%s%d" % (nm, i), [128, 512], dt))
                                    for i in range(nb)]
                raw, rg, sq, xbf = T("q_raw", F32), T("q_rg", F32), T("q_sq", BF16), T("q_xb", BF16)
                rstd, t1, t2, ob = T("q_rs", F32), T("q_t1", F32), T("q_t2", F32), T("q_ob", BF16)
                it = 0
                for hd in range(c.QH + c.KVH):
                    gcol = pcol("qg") if hd < c.QH else pcol("kg")
                    for tg in range(NT):
                        b = it % nb
                        it += 1
                        ts = slice(tg * 512, (tg + 1) * 512)
                        kb.dma("sp", raw[b][:], qraw[hd * 128:(hd + 1) * 128, ts], writes=[("raw", b)])
                        kb.op("act", lambda e: e.activation(out=sq[b][:], in_=raw[b][:], func=AF.Square),
                              reads=[("raw", b)], writes=[("sq", b)])
                        kb.op("act", lambda e: e.activation(out=rg[b][:], in_=raw[b][:], func=AF.Copy, scale=gcol),
                              reads=[("raw", b), "pvs"], writes=[("rg", b)])
                        kb.op("dve", lambda e: e.tensor_copy(out=xbf[b][:], in_=rg[b][:]),
                              reads=[("rg", b)], writes=[("xbf", b)])
                        p1, k1 = next_ps()
                        kb.op("pe", lambda e: e.matmul(p1[:, :], lhsT=ones, rhs=sq[b][:], start=True, stop=True),
                              reads=[("sq", b), "mats"], writes=[k1])
                        p2, k2 = next_ps()
                        kb.op("pe", lambda e: e.matmul(p2[:, :], lhsT=permT, rhs=xbf[b][:], start=True, stop=True),
                              reads=[("xbf", b), "mats"], writes=[k2])
                        rsqrt_to(rstd[b][:], ("rstd", b), p1[:, :], k1, 1.0 / 128, 1e-6)
                        kb.op("dve", lambda e: e.tensor_tensor(out=t1[b][:], in0=rg[b][:], in1=rope[:, 0, ts],
                                                              op=ALU.mult),
                              reads=[("rg", b), "rope"], writes=[("t1", b)])
                        kb.op("dve", lambda e: e.tensor_tensor(out=t2[b][:], in0=p2[:, :], in1=rope[:, 1, ts],
                                                              op=ALU.mult),
                              reads=[k2, "rope"], writes=[("t2", b)])
                        kb.op("dve", lambda e: e.tensor_tensor(out=t1[b][:], in0=t1[b][:], in1=t2[b][:], op=ALU.add),
                              reads=[("t1", b), ("t2", b)], writes=[("t1", b)])
                        kb.op("dve", lambda e: e.tensor_tensor(out=ob[b][:], in0=t1[b][:], in1=rstd[b][:],
                                                              op=ALU.mult),
                              reads=[("t1", b), ("rstd", b)], writes=[("ob", b)])
                        kb.dma("sp", QKT[hd * 128:(hd + 1) * 128, ts], ob[b][:], reads=[("ob", b)])
                vin = T("q_vin", BF16)
                vo = [es.enter_context(SBT("q_vo%d" % i, [128, 4, 128], BF16)) for i in range(2)]
                it = 0
                for h in range(c.KVH):
                    for tg in range(NT):
                        b = it % 2
                        it += 1
                        kb.dma("sp", vin[b][:], vT[h * 128:(h + 1) * 128, tg * 512:(tg + 1) * 512],
                               writes=[("vin", b)])
                        pt, pk = next_psb()
                        for j in range(4):
                            kb.op("pe", lambda e: e.transpose(pt[:, j * 128:(j + 1) * 128],
                                                              vin[b][:, j * 128:(j + 1) * 128], ident),
                                  reads=[("vin", b), "mats"], writes=[pk], signal=(j == 3))
                        kb.op("dve", lambda e: e.tensor_copy(out=vo[b][:].rearrange("p a b -> p (a b)"),
                                                            in_=pt[:, 0:512]),
                              reads=[pk], writes=[("vo", b)])
                        kb.dma("sp", Vtm.rearrange("(a p) f -> p a f", p=128)[:, tg * 4:(tg + 1) * 4,
                                                                             h * 128:(h + 1) * 128],
                               vo[b][:], reads=[("vo", b)])
                kb.barrier()

        def phase_attn():
            with ExitStack() as es:
                NKT = N // 128
                attb = es.enter_context(SBT("attb", [128, NKT * 2], F32))
                kb.dma("sp", attb[:], c_attb, writes=["attb"])
                Kt = es.enter_context(SBT("a_K", [128, N], BF16))
                Vt = es.enter_context(SBT("a_V", [128, NKT, 128], BF16))
                Qt = [es.enter_context(SBT("a_Q%d" % i, [128, N], BF16)) for i in range(2)]
                Pb = [es.enter_context(SBT("a_P%d" % i, [128, 512], BF16)) for i in range(4)]
                rz = [es.enter_context(SBT("a_rz%d" % i, [128, 512], F32)) for i in range(2)]
                ob = [es.enter_context(SBT("a_o%d" % i, [128, 512], BF16)) for i in range(2)]
                scale = 128.0 ** -0.5
                qi = 0
                pi = 0
                oi = 0
                for h in range(c.KVH):
                    kb.dma("sp", Kt[:], QKT[c.AW + h * 128:c.AW + (h + 1) * 128, :], writes=["aK"])
                    kb.dma("sp", Vt[:], Vtm.rearrange("(a p) f -> p a f", p=128)[:, :, h * 128:(h + 1) * 128],
                           writes=["aV"])
                    for g in range(4):
                        qh = h * 4 + g
                        qb = qi % 2
                        qi += 1
                        kb.dma("sp", Qt[qb][:], QKT[qh * 128:(qh + 1) * 128, :], writes=[("aQ", qb)])
                        for qt in range(NT):
                            a = oi % 2
                            oi += 1
                            po, ko = PS[0], ("ps", 0)
                            pz, kz = PS[1], ("ps", 1)
                            qhalf = 0 if (qt * 512) < HALF else 1
                            sts = {}

                            def issue_S(kt):
                                pst, kst = next_ps(2, 6)
                                kb.op("pe", lambda e: e.matmul(pst[:, :], lhsT=Kt[:, kt * 128:(kt + 1) * 128],
                                                               rhs=Qt[qb][:, qt * 512:(qt + 1) * 512],
                                                               start=True, stop=True),
                                      reads=["aK", ("aQ", qb)], writes=[kst])
                                sts[kt] = (pst, kst)

                            issue_S(0)
                            if NKT > 1:
                                issue_S(1)
                            for kt in range(NKT):
                                if kt + 2 < NKT:
                                    issue_S(kt + 2)
                                pst, kst = sts.pop(kt)
                                pb = pi % 4
                                pi += 1
                                kb.op("act", lambda e: e.activation(out=Pb[pb][:], in_=pst[:, :], func=AF.Exp,
                                                                    bias=attb[:, kt * 2 + qhalf:kt * 2 + qhalf + 1],
                                                                    scale=scale),
                                      reads=[kst, "attb"], writes=[("aP", pb)])
                                kb.op("pe", lambda e: e.matmul(po[:, :], lhsT=Vt[:, kt, :], rhs=Pb[pb][:],
                                                               start=(kt == 0), stop=(kt == NKT - 1)),
                                      reads=["aV", ("aP", pb)], writes=[ko], signal=False)
                                kb.op("pe", lambda e: e.matmul(pz[:, :], lhsT=ones, rhs=Pb[pb][:],
                                                               start=(kt == 0), stop=(kt == NKT - 1)),
                                      reads=["mats", ("aP", pb)], writes=[kz], signal=True)
                            kb.op("dve", lambda e: e.reciprocal(out=rz[a][:], in_=pz[:, :]),
                                  reads=[kz], writes=[("rz", a)])
                            kb.op("dve", lambda e: e.tensor_tensor(out=ob[a][:], in0=po[:, :], in1=rz[a][:],
                                                                  op=ALU.mult),
                                  reads=[ko, ("rz", a)], writes=[("ao", a)])
                            kb.dma("sp", attT[qh * 128:(qh + 1) * 128, qt * 512:(qt + 1) * 512], ob[a][:],
                                   reads=[("ao", a)])
                kb.barrier()

        def phase_rwkvprep(l):
            with ExitStack() as es:
                NB = c.NB
                ps_state["lo"], ps_state["hi"] = 0, 4
                rst = es.enter_context(SBT("rp_rst", [128, 512], F32))
                kb.dma("sp", rst[:], c_reset, writes=["rst"])
                w2s = es.enter_context(SBT("rp_w2", [128, 2, c.RW], BF16))
                a2s = es.enter_context(SBT("rp_a2", [128, 2, c.RW], BF16))
                g2s = es.enter_context(SBT("rp_g2", [128, 4, c.RW], BF16))
                for d in range(2):
                    kb.dma("pool", w2s[:, d, :], dw2[l, d], writes=["w2s"])
                    kb.dma("pool", a2s[:, d, :], ia2[l, d], writes=["a2s"])
                for kc in range(4):
                    r = min(128, c.GL - kc * 128)
                    kb.dma("pool", g2s[0:r, kc, :], g2[l, kc * 128:kc * 128 + r, :], writes=["g2s"])
                names32 = ["rr", "kr", "vr", "r", "k", "v", "kku", "kk", "ag", "lw", "cum", "exc", "kd", "bb", "ee",
                           "tmp", "tmp2", "gg", "bon"]
                S2 = [{nm: es.enter_context(SBT("rp_" + nm, [128, 514 if nm in ("rr", "kr", "vr") else 512],
                                                          F32)) for nm in names32} for _ in range(2)]
                names16 = ["sqb", "rkb", "oA", "oB", "oK", "oR", "oBh", "oKh", "vb"]
                Sb2 = [{nm: es.enter_context(SBT("rp_" + nm, [128, 512], BF16)) for nm in names16} for _ in range(2)]
                Lw = {nm: es.enter_context(SBT("rp_" + nm, [128, 512], BF16)) for nm in ("wl0", "wl1", "al0", "al1")}
                dbl = set(names32) | set(names16) | {"tot"}
                tl = threading.local()
                baton = {"ev": None, "alive": None}

                def km(keys):
                    return [((k, tl.par) if (isinstance(k, str) and k in dbl) else k) for k in keys]

                def switch():
                    ev, alive = baton["ev"], baton["alive"]
                    i = tl.tid
                    j = 1 - i
                    if ev is None or not alive[j]:
                        return
                    ev[i].clear()
                    ev[j].set()
                    ev[i].wait()
                glb = es.enter_context(SBT("rp_gl", [128, 4, 512], BF16))
                tot = es.enter_context(SBT("rp_tot", [128, 64], F32))
                otm = [es.enter_context(SBT("rp_otm%d" % i, [128, 4, 128], BF16)) for i in range(3)]
                nchk = 512 // 64

                def V(eng, fn, r, w, signal=True):
                    kb.op(eng, fn, reads=km(r), writes=km(w), signal=signal)
                    if signal:
                        switch()

                def Dm(out, in_, r=(), w=()):
                    kb.dma("sp", out, in_, reads=km(r), writes=km(w))

                for tg in range(NT):
                    t0 = tg * 512
                    ts = slice(t0, t0 + 512)
                    for d in range(2):
                        kb.dma("sp", Lw["wl%d" % d][:], wlowT[d * c.DL:(d + 1) * c.DL, ts], writes=["wl%d" % d])
                        kb.dma("sp", Lw["al%d" % d][:], alowT[d * c.IL:(d + 1) * c.IL, ts], writes=["al%d" % d])
                    for kc in range(4):
                        r = min(128, c.GL - kc * 128)
                        kb.dma("sp", glb[0:r, kc, :], glowT[kc * 128:kc * 128 + r, ts], writes=["glb"])
                    def cb_body(cb, tid):
                        tl.tid = tid
                        cs = slice(cb * 128, (cb + 1) * 128)
                        par = cb % 2
                        tl.par = par
                        S = S2[par]
                        Sb = Sb2[par]
                        for i, nm in enumerate(("rr", "kr", "vr")):
                            lo = max(t0 - 1, 0)
                            hi = min(t0 + 513, N)
                            Dm(S[nm][:, (lo - (t0 - 1)):(hi - (t0 - 1))],
                                   rkvraw[i * c.RW + cb * 128:i * c.RW + (cb + 1) * 128, lo:hi], w=[nm])
                            if lo == 0 and t0 == 0:
                                V("dve", lambda e: e.memset(S[nm][:, 0:1], 0.0), [], [nm])
                            if hi == N and t0 + 512 == N:
                                V("dve", lambda e: e.memset(S[nm][:, 513:514], 0.0), [], [nm])
                            on = ("r", "k", "v")[i]
                            cw = lambda tap: pcol("conv", (tap * 3 + i) * NB + cb)
                            src = S[nm]
                            V("dve", lambda e: e.tensor_scalar(out=S[on][:], in0=src[:, 1:513], scalar1=cw(1),
                                                               scalar2=None, op0=ALU.mult), [nm, "pvs"], [on])
                            V("dve", lambda e: e.scalar_tensor_tensor(out=S[on][:], in0=src[:, 0:512], scalar=cw(0),
                                                                      in1=S[on][:], op0=ALU.mult, op1=ALU.add),
                              [nm, on, "pvs"], [on])
                            V("dve", lambda e: e.scalar_tensor_tensor(out=S[on][:], in0=src[:, 2:514], scalar=cw(2),
                                                                      in1=S[on][:], op0=ALU.mult, op1=ALU.add),
                              [nm, on, "pvs"], [on])
                            if t0 == HALF:
                                V("dve", lambda e: e.tensor_scalar(out=S["tmp"][:, 0:1], in0=src[:, 0:1],
                                                                   scalar1=cw(0), scalar2=edge[:, 0:1],
                                                                   op0=ALU.mult, op1=ALU.mult),
                                  [nm, "pvs", "edge"], ["tmp"])
                                V("dve", lambda e: e.tensor_tensor(out=S[on][:, 0:1], in0=S[on][:, 0:1],
                                                                   in1=S["tmp"][:, 0:1], op=ALU.add),
                                  ["tmp", on], [on])
                            if t0 + 512 == HALF:
                                V("dve", lambda e: e.tensor_scalar(out=S["tmp"][:, 0:1], in0=src[:, 513:514],
                                                                   scalar1=cw(2), scalar2=edge[:, 0:1],
                                                                   op0=ALU.mult, op1=ALU.mult),
                                  [nm, "pvs", "edge"], ["tmp"])
                                V("dve", lambda e: e.tensor_tensor(out=S[on][:, 511:512], in0=S[on][:, 511:512],
                                                                   in1=S["tmp"][:, 0:1], op=ALU.add),
                                  ["tmp", on], [on])
                        V("dve", lambda e: e.tensor_scalar(out=S["kku"][:], in0=S["k"][:], scalar1=pcol("kk", cb),
                                                           scalar2=None, op0=ALU.mult), ["k", "pvs"], ["kku"])
                        V("act", lambda e: e.activation(out=Sb["sqb"][:], in_=S["kku"][:], func=AF.Square),
                          ["kku"], ["sqb"])
                        p1, k1 = next_ps()
                        V("pe", lambda e: e.matmul(p1[:, :], lhsT=blk, rhs=Sb["sqb"][:], start=True, stop=True),
                              r=["sqb", "mats"], w=[k1])
                        rsqrt_to(S["tmp"][:], "tmp", p1[:, :], k1, 1.0, 1e-24)
                        V("dve", lambda e: e.tensor_tensor(out=S["kk"][:], in0=S["kku"][:], in1=S["tmp"][:],
                                                           op=ALU.mult), ["kku", "tmp"], ["kk"])
                        V("act", lambda e: e.activation(out=Sb["vb"][:], in_=S["v"][:], func=AF.Copy), ["v"], ["vb"])
                        pg, kg_ = next_ps()
                        for kc in range(4):
                            r = min(128, c.GL - kc * 128)
                            V("pe", lambda e: e.matmul(pg[:, :], lhsT=g2s[0:r, kc, cs], rhs=glb[0:r, kc, :],
                                                           start=(kc == 0), stop=(kc == 3)),
                                  r=["g2s", "glb"], w=[kg_], signal=(kc == 3))
                        V("act", lambda e: e.activation(out=S["gg"][:], in_=pg[:, :], func=AF.Copy), [kg_], ["gg"])
                        Dm(gout[cs, ts], S["gg"][:], r=["gg"])
                        pbn, kbn = PS[4 + par], ("ps", 4 + par)
                        for d in range(2):
                            pw, kw = next_ps()
                            V("pe", lambda e: e.matmul(pw[:, :], lhsT=w2s[:, d, cs], rhs=Lw["wl%d" % d][:],
                                                           start=True, stop=True),
                                  r=["w2s", "wl%d" % d], w=[kw])
                            pa, ka = next_ps()
                            V("pe", lambda e: e.matmul(pa[:, :], lhsT=a2s[:, d, cs], rhs=Lw["al%d" % d][:],
                                                           start=True, stop=True),
                                  r=["a2s", "al%d" % d], w=[ka])
                            V("act", lambda e: e.activation(out=S["lw"][:], in_=pw[:, :], func=AF.Sigmoid,
                                                            bias=pcol("w0", d * NB + cb), scale=1.0),
                              [kw, "pvs"], ["lw"])
                            V("act", lambda e: e.activation(out=S["ag"][:], in_=pa[:, :], func=AF.Sigmoid,
                                                            bias=pcol("a0", d * NB + cb), scale=1.0),
                              [ka, "pvs"], ["ag"])
                            V("dve", lambda e: e.tensor_scalar(out=S["lw"][:], in0=S["lw"][:],
                                                               scalar1=-math.exp(-0.5), scalar2=None, op0=ALU.mult),
                              ["lw"], ["lw"])
                            V("dve", lambda e: e.tensor_tensor_scan(out=S["cum"][:], data0=rst[:], data1=S["lw"][:],
                                                                    initial=0.0, op0=ALU.mult, op1=ALU.add),
                              ["lw", "rst"], ["cum"])
                            V("dve", lambda e: e.tensor_tensor(out=S["exc"][:], in0=S["cum"][:], in1=S["lw"][:],
                                                               op=ALU.subtract), ["cum", "lw"], ["exc"])
                            cum3 = S["cum"][:].rearrange("p (a b) -> p a b", b=64)
                            to = par * 32 + d * 16
                            V("dve", lambda e: e.tensor_copy(out=tot[:, to:to + nchk], in_=cum3[:, :, 63]),
                              ["cum"], ["tot"])
                            V("dve", lambda e: e.tensor_scalar(out=tot[:, to + 8:to + 8 + nchk], in0=tot[:, to:to + nchk],
                                                               scalar1=-1.0, scalar2=None, op0=ALU.mult),
                              ["tot"], ["tot"])
                            V("act", lambda e: e.activation(out=S["tmp"][:, 0:nchk], in_=tot[:, to:to + nchk],
                                                            func=AF.Exp), ["tot"], ["tmp"])
                            Dm(sG[d][cs, tg * nchk:(tg + 1) * nchk], S["tmp"][:, 0:nchk], r=["tmp"])
                            V("dve", lambda e: e.tensor_scalar(out=S["kd"][:], in0=S["ag"][:], scalar1=-1.0,
                                                               scalar2=pcol("ka", cb), op0=ALU.add, op1=ALU.mult),
                              ["ag", "pvs"], ["kd"])
                            V("dve", lambda e: e.scalar_tensor_tensor(out=S["kd"][:], in0=S["kd"][:], scalar=1.0,
                                                                      in1=S["k"][:], op0=ALU.add, op1=ALU.mult),
                              ["kd", "k"], ["kd"])
                            V("dve", lambda e: e.tensor_tensor(out=S["bb"][:], in0=S["ag"][:], in1=S["kk"][:],
                                                               op=ALU.mult), ["ag", "kk"], ["bb"])
                            V("dve", lambda e: e.scalar_tensor_tensor(out=Sb["rkb"][:], in0=S["r"][:],
                                                                      scalar=pcol("rk", cb), in1=S["kd"][:],
                                                                      op0=ALU.mult, op1=ALU.mult),
                              ["r", "kd", "pvs"], ["rkb"])
                            V("pe", lambda e: e.matmul(pbn[:, :], lhsT=blk, rhs=Sb["rkb"][:], start=(d == 0),
                                                           stop=(d == 1)),
                                  r=["rkb", "mats"], w=[kbn], signal=True)
                            def expo(srcn, sgn, sb):
                                if sb == 0:
                                    V("act", lambda e: e.activation(out=S["ee"][:], in_=S[srcn][:], func=AF.Exp,
                                                                    scale=float(sgn)), [srcn], ["ee"])
                                else:
                                    base = to if sb > 0 else to + 8
                                    for ch in range(nchk):
                                        V("act", lambda e: e.activation(out=S["ee"][:, ch * 64:(ch + 1) * 64],
                                                                        in_=S[srcn][:, ch * 64:(ch + 1) * 64],
                                                                        func=AF.Identity,
                                                                        bias=tot[:, base + ch:base + ch + 1],
                                                                        scale=float(sgn)), [srcn, "tot"], ["ee"])
                                    V("act", lambda e: e.activation(out=S["ee"][:], in_=S["ee"][:], func=AF.Exp),
                                      ["ee"], ["ee"])

                            def prod(onm, anm, neg=False):
                                if neg:
                                    V("dve", lambda e: e.scalar_tensor_tensor(out=Sb[onm][:], in0=S[anm][:],
                                                                              scalar=-1.0, in1=S["ee"][:],
                                                                              op0=ALU.mult, op1=ALU.mult),
                                      [anm, "ee"], [onm])
                                else:
                                    V("dve", lambda e: e.tensor_tensor(out=Sb[onm][:], in0=S[anm][:], in1=S["ee"][:],
                                                                       op=ALU.mult), [anm, "ee"], [onm])

                            if d == 0:
                                specs = [("cum", 1, 0, [("oR", "r", False)]), ("exc", 1, 0, [("oA", "kk", True)]),
                                         ("cum", -1, 0, [("oB", "bb", False), ("oK", "kd", False)]),
                                         ("cum", -1, 1, [("oBh", "bb", False), ("oKh", "kd", False)])]
                            else:
                                specs = [("exc", -1, 1, [("oR", "r", False)]), ("cum", -1, 1, [("oA", "kk", True)]),
                                         ("exc", 1, -1, [("oB", "bb", False), ("oK", "kd", False)]),
                                         ("exc", 1, 0, [("oBh", "bb", False), ("oKh", "kd", False)])]
                            for (srcn, sgn, sb, outs) in specs:
                                expo(srcn, sgn, sb)
                                for (onm, anm, neg) in outs:
                                    prod(onm, anm, neg)
                            for onm, dten in (("oA", sA[d]), ("oB", sB[d]), ("oK", sK[d]), ("oR", sR[d])):
                                Dm(dten[cs, ts], Sb[onm][:], r=[onm])
                            for onm, dten in (("oBh", sBh[d]), ("oKh", sKh[d])) + ((("vb", sV),) if d == 0 else ()):
                                pt, pk = next_psb()
                                for j in range(4):
                                    V("pe", lambda e: e.transpose(pt[:, j * 128:(j + 1) * 128],
                                                                      Sb[onm][:, j * 128:(j + 1) * 128], ident),
                                          r=[onm, "mats"], w=[pk], signal=(j == 3))
                                ob_i = ps_state.setdefault("otm", 0) % 3
                                ps_state["otm"] += 1
                                V("act", lambda e: e.activation(out=otm[ob_i][:].rearrange("p a b -> p (a b)"),
                                                                in_=pt[:, 0:512], func=AF.Copy),
                                  [pk], [("otm", ob_i)])
                                Dm(dten.rearrange("(a p) f -> p a f", p=128)[:, tg * 4:(tg + 1) * 4, cs],
                                       otm[ob_i][:], r=[("otm", ob_i)])
                        V("dve", lambda e: e.tensor_tensor(out=S["bon"][:], in0=pbn[:, :], in1=S["v"][:],
                                                           op=ALU.mult), [kbn, "v"], ["bon"])
                        V("dve", lambda e: e.tensor_scalar(out=S["bon"][:], in0=S["bon"][:], scalar1=pcol("lb", cb),
                                                           scalar2=None, op0=ALU.add), ["bon", "pvs"], ["bon"])
                        Dm(bonus[cs, ts], S["bon"][:], r=["bon"])
                    for cb0 in range(0, NB, 2):
                        cbs = [cb for cb in (cb0, cb0 + 1) if cb < NB]
                        ev = [threading.Event() for _ in cbs] + [threading.Event()]
                        alive = [True] * len(cbs) + [False] * (2 - len(cbs))
                        baton["ev"], baton["alive"] = ev, alive
                        errs = []

                        def runner(cb, tid):
                            ev[tid].wait()
                            try:
                                cb_body(cb, tid)
                            except BaseException as ex:
                                errs.append(ex)
                            finally:
                                alive[tid] = False
                                if len(cbs) == 2 and alive[1 - tid]:
                                    ev[1 - tid].set()
                        ths = [threading.Thread(target=runner, args=(cb, i)) for i, cb in enumerate(cbs)]
                        for t_ in ths:
                            t_.start()
                        ev[0].set()
                        for t_ in ths:
                            t_.join()
                        baton["ev"] = None
                        if errs:
                            raise errs[0]
                kb.barrier()
                ps_state["lo"], ps_state["hi"] = 0, 6

        def phase_scan():
            with ExitStack() as es:
                carry = es.enter_context(SBT("sc_carry", [64, 2 * c.NCH], F32))
                kb.dma("sp", carry[:], c_carry, writes=["carry"])
                NHG = c.RH // 8
                chains = [(d, hg) for hg in range(NHG) for d in range(2)]
                CPR = 3
                for r0 in range(0, len(chains), CPR):
                    with ExitStack() as es2:
                        act_ch = chains[r0:r0 + CPR]
                        st = {}
                        for ci, (d, hg) in enumerate(act_ch):
                            A = lambda nm, shp, dt: es2.enter_context(
                                SBT("sc%d_%s" % (ci, nm), shp, dt))
                            s_ = {"fm": [A("fm%d" % b, [64, 4, 8, 128], BF16) for b in range(2)],
                                  "tm": [A("tm%d" % b, [64, 3, 512], BF16) for b in range(2)],
                                  "H32": A("H32", [64, 512], F32), "Hb": A("Hb", [64, 512], BF16),
                                  "gt": A("gt", [64, 8, c.NCH], F32),
                                  "yb": [A("yb%d" % b, [64, 8, 128], F32) for b in range(2)]}
                            for nm in ("Nn", "NT", "AK", "RB", "RK", "X", "U"):
                                s_[nm] = A(nm, [64, 512], BF16)
                            for nm in ("Q", "QT", "P"):
                                s_[nm] = [A("%s%d" % (nm, b), [64, 512], BF16) for b in range(2)]
                            st[ci] = s_
                            kb.op("dve", lambda e: e.memset(s_["H32"][:], 0.0), [], [("H32", ci)])
                            kb.op("dve", lambda e: e.memset(s_["Hb"][:], 0.0), [], [("Hb", ci)])
                            kb.dma("sp", s_["gt"][:],
                                   sG[d].rearrange("(h c) n -> c h n", c=64)[:, hg * 8:(hg + 1) * 8, :],
                                   writes=[("gt", ci)])

                        def unit(ci, d, hg, step):
                            s_ = st[ci]
                            ch = step if d == 0 else c.NCH - 1 - step
                            pair = step // 2
                            pb = pair % 2
                            sub = (ch % 2)
                            K = lambda nm: (nm, ci)
                            if step % 2 == 0:
                                c2 = (ch // 2) * 2
                                tsl = slice(c2 * 64, c2 * 64 + 128)
                                for i, ten in enumerate((sA[d], sB[d], sK[d], sR[d])):
                                    kb.dma("sp", s_["fm"][pb][:, i, :, :],
                                           ten.rearrange("(h c) n -> c h n", c=64)[:, hg * 8:(hg + 1) * 8, tsl],
                                           writes=[("fm", ci, pb)])
                            tmb = step % 2
                            for i, ten in enumerate((sV, sBh[d], sKh[d])):
                                kb.dma("sp", s_["tm"][tmb][:, i, :], ten[ch * 64:(ch + 1) * 64, hg * 512:(hg + 1) * 512],
                                       writes=[("tm", ci, tmb)])
                            fm = s_["fm"][pb]
                            tsub = slice(sub * 64, (sub + 1) * 64)
                            At = lambda h: fm[:, 0, h, tsub]
                            Bt = lambda h: fm[:, 1, h, tsub]
                            Kt = lambda h: fm[:, 2, h, tsub]
                            Rt = lambda h: fm[:, 3, h, tsub]
                            tm = s_["tm"][tmb]
                            Vh = lambda h: tm[:, 0, h * 64:(h + 1) * 64]
                            Bh = lambda h: tm[:, 1, h * 64:(h + 1) * 64]
                            Kh = lambda h: tm[:, 2, h * 64:(h + 1) * 64]
                            hs = lambda t, h: t[0:64, h * 64:(h + 1) * 64]
                            FMK = ("fm", ci, pb)
                            TMK = ("tm", ci, tmb)
                            ms, mi = (0, 1) if d == 0 else (2, 3)
                            mt = 2 if d == 0 else 0

                            def mm8(lh, rh, reads, masks, outs):
                                pt, pk = next_ps()
                                for h in range(8):
                                    kb.op("pe", lambda e: e.matmul(hs(pt, h), lhsT=lh(h), rhs=rh(h), start=True,
                                                                   stop=True),
                                          reads=reads, writes=[pk], signal=(h == 7))
                                return pt, pk

                            def evm(pt, pk, mask, dst, dk, eng="dve"):
                                kb.op(eng, lambda e: e.tensor_tensor(
                                    out=dst[:], in0=pt[0:64, :], in1=tri[:, mask, :, :].rearrange("p a b -> p (a b)"),
                                    op=ALU.mult), reads=[pk, "tri"], writes=[dk])

                            def evc(pt, pk, dst, dk, eng):
                                if eng == "act":
                                    kb.op("act", lambda e: e.activation(out=dst[:], in_=pt[0:64, :], func=AF.Copy),
                                          reads=[pk], writes=[dk])
                                else:
                                    kb.op("dve", lambda e: e.tensor_copy(out=dst[:], in_=pt[0:64, :]),
                                          reads=[pk], writes=[dk])
                            pt, pk = mm8(Bt, At, [FMK], None, None)
                            evm(pt, pk, ms, s_["Nn"], K("Nn"))
                            pt, pk = mm8(At, Bt, [FMK], None, None)
                            evm(pt, pk, mt, s_["NT"], K("NT"))
                            yield
                            pt, pk = mm8(Kt, At, [FMK], None, None)
                            evm(pt, pk, ms, s_["AK"], K("AK"))
                            pt, pk = mm8(Bt, Rt, [FMK], None, None)
                            evm(pt, pk, mi, s_["RB"], K("RB"))
                            pt, pk = mm8(Kt, Rt, [FMK], None, None)
                            evm(pt, pk, mi, s_["RK"], K("RK"))
                            yield
                            kb.op("dve", lambda e: e.tensor_tensor(
                                out=s_["P"][0][:], in0=s_["Nn"][:],
                                in1=tri[:, 4, :, :].rearrange("p a b -> p (a b)"), op=ALU.add),
                                reads=[K("Nn"), "tri"], writes=[K("P0")])
                            Qc, QTc, Pc = s_["Nn"], s_["NT"], s_["P"][0]
                            Qk, QTk, Pk_ = K("Nn"), K("NT"), K("P0")
                            nlev = 5
                            for j in range(1, nlev + 1):
                                jb = j % 2
                                last = (j == nlev)
                                if not last:
                                    pt, pk = mm8(lambda h: hs(QTc, h), lambda h: hs(Qc, h), [Qk, QTk], None, None)
                                    Qn, Qnk = s_["Q"][jb], K("Q%d" % jb)
                                    evc(pt, pk, Qn, Qnk, "act")
                                pt, pk = mm8(lambda h: hs(Qc, h), lambda h: hs(QTc, h), [Qk, QTk], None, None)
                                QTn, QTnk = s_["QT"][jb], K("QT%d" % jb)
                                evc(pt, pk, QTn, QTnk, "act")
                                yield
                                pt, pk = mm8(lambda h: hs(QTn, h), lambda h: hs(Pc, h), [QTnk, Pk_], None, None)
                                Pn, Pnk = s_["P"][jb], K("P%d" % jb)
                                kb.op("dve", lambda e: e.tensor_tensor(out=Pn[:], in0=pt[0:64, :], in1=Pc[:], op=ALU.add),
                                      reads=[pk, Pk_], writes=[Pnk])
                                Pc, Pk_ = Pn, Pnk
                                if not last:
                                    Qc, Qk = Qn, Qnk
                                QTc, QTk = QTn, QTnk
                                yield
                            Hb = s_["Hb"]
                            pt, pk = next_ps()
                            for h in range(8):
                                kb.op("pe", lambda e: e.matmul(hs(pt, h), lhsT=At(h), rhs=hs(Hb, h), start=True,
                                                               stop=False),
                                      reads=[FMK, K("Hb")], writes=[pk], signal=False)
                                kb.op("pe", lambda e: e.matmul(hs(pt, h), lhsT=hs(s_["AK"], h), rhs=Vh(h),
                                                               start=False, stop=True),
                                      reads=[K("AK"), TMK], writes=[pk], signal=(h == 7))
                            evc(pt, pk, s_["X"], K("X"), "dve")
                            yield
                            pt, pk = mm8(lambda h: hs(Pc, h), lambda h: hs(s_["X"], h), [Pk_, K("X")], None, None)
                            evc(pt, pk, s_["U"], K("U"), "act")
                            yield
                            pt, pk = next_ps()
                            for h in range(8):
                                kb.op("pe", lambda e: e.matmul(hs(pt, h), lhsT=hs(Hb, h), rhs=Rt(h), start=True,
                                                               stop=False),
                                      reads=[FMK, K("Hb")], writes=[pk], signal=False)
                                kb.op("pe", lambda e: e.matmul(hs(pt, h), lhsT=hs(s_["U"], h), rhs=hs(s_["RB"], h),
                                                               start=False, stop=False),
                                      reads=[K("U"), K("RB")], writes=[pk], signal=False)
                                kb.op("pe", lambda e: e.matmul(hs(pt, h), lhsT=Vh(h), rhs=hs(s_["RK"], h),
                                                               start=False, stop=True),
                                      reads=[TMK, K("RK")], writes=[pk], signal=(h == 7))
                            yb = s_["yb"][pb]
                            kb.op("dve", lambda e: e.tensor_copy(
                                out=yb[:, :, tsub], in_=pt[0:64, :].rearrange("p (a b) -> p a b", b=64)),
                                reads=[pk], writes=[("yb", ci, pb)])
                            if step % 2 == 1:
                                c2 = (ch // 2) * 2
                                kb.dma("sp", yT[d].rearrange("(h c) n -> c h n", c=64)[:, hg * 8:(hg + 1) * 8,
                                                                                      c2 * 64:c2 * 64 + 128],
                                       yb[:], reads=[("yb", ci, pb)])
                            pt, pk = next_ps()
                            for h in range(8):
                                kb.op("pe", lambda e: e.matmul(hs(pt, h), lhsT=Bh(h), rhs=hs(s_["U"], h), start=True,
                                                               stop=False),
                                      reads=[TMK, K("U")], writes=[pk], signal=False)
                                kb.op("pe", lambda e: e.matmul(hs(pt, h), lhsT=Kh(h), rhs=Vh(h), start=False,
                                                               stop=True),
                                      reads=[TMK], writes=[pk], signal=(h == 7))
                            H32 = s_["H32"]
                            for h in range(8):
                                kb.op("dve", lambda e: e.scalar_tensor_tensor(
                                    out=hs(H32, h), in0=hs(H32, h), scalar=s_["gt"][:, h, ch:ch + 1],
                                    in1=hs(pt, h), op0=ALU.mult, op1=ALU.add),
                                    reads=[pk, K("H32"), K("gt")], writes=[K("H32")])
                            ccol = carry[:, d * c.NCH + ch:d * c.NCH + ch + 1]
                            kb.op("dve", lambda e: e.tensor_scalar(out=H32[:], in0=H32[:], scalar1=ccol, scalar2=None,
                                                                  op0=ALU.mult),
                                  reads=[K("H32"), "carry"], writes=[K("H32")])
                            kb.op("act", lambda e: e.activation(out=Hb[:], in_=H32[:], func=AF.Copy),
                                  reads=[K("H32")], writes=[K("Hb")])
                            yield

                        for step in range(c.NCH):
                            gens = [unit(ci, d, hg, step) for ci, (d, hg) in enumerate(act_ch)]
                            while gens:
                                for g in list(gens):
                                    try:
                                        next(g)
                                    except StopIteration:
                                        gens.remove(g)
                        kb.barrier()

        def phase_rwkvpost():
            with ExitStack() as es:
                nb = 2
                T = lambda nm, dt: [es.enter_context(SBT("po_%s%d" % (nm, i), [128, 512], dt))
                                    for i in range(nb)]
                y0, y1, bn, gg, dd, t1 = T("y0", F32), T("y1", F32), T("bn", F32), T("gg", F32), T("dd", F32), T("t1", F32)
                yb16, sq16, ob = T("yb", BF16), T("sq", BF16), T("ob", BF16)
                it = 0
                for cb in range(c.NB):
                    cs = slice(cb * 128, (cb + 1) * 128)
                    for tg in range(NT):
                        b = it % nb
                        it += 1
                        ts = slice(tg * 512, (tg + 1) * 512)
                        kb.dma("sp", y0[b][:], yT[0][cs, ts], writes=[("y0", b)])
                        kb.dma("sp", y1[b][:], yT[1][cs, ts], writes=[("y1", b)])
                        kb.dma("sp", bn[b][:], bonus[cs, ts], writes=[("bn", b)])
                        kb.dma("sp", gg[b][:], gout[cs, ts], writes=[("gg", b)])
                        kb.op("dve", lambda e: e.tensor_tensor(out=y0[b][:], in0=y0[b][:], in1=y1[b][:], op=ALU.add),
                              reads=[("y0", b), ("y1", b)], writes=[("y0", b)])
                        kb.op("act", lambda e: e.activation(out=yb16[b][:], in_=y0[b][:], func=AF.Copy),
                              reads=[("y0", b)], writes=[("yb", b)])
                        p1, k1 = next_ps()
                        kb.op("pe", lambda e: e.matmul(p1[:, :], lhsT=blk, rhs=yb16[b][:], start=True, stop=True),
                              reads=[("yb", b), "mats"], writes=[k1])
                        kb.op("dve", lambda e: e.scalar_tensor_tensor(out=dd[b][:], in0=p1[:, :], scalar=-1.0 / 64,
                                                                     in1=y0[b][:], op0=ALU.mult, op1=ALU.add),
                              reads=[k1, ("y0", b)], writes=[("dd", b)])
                        kb.op("act", lambda e: e.activation(out=sq16[b][:], in_=dd[b][:], func=AF.Square),
                              reads=[("dd", b)], writes=[("sq", b)])
                        p2, k2 = next_ps()
                        kb.op("pe", lambda e: e.matmul(p2[:, :], lhsT=blk, rhs=sq16[b][:], start=True, stop=True),
                              reads=[("sq", b), "mats"], writes=[k2])
                        rsqrt_to(t1[b][:], ("t1", b), p2[:, :], k2, 1.0 / 64, 64e-5)
                        kb.op("dve", lambda e: e.tensor_scalar(out=t1[b][:], in0=t1[b][:], scalar1=pcol("lw", cb),
                                                              scalar2=None, op0=ALU.mult),
                              reads=[("t1", b), "pvs"], writes=[("t1", b)])
                        kb.op("dve", lambda e: e.tensor_tensor(out=dd[b][:], in0=dd[b][:], in1=t1[b][:], op=ALU.mult),
                              reads=[("dd", b), ("t1", b)], writes=[("dd", b)])
                        kb.op("dve", lambda e: e.tensor_tensor(out=dd[b][:], in0=dd[b][:], in1=bn[b][:], op=ALU.add),
                              reads=[("dd", b), ("bn", b)], writes=[("dd", b)])
                        kb.op("dve", lambda e: e.tensor_tensor(out=ob[b][:], in0=dd[b][:], in1=gg[b][:], op=ALU.mult),
                              reads=[("dd", b), ("gg", b)], writes=[("ob", b)])
                        kb.dma("sp", rwT[cs, ts], ob[b][:], reads=[("ob", b)])
                kb.barrier()

        def phase_merge(l):
            with ExitStack() as es:
                ga_b = [es.enter_context(SBT("m_ga%d" % i, [128, 512], BF16)) for i in range(3)]
                gr_b = [es.enter_context(SBT("m_gr%d" % i, [128, 512], BF16)) for i in range(3)]
                t1 = [es.enter_context(SBT("m_t%d" % i, [128, 512], F32)) for i in range(3)]
                ob = [es.enter_context(SBT("m_o%d" % i, [128, 512], BF16)) for i in range(3)]
                cnt = [0]

                def epi(psl, pkl, tag, offabs, w, tok0):
                    b = cnt[0] % 3
                    cnt[0] += 1
                    ts = slice(tok0, tok0 + 512)
                    kb.dma("sp", ga_b[b][:], gA[offabs:offabs + 128, ts], writes=[("mga", b)])
                    kb.dma("sp", gr_b[b][:], gR[offabs:offabs + 128, ts], writes=[("mgr", b)])
                    kb.op("dve", lambda e: e.tensor_tensor(out=t1[b][:], in0=psl[0], in1=ga_b[b][:], op=ALU.mult),
                          reads=[pkl[0], ("mga", b)], writes=[("mt", b)])
                    kb.op("dve", lambda e: e.tensor_tensor(out=ob[b][:], in0=psl[1], in1=gr_b[b][:], op=ALU.mult),
                          reads=[pkl[1], ("mgr", b)], writes=[("mo", b)])
                    kb.op("dve", lambda e: e.tensor_tensor(out=ob[b][:], in0=ob[b][:], in1=t1[b][:], op=ALU.add),
                          reads=[("mo", b), ("mt", b)], writes=[("mo", b)])
                    kb.dma("sp", mixT[tok0 // 128:tok0 // 128 + 4, :, offabs // 128, :].rearrange("t p k -> p t k"),
                           ob[b][:].rearrange("p (t k) -> p t k", k=128), reads=[("mo", b)])

                groups = [(g * 512, 512, [(j * 128, 128, None) for j in range(4)]) for g in range(D // 512)]
                gemm_fm([(attT, w_ua[l], c.AW), (rwT, w_ur[l], c.RW)], groups, min(N, 1024), epi)

        def phase_ffnup(l):
            with ExitStack() as es:
                ev16 = Evac(es, "fu_b", BF16)
                cnt = [0]

                def epi(psl, pkl, tag, offabs, w, tok0):
                    buf, bk = ev16.get()
                    cnt[0] += 1
                    if cnt[0] % 2 == 0:
                        kb.op("dve", lambda e: e.tensor_copy(out=buf[0:w, :], in_=psl[0]), reads=[pkl[0]], writes=[bk])
                    else:
                        kb.op("act", lambda e: e.activation(out=buf[0:w, :], in_=psl[0], func=AF.Copy),
                              reads=[pkl[0]], writes=[bk])
                    kb.dma("sp", uT[offabs:offabs + w, tok0:tok0 + 512], buf[0:w, :], reads=[bk])

                M = 2 * c.DFF
                groups = []
                o = 0
                while o < M:
                    gw = min(512, M - o)
                    groups.append((o, gw, [(j * 128, 128, None) for j in range(gw // 128)]))
                    o += gw
                gemm_fm([(hT, w_fu[l], D)], groups, min(N, 1024), epi)

        def phase_ffnact():
            with ExitStack() as es:
                nb = 2
                uv = [es.enter_context(SBT("fa_uv%d" % i, [128, N + 2], BF16)) for i in range(nb)]
                ug = [es.enter_context(SBT("fa_ug%d" % i, [128, N + 2], BF16)) for i in range(nb)]
                cv = [es.enter_context(SBT("fa_cv%d" % i, [128, N], F32)) for i in range(nb)]
                cg = [es.enter_context(SBT("fa_cg%d" % i, [128, N], F32)) for i in range(nb)]
                ob = [es.enter_context(SBT("fa_o%d" % i, [128, N], BF16)) for i in range(nb)]
                tmp = es.enter_context(SBT("fa_tmp", [128, 4], F32))
                for i in range(nb):
                    for t in (uv[i], ug[i]):
                        kb.op("dve", lambda e: e.memset(t[:, 0:1], 0.0), [], [("fau", i)])
                        kb.op("dve", lambda e: e.memset(t[:, N + 1:N + 2], 0.0), [], [("fau", i)])
                FBh = c.FB // 2
                for j in range(FBh):
                    b = j % nb
                    kb.dma("sp", uv[b][:, 1:N + 1], uT[j * 128:(j + 1) * 128, :], writes=[("fau", b)])
                    kb.dma("sp", ug[b][:, 1:N + 1], uT[(FBh + j) * 128:(FBh + j + 1) * 128, :], writes=[("fau", b)])
                    for (src, dst, fbk, dk) in ((uv[b], cv[b], j, ("cv", b)), (ug[b], cg[b], FBh + j, ("cg", b))):
                        fc = lambda tap: pcol("fc", tap * c.FB + fbk)
                        kb.op("act", lambda e: e.activation(out=dst[:], in_=src[:, 1:N + 1], func=AF.Identity,
                                                            bias=pcol("fb", fbk), scale=fc(1)),
                              reads=[("fau", b), "pvs"], writes=[dk])
                        kb.op("dve", lambda e: e.scalar_tensor_tensor(out=dst[:], in0=src[:, 0:N], scalar=fc(0),
                                                                     in1=dst[:], op0=ALU.mult, op1=ALU.add),
                              reads=[("fau", b), dk, "pvs"], writes=[dk])
                        kb.op("dve", lambda e: e.scalar_tensor_tensor(out=dst[:], in0=src[:, 2:N + 2], scalar=fc(2),
                                                                     in1=dst[:], op0=ALU.mult, op1=ALU.add),
                              reads=[("fau", b), dk, "pvs"], writes=[dk])
                        kb.op("dve", lambda e: e.tensor_scalar(out=tmp[:, 0:1], in0=src[:, HALF:HALF + 1], scalar1=fc(0),
                                                              scalar2=edge[:, 0:1], op0=ALU.mult, op1=ALU.mult),
                              reads=[("fau", b), "pvs", "edge"], writes=["fatmp"])
                        kb.op("dve", lambda e: e.tensor_tensor(out=dst[:, HALF:HALF + 1], in0=dst[:, HALF:HALF + 1],
                                                              in1=tmp[:, 0:1], op=ALU.add),
                              reads=["fatmp", dk], writes=[dk])
                        kb.op("dve", lambda e: e.tensor_scalar(out=tmp[:, 1:2], in0=src[:, HALF + 1:HALF + 2],
                                                              scalar1=fc(2), scalar2=edge[:, 0:1], op0=ALU.mult,
                                                              op1=ALU.mult),
                              reads=[("fau", b), "pvs", "edge"], writes=["fatmp"])
                        kb.op("dve", lambda e: e.tensor_tensor(out=dst[:, HALF - 1:HALF], in0=dst[:, HALF - 1:HALF],
                                                              in1=tmp[:, 1:2], op=ALU.add),
                              reads=["fatmp", dk], writes=[dk])
                    kb.op("act", lambda e: e.activation(out=cg[b][:], in_=cg[b][:], func=AF.Silu),
                          reads=[("cg", b)], writes=[("cg", b)])
                    kb.op("pool", lambda e: e.tensor_tensor(out=ob[b][:], in0=cg[b][:], in1=cv[b][:], op=ALU.mult),
                          reads=[("cg", b), ("cv", b)], writes=[("fao", b)])
                    kb.dma("sp", actT[:, :, j, :].rearrange("t p k -> p t k"),
                           ob[b][:].rearrange("p (t k) -> p t k", k=128), reads=[("fao", b)])
                kb.barrier()

        xcur = x_in
        for l in range(L):
            kb.dma("sp", pvs[:], pv[l], writes=["pvs"])
            kb.barrier()
            phase_norm(xcur, "g1")
            phase_inproj(l)
            phase_qkprep()
            phase_attn()
            phase_rwkvprep(l)
            phase_scan()
            phase_rwkvpost()
            phase_merge(l)
            gemm_tm(mixT, w_o[l], D, xcur, x1)
            phase_norm(x1, "g2")
            phase_ffnup(l)
            phase_ffnact()
            xnext = y_out if l == L - 1 else x2
            gemm_tm(actT, w_fd[l], c.DFF, x1, xnext)
            xcur = xnext
        kb.barrier()
    return nc


def _cols(v):
    v = np.asarray(v, np.float32).reshape(-1, 128)
    return np.ascontiguousarray(v.T)


def host_consts(cfg, packed):
    c = cfg
    N = c.N
    HALF = N // 2
    NKT = N // 128
    attb = np.zeros((128, NKT, 2), np.float32)
    if packed:
        for kt in range(NKT):
            kh = 0 if kt * 128 < HALF else 1
            attb[:, kt, 1 - kh] = -30000.0
    carry = np.ones((64, 2, c.NCH), np.float32)
    if packed:
        carry[:, 0, c.NCH // 2 - 1] = 0.0
        carry[:, 1, c.NCH // 2] = 0.0
    edge = np.full((128, 1), -1.0 if packed else 0.0, np.float32)
    T = HALF if packed else N
    pos = np.arange(N) % T
    row = (pos // 64).astype(np.float32)
    col = (pos % 64).astype(np.float32)
    inv = (10000.0 ** (-np.arange(0, 64, 2, dtype=np.float32) / 64)).astype(np.float32)
    rope = np.zeros((128, 2, N), np.float32)
    for p in range(128):
        ax, j = p // 64, p % 64
        half, f = j // 32, j % 32
        ang = (row if ax == 0 else col) * inv[f]
        rope[p, 0] = np.cos(ang.astype(np.float32))
        rope[p, 1] = (-np.sin(ang.astype(np.float32))) if half == 0 else np.sin(ang.astype(np.float32))
    mats = np.zeros((128, 4, 128), np.float32)
    mats[:, 0, :] = np.eye(128)
    mats[:, 1, :] = 1.0
    mats[0:64, 2, 0:64] = 1.0
    mats[64:128, 2, 64:128] = 1.0
    for p in range(128):
        j = p % 64
        partner = p + 32 if (j // 32) == 0 else p - 32
        mats[partner, 3, p] = 1.0
    s = np.arange(64)[:, None]
    t = np.arange(64)[None, :]
    tri = np.zeros((64, 5, 8, 64), np.float32)
    for i, m in enumerate((s < t, s <= t, s > t, s >= t, s == t)):
        tri[:, i, :, :] = m.astype(np.float32)[:, None, :]
    reset = np.ones((128, 512), np.float32)
    reset[:, 0::64] = 0.0
    return {"c_attb": attb.reshape(128, NKT * 2), "c_carry": carry.reshape(64, 2 * c.NCH), "c_edge": edge,
            "c_rope": rope, "c_mats": mats.astype(NP_BF16), "c_tri": tri.astype(NP_BF16), "c_reset": reset}


def host_pv(cfg, inp):
    c = cfg
    pv = np.zeros((c.L, 128, c.PC), np.float32)
    NB, FB = c.NB, c.FB
    for l in range(c.L):
        def put(nm, arr):
            pv[l, :, c.po[nm]:c.po[nm] + arr.shape[1]] = arr
        put("g1", _cols(inp["norm_mix"][l]))
        put("g2", _cols(inp["norm_ffn"][l]))
        put("qg", _cols(inp["q_gain"][l]))
        put("kg", _cols(inp["k_gain"][l]))
        cv = np.asarray(inp["rwkv_conv"][l], np.float32)
        put("conv", np.concatenate([_cols(cv[tap, i * c.RW:(i + 1) * c.RW]) for tap in range(3) for i in range(3)], 1))
        put("w0", np.concatenate([_cols(inp["decay_w0"][l][d]) for d in range(2)], 1))
        put("a0", np.concatenate([_cols(inp["iclr_a0"][l][d]) for d in range(2)], 1))
        put("kk", _cols(inp["k_k"][l]))
        put("ka", _cols(inp["k_a"][l]))
        put("rk", _cols(np.asarray(inp["r_k"][l]).reshape(-1)))
        put("lw", _cols(inp["lnx_w"][l]))
        put("lb", _cols(inp["lnx_b"][l]))
        fc = np.asarray(inp["ffn_conv"][l], np.float32)
        put("fc", np.concatenate([_cols(fc[tap]) for tap in range(3)], 1))
        put("fb", _cols(inp["ffn_conv_b"][l]))
    return pv


_NC_CACHE = {}


def run(cfg, inputs, debug_outs=()):
    c = cfg
    key = (c.D, c.SEQ, c.L, tuple(debug_outs))
    if key not in _NC_CACHE:
        _NC_CACHE[key] = build(c, debug_outs)
    nc = _NC_CACHE[key]
    xp = np.asarray(inputs["x_prompt"], np.float32)
    xs = np.asarray(inputs["x_sample"], np.float32)
    D = c.D
    groups = {}
    npk = c.BATCH // 2
    act_cores = [0, 1, 4, 5, 2, 3, 6, 7]
    gi = 0
    for g in range(npk):
        groups[act_cores[gi]] = ("p", g); gi += 1
    for g in range(c.DEC_BATCH):
        groups[act_cores[gi]] = ("s", g); gi += 1
    pvh = host_pv(c, inputs)
    wkeys = {"w_in": "w_in", "w_up_attn": "w_up_attn", "w_up_rwkv": "w_up_rwkv", "w_o": "w_o",
             "w_ffn_up": "w_ffn_up", "w_ffn_down": "w_ffn_down", "decay_w2": "decay_w2", "iclr_a2": "iclr_a2",
             "gate_g2": "gate_g2"}
    shared = {k: np.asarray(inputs[v], np.float32) for k, v in wkeys.items()}
    shared["pv"] = pvh
    cp = host_consts(c, True)
    cs = host_consts(c, False)
    in_maps = []
    zx = np.zeros((c.N, D), np.float32)
    for core in range(8):
        m = dict(shared)
        gk = groups.get(core)
        if gk is None:
            m["x"] = zx
            m.update(cs)
        elif gk[0] == "p":
            m["x"] = np.ascontiguousarray(xp[2 * gk[1]:2 * gk[1] + 2].reshape(c.N, D))
            m.update(cp)
        else:
            m["x"] = np.ascontiguousarray(xs[gk[1]])
            m.update(cs)
        in_maps.append(m)
    res = run_bass_kernel_spmd(nc, in_maps, core_ids=list(range(8)))
    yp = np.zeros((c.BATCH, c.SEQ, D), np.float32)
    ys = np.zeros((c.DEC_BATCH, 2 * c.SEQ, D), np.float32)
    for core, gk in groups.items():
        y = np.asarray(res.results[core]["y"])
        if gk[0] == "p":
            yp[2 * gk[1]:2 * gk[1] + 2] = y.reshape(2, c.SEQ, D)
        else:
            ys[gk[1]] = y
    return (yp, ys), res


def kernel(**inputs):
    cfg = Cfg()
    (yp, ys), _ = run(cfg, inputs)
    return (yp, ys)
```
